# Optimizing a Trainium2 kernel written in Bass

```python
import math
import jax, jax.numpy as jnp
from jax import lax
import numpy as np

D_MODEL = 1024
BATCH = 8
SEQ = 4096
DEPTH = 2

CHUNK = 64
Q_BLOCK = 128
DA_HEADS = 8
DA_HEAD_DIM = 64
DA_QK_WIDTH = DA_HEADS * 2 * DA_HEAD_DIM
DA_V_WIDTH = DA_HEADS * 2 * DA_HEAD_DIM
S5_GROUP = 16
S5_WIDTH = D_MODEL
S5_GROUPS = S5_WIDTH // S5_GROUP
S5_STATE = 64
CONV_WIDTH = D_MODEL
CONV_KERNEL = 31
FFN_HIDDEN = 2816
FFN_KERNEL = 3
N_BRANCHES = 3
REL_BUCKETS = 32
REL_MAX_DIST = 128
EPS = 1e-6
OFF_Q = 0
OFF_K = OFF_Q + DA_QK_WIDTH
OFF_V = OFF_K + DA_QK_WIDTH
OFF_U = OFF_V + DA_V_WIDTH
OFF_C = OFF_U + S5_WIDTH
OFF_G = OFF_C + 2 * CONV_WIDTH
IN_WIDTH = OFF_G + N_BRANCHES * D_MODEL

kernel_name = 'hybrid_diffattn_s5_conformer_encoder'


def _rmsnorm(x, g):
    x32 = x.astype(jnp.float32)
    y = x32 * lax.rsqrt(jnp.mean(x32 * x32, axis=-1, keepdims=True) + EPS)
    return (y * g.astype(jnp.float32)).astype(x.dtype)


def _layernorm(x, g, b):
    x32 = x.astype(jnp.float32)
    xc = x32 - jnp.mean(x32, axis=-1, keepdims=True)
    y = xc * lax.rsqrt(jnp.mean(xc * xc, axis=-1, keepdims=True) + EPS)
    return (y * g.astype(jnp.float32) + b.astype(jnp.float32)).astype(x.dtype)


def _causal_dwconv(x, w):
    k_len, ch = w.shape
    return lax.conv_general_dilated(
        x, w[:, None, :].astype(x.dtype), window_strides=(1,), padding=[(k_len - 1, 0)],
        dimension_numbers=('NWC', 'WIO', 'NWC'), feature_group_count=ch)


def _t5_bucket(rel):
    nb = REL_BUCKETS // 2
    n = -rel
    ret = jnp.where(n < 0, nb, 0)
    n = jnp.abs(n)
    max_exact = nb // 2
    nf = jnp.maximum(n, 1).astype(jnp.float32)
    large = max_exact + (jnp.log(nf / max_exact) / math.log(REL_MAX_DIST / max_exact)
                         * (nb - max_exact)).astype(jnp.int32)
    large = jnp.minimum(large, nb - 1)
    return ret + jnp.where(n < max_exact, n, large)


def _diff_attention(q, k, v, bias_table, lam):
    b_, s_ = q.shape[:2]
    n_blocks = s_ // Q_BLOCK
    kpos = jnp.arange(s_)
    kchunk = kpos // CHUNK
    scale = DA_HEAD_DIM ** -0.5
    qb = jnp.moveaxis(q.reshape(b_, n_blocks, Q_BLOCK, DA_HEADS, 2, DA_HEAD_DIM), 1, 0)

    def block(args):
        q_blk, idx = args
        qpos = idx * Q_BLOCK + jnp.arange(Q_BLOCK)
        s = jnp.einsum('bqhcd,bkhcd->bhcqk', q_blk, k,
                       preferred_element_type=jnp.float32) * scale
        bias = jnp.moveaxis(bias_table[_t5_bucket(kpos[None, :] - qpos[:, None])], -1, 0)
        mask = kchunk[None, :] <= (qpos // CHUNK)[:, None]
        s = jnp.where(mask, s + bias.astype(jnp.float32)[None, :, None], -1e30)
        p = jax.nn.softmax(s, axis=-1)
        attn = p[:, :, 0] - lam * p[:, :, 1]
        return jnp.einsum('bhqk,bkhe->bqhe', attn.astype(v.dtype), v)

    out = lax.map(block, (qb, jnp.arange(n_blocks)))
    return jnp.moveaxis(out, 0, 1).reshape(b_, s_, DA_HEADS, 2 * DA_HEAD_DIM)


def _s5(u, lam_re, lam_im, log_step, b_re, b_im, c_re, c_im, d):
    f32 = jnp.float32
    b_, s_ = u.shape[:2]
    n_chunks = s_ // CHUNK
    step = jnp.exp(log_step.astype(f32))[:, None]
    lr, li = lam_re.astype(f32), lam_im.astype(f32)
    mag = jnp.exp(lr * step)
    ab_re, ab_im = mag * jnp.cos(li * step), mag * jnp.sin(li * step)
    den = lr * lr + li * li
    nr, ni = ab_re - 1.0, ab_im
    f_re = (nr * lr + ni * li) / den
    f_im = (ni * lr - nr * li) / den
    br, bi = b_re.astype(f32), b_im.astype(f32)
    bb_re = f_re[..., None] * br - f_im[..., None] * bi
    bb_im = f_re[..., None] * bi + f_im[..., None] * br
    cr, ci, d32 = c_re.astype(f32), c_im.astype(f32), d.astype(f32)
    ug = jnp.moveaxis(u.astype(f32).reshape(b_, n_chunks, CHUNK, S5_GROUPS, S5_GROUP), 1, 0)
    a_re = jnp.broadcast_to(ab_re, (b_, CHUNK, S5_GROUPS, S5_STATE))
    a_im = jnp.broadcast_to(ab_im, (b_, CHUNK, S5_GROUPS, S5_STATE))

    def combine(e1, e2):
        a1r, a1i, b1r, b1i = e1
        a2r, a2i, b2r, b2i = e2
        return (a2r * a1r - a2i * a1i, a2r * a1i + a2i * a1r,
                a2r * b1r - a2i * b1i + b2r, a2r * b1i + a2i * b1r + b2i)

    def chunk_step(carry, uc):
        xr0, xi0 = carry
        bur = jnp.einsum('blgh,gph->blgp', uc, bb_re)
        bui = jnp.einsum('blgh,gph->blgp', uc, bb_im)
        acr, aci, sr, si = lax.associative_scan(combine, (a_re, a_im, bur, bui), axis=1)
        xr = sr + acr * xr0[:, None] - aci * xi0[:, None]
        xi = si + acr * xi0[:, None] + aci * xr0[:, None]
        y = (jnp.einsum('blgp,ghp->blgh', xr, cr) - jnp.einsum('blgp,ghp->blgh', xi, ci)
             + d32 * uc)
        return (xr[:, -1], xi[:, -1]), y

    init = (jnp.zeros((b_, S5_GROUPS, S5_STATE), f32), jnp.zeros((b_, S5_GROUPS, S5_STATE), f32))
    _, y = lax.scan(chunk_step, init, ug)
    return jnp.moveaxis(y, 0, 1).reshape(b_, s_, S5_WIDTH).astype(u.dtype)


def setup_inputs(seed: int = 0) -> dict:
    key = jax.random.key(seed)
    ks = jax.random.split(key, 32)
    f32 = jnp.float32

    def nrm(k, shape, scale):
        return jax.random.normal(k, shape, f32) * scale

    def gain(k, shape):
        return 1.0 + 0.02 * jax.random.normal(k, shape, f32)

    L = DEPTH
    return {
        'x': jax.random.normal(ks[0], (BATCH, SEQ, D_MODEL), f32),
        'norm_mix': gain(ks[1], (L, D_MODEL)),
        'w_in': nrm(ks[2], (L, D_MODEL, IN_WIDTH), D_MODEL ** -0.5),
        'qk_gain_q': gain(ks[3], (L, DA_HEAD_DIM)),
        'qk_gain_k': gain(ks[4], (L, DA_HEAD_DIM)),
        'lambda_q1': nrm(ks[5], (L, DA_HEAD_DIM), 0.1),
        'lambda_k1': nrm(ks[6], (L, DA_HEAD_DIM), 0.1),
        'lambda_q2': nrm(ks[7], (L, DA_HEAD_DIM), 0.1),
        'lambda_k2': nrm(ks[8], (L, DA_HEAD_DIM), 0.1),
        'diff_subln': gain(ks[9], (L, 2 * DA_HEAD_DIM)),
        'rel_bias': nrm(ks[10], (REL_BUCKETS, DA_HEADS), 0.5),
        'w_attn_out': nrm(ks[11], (L, DA_V_WIDTH, D_MODEL), DA_V_WIDTH ** -0.5),
        's5_lambda_re': -0.5 + 0.01 * jax.random.normal(ks[12], (L, S5_GROUPS, S5_STATE), f32),
        's5_lambda_im': (jnp.pi * jnp.arange(S5_STATE, dtype=f32)
                         + 0.01 * jax.random.normal(ks[13], (L, S5_GROUPS, S5_STATE), f32)),
        's5_log_step': jax.random.uniform(ks[14], (L, S5_GROUPS), f32,
                                          minval=math.log(0.001), maxval=math.log(0.1)),
        's5_b_re': nrm(ks[15], (L, S5_GROUPS, S5_STATE, S5_GROUP), (2 * S5_GROUP) ** -0.5),
        's5_b_im': nrm(ks[16], (L, S5_GROUPS, S5_STATE, S5_GROUP), (2 * S5_GROUP) ** -0.5),
        's5_c_re': nrm(ks[17], (L, S5_GROUPS, S5_GROUP, S5_STATE), (2 * S5_STATE) ** -0.5),
        's5_c_im': nrm(ks[18], (L, S5_GROUPS, S5_GROUP, S5_STATE), (2 * S5_STATE) ** -0.5),
        's5_d': nrm(ks[19], (L, S5_GROUPS, S5_GROUP), 1.0),
        's5_glu_w1': nrm(ks[20], (L, S5_WIDTH, D_MODEL), S5_WIDTH ** -0.5),
        's5_glu_w2': nrm(ks[21], (L, S5_WIDTH, D_MODEL), S5_WIDTH ** -0.5),
        'conv_dw_w': nrm(ks[22], (L, CONV_KERNEL, CONV_WIDTH), CONV_KERNEL ** -0.5),
        'conv_dw_b': nrm(ks[23], (L, CONV_WIDTH), 0.02),
        'conv_ln_g': gain(ks[24], (L, CONV_WIDTH)),
        'conv_ln_b': nrm(ks[25], (L, CONV_WIDTH), 0.02),
        'conv_w_out': nrm(ks[26], (L, CONV_WIDTH, D_MODEL), CONV_WIDTH ** -0.5),
        'w_out': nrm(ks[27], (L, D_MODEL, D_MODEL), D_MODEL ** -0.5),
        'norm_ffn': gain(ks[28], (L, D_MODEL)),
        'ffn_w_up': nrm(ks[29], (L, D_MODEL, 2 * FFN_HIDDEN), D_MODEL ** -0.5),
        'ffn_dw_w': nrm(ks[30], (L, FFN_KERNEL, 2 * FFN_HIDDEN), FFN_KERNEL ** -0.5),
        'ffn_w_down': nrm(ks[31], (L, FFN_HIDDEN, D_MODEL), FFN_HIDDEN ** -0.5),
    }


def reference(x, norm_mix, w_in, qk_gain_q, qk_gain_k, lambda_q1, lambda_k1, lambda_q2, lambda_k2,
              diff_subln, rel_bias, w_attn_out, s5_lambda_re, s5_lambda_im, s5_log_step,
              s5_b_re, s5_b_im, s5_c_re, s5_c_im, s5_d, s5_glu_w1, s5_glu_w2,
              conv_dw_w, conv_dw_b, conv_ln_g, conv_ln_b, conv_w_out, w_out,
              norm_ffn, ffn_w_up, ffn_dw_w, ffn_w_down):
    f32 = jnp.float32
    b_, s_ = x.shape[:2]
    for l in range(DEPTH):
        h = _rmsnorm(x, norm_mix[l])
        z = h @ w_in[l]
        q = z[..., OFF_Q:OFF_K].reshape(b_, s_, DA_HEADS, 2, DA_HEAD_DIM)
        k = z[..., OFF_K:OFF_V].reshape(b_, s_, DA_HEADS, 2, DA_HEAD_DIM)
        v = z[..., OFF_V:OFF_U].reshape(b_, s_, DA_HEADS, 2 * DA_HEAD_DIM)
        u = z[..., OFF_U:OFF_C]
        c_in = z[..., OFF_C:OFF_G]
        gates = jax.nn.sigmoid(z[..., OFF_G:].astype(f32)).reshape(b_, s_, N_BRANCHES, D_MODEL)

        q = _rmsnorm(q, qk_gain_q[l])
        k = _rmsnorm(k, qk_gain_k[l])
        lam_init = 0.8 - 0.6 * math.exp(-0.3 * l)
        lam = (jnp.exp(jnp.sum(lambda_q1[l].astype(f32) * lambda_k1[l].astype(f32)))
               - jnp.exp(jnp.sum(lambda_q2[l].astype(f32) * lambda_k2[l].astype(f32))) + lam_init)
        o = _diff_attention(q, k, v, rel_bias, lam)
        o = _rmsnorm(o, diff_subln[l]) * (1.0 - lam_init)
        br_a = o.reshape(b_, s_, DA_V_WIDTH) @ w_attn_out[l]

        y = jax.nn.gelu(_s5(u, s5_lambda_re[l], s5_lambda_im[l], s5_log_step[l], s5_b_re[l],
                            s5_b_im[l], s5_c_re[l], s5_c_im[l], s5_d[l]))
        br_b = (y @ s5_glu_w1[l]) * jax.nn.sigmoid(y @ s5_glu_w2[l])

        c = c_in[..., :CONV_WIDTH] * jax.nn.sigmoid(c_in[..., CONV_WIDTH:])
        c = _causal_dwconv(c, conv_dw_w[l]) + conv_dw_b[l]
        c = jax.nn.silu(_layernorm(c, conv_ln_g[l], conv_ln_b[l]))
        br_c = c @ conv_w_out[l]

        mix = (gates[:, :, 0] * br_a.astype(f32) + gates[:, :, 1] * br_b.astype(f32)
               + gates[:, :, 2] * br_c.astype(f32)).astype(x.dtype)
        x = x + mix @ w_out[l]

        h2 = _rmsnorm(x, norm_ffn[l])
        up = _causal_dwconv(h2 @ ffn_w_up[l], ffn_dw_w[l])
        x = x + (jax.nn.gelu(up[..., FFN_HIDDEN:]) * up[..., :FFN_HIDDEN]) @ ffn_w_down[l]
    return x
```

```python
import math
import numpy as np
import ml_dtypes
import concourse.bass as bass
import concourse.mybir as mybir
from concourse.bass_utils import run_bass_kernel_spmd

F32 = mybir.dt.float32
BF16 = mybir.dt.bfloat16
I32 = mybir.dt.int32
ALU = mybir.AluOpType
AF = mybir.ActivationFunctionType
AX = mybir.AxisListType

D = 1024
S = 4096
DEPTH = 2
TB = 512
NTB = S // TB
FFN_H = 2816
IN_W = 9216
OFF_Q, OFF_K, OFF_V, OFF_U, OFF_C, OFF_G = 0, 1024, 2048, 3072, 4096, 6144
EPS = 1e-6
GC = 1.5957691216057308


class Res:
    __slots__ = ("name", "lw", "rd")

    def __init__(self, name=""):
        self.name = name
        self.lw = None
        self.rd = []


class Prog:
    ENGS = ("pe", "dve", "act", "pool", "sp")

    def __init__(self, nc, n_dma_sems=32):
        self.nc = nc
        self.eng = {"pe": nc.tensor, "dve": nc.vector, "act": nc.scalar,
                    "pool": nc.gpsimd, "sp": nc.sync}
        self.sems = {}
        self.cnt = {}
        for e in self.ENGS:
            self.sems[e] = nc.alloc_semaphore("c_" + e)
            self.cnt[e] = 0
        self.n_dma = n_dma_sems
        for i in range(n_dma_sems):
            k = "d%d" % i
            self.sems[k] = nc.alloc_semaphore("s_" + k)
            self.cnt[k] = 0
        self.dma_rr = 0
        self.known = {e: {} for e in self.ENGS}
        self.ninstr = 0

    def _deps(self, reads, writes):
        deps = {}

        def add(t):
            if t is None:
                return
            k, v = t
            if deps.get(k, 0) < v:
                deps[k] = v
        for r in reads:
            add(r.lw)
        for w in writes:
            add(w.lw)
            for t in w.rd:
                add(t)
        return deps

    def _wait(self, e, deps):
        kn = self.known[e]
        for k, v in deps.items():
            if k == e and e == "pe":
                continue
            if kn.get(k, 0) >= v:
                continue
            self.eng[e].wait_ge(self.sems[k], v)
            kn[k] = v

    def _commit(self, tok, reads, writes):
        for r in reads:
            r.rd.append(tok)
            if len(r.rd) > 48:
                m = {}
                for k, v in r.rd:
                    if m.get(k, 0) < v:
                        m[k] = v
                r.rd = list(m.items())
        for w in writes:
            w.lw = tok
            w.rd = []

    def op(self, e, fn, reads=(), writes=()):
        deps = self._deps(reads, writes)
        self._wait(e, deps)
        ins = fn()
        self.cnt[e] += 1
        ins.then_inc(self.sems[e], 1)
        self._commit((e, self.cnt[e]), reads, writes)
        self.ninstr += 1
        return ins

    def dma(self, out, in_, reads=(), writes=(), q="sp", **kw):
        deps = self._deps(reads, writes)
        self._wait(q, deps)
        k = "d%d" % self.dma_rr
        self.dma_rr = (self.dma_rr + 1) % self.n_dma
        self._wait(q, {k: self.cnt[k]})
        ins = self.eng[q].dma_start(out=out, in_=in_, **kw)
        self.cnt[k] += 16
        ins.then_inc(self.sems[k], 16)
        self._commit((k, self.cnt[k]), reads, writes)
        self.ninstr += 1
        return ins

    def barrier(self):
        allv = {k: v for k, v in self.cnt.items() if v > 0}
        for e in self.ENGS:
            self._wait(e, allv)

    def finish(self, e="sp"):
        allv = {k: v for k, v in self.cnt.items() if v > 0}
        self._wait(e, allv)


class Arena:
    def __init__(self, t, nelem):
        self.t = t
        self.n = nelem
        self.off = 0

    def reset(self):
        self.off = 0

    def alloc(self, shape, dt):
        nel = 1
        for d in shape[1:]:
            nel *= d
        nb = nel * (2 if dt == BF16 else 4)
        nb = (nb + 31) // 32 * 32
        assert self.off + nb // 2 <= self.n, ("arena overflow", self.off, nb, self.n)
        ap = self.t[:, self.off:self.off + nb // 2]
        self.off += nb // 2
        if dt != BF16:
            ap = ap.bitcast(dt)
        ap = ap[:, 0:nel]
        if len(shape) == 3:
            ap = ap.rearrange("p (a b) -> p a b", a=shape[1])
        elif len(shape) == 4:
            ap = ap.rearrange("p (a b c) -> p a b c", a=shape[1], b=shape[2])
        return ap


class SubPool:
    def __init__(self, items):
        self.t = list(items)
        self.i = 0

    def get(self):
        r = self.t[self.i]
        self.i = (self.i + 1) % len(self.t)
        return r


class TPool:
    def __init__(self, nc, name, shape, dt, n, psum=False, arena=None):
        self.t = []
        for i in range(n):
            if psum:
                h = nc.alloc_psum_tensor("%s%d" % (name, i), shape, dt)
            elif arena is not None:
                h = arena.alloc(shape, dt)
            else:
                h = nc.alloc_sbuf_tensor("%s%d" % (name, i), shape, dt)
            self.t.append((h, Res("%s%d" % (name, i))))
        self.i = 0

    def get(self):
        r = self.t[self.i]
        self.i = (self.i + 1) % len(self.t)
        return r


def t5_bucket_np(rel):
    nb = 16
    n = -rel
    ret = np.where(n < 0, nb, 0)
    n = np.abs(n)
    max_exact = nb // 2
    nf = np.maximum(n, 1).astype(np.float32)
    large = max_exact + (np.log(nf / max_exact) / math.log(128 / max_exact) * (nb - max_exact)).astype(np.int32)
    large = np.minimum(large, nb - 1)
    return ret + np.where(n < max_exact, n, large)


def host_consts():
    c = {}
    c["ident_f"] = np.eye(128, dtype=np.float32)
    jsw = np.zeros((128, 128), np.float32)
    for i in range(128):
        jsw[i, (i + 64) % 128] = 1.0
    c["jswap_f"] = jsw
    c["jrev_f"] = np.eye(128, dtype=np.float32)[::-1].copy()
    bd = np.zeros((128, 128), np.float32)
    bd[:64, :64] = 1.0
    bd[64:, 64:] = 1.0
    c["bdones_f"] = bd
    m = np.arange(384)
    rel = 127 - m
    bk = t5_bucket_np(rel)
    oh = np.zeros((32, 384), np.float32)
    oh[bk, m] = 1.0
    oh[15, :] -= 1.0
    oh[:, 383] = 0.0
    c["onehot"] = oh
    assert np.all(t5_bucket_np(-np.arange(129, 4096)) == 15)
    mk = np.ones((128, 256), np.float32)
    mk[64:, :64] = 0.0
    c["bandmask"] = mk
    ii = np.arange(128) // 16
    c["mask_intra"] = (ii[None, :] >= ii[:, None]).astype(np.float32)
    sg = np.zeros((128, 2), np.float32)
    sg[:64, 0] = -1.0
    sg[64:, 0] = 1.0
    sg[:64, 1] = 1.0
    sg[64:, 1] = -1.0
    c["sig"] = sg
    return c


NVEC = 8 + 8 + 1 + 1 + 1 + 8 + 8 + 8 + 8 * 31 + 44 * 3
V_GMIX, V_GFFN, V_QG, V_KG, V_SUB, V_CB, V_LNG, V_LNB, V_CW, V_FW = 0, 8, 16, 17, 18, 19, 27, 35, 43, 43 + 248


def host_layer_params(inp, l):
    v = np.zeros((128, NVEC), np.float32)
    v[:, V_GMIX:V_GMIX + 8] = inp["norm_mix"][l].reshape(8, 128).T
    v[:, V_GFFN:V_GFFN + 8] = inp["norm_ffn"][l].reshape(8, 128).T
    v[:, V_QG] = np.tile(inp["qk_gain_q"][l], 2)
    v[:, V_KG] = np.tile(inp["qk_gain_k"][l], 2)
    v[:, V_SUB] = inp["diff_subln"][l]
    v[:, V_CB:V_CB + 8] = inp["conv_dw_b"][l].reshape(8, 128).T
    v[:, V_LNG:V_LNG + 8] = inp["conv_ln_g"][l].reshape(8, 128).T
    v[:, V_LNB:V_LNB + 8] = inp["conv_ln_b"][l].reshape(8, 128).T
    v[:, V_CW:V_CW + 248] = inp["conv_dw_w"][l].reshape(31, 8, 128).transpose(2, 1, 0).reshape(128, 248)
    v[:, V_FW:V_FW + 132] = inp["ffn_dw_w"][l].reshape(3, 44, 128).transpose(2, 1, 0).reshape(128, 132)
    lamv = np.concatenate([inp["lambda_q1"][l], inp["lambda_k1"][l], inp["lambda_q2"][l], inp["lambda_k2"][l]])
    lamv = np.broadcast_to(lamv[None, :], (128, 256)).copy()
    s5 = np.zeros((128, 64 * 4 + 4 * 64 * 16), np.float32)
    lr = inp["s5_lambda_re"][l].T
    li = inp["s5_lambda_im"][l].T
    s5[:, 0:64] = np.concatenate([lr, lr], 0)
    s5[:, 64:128] = np.concatenate([li, li], 0)
    s5[:, 128:192] = np.broadcast_to(inp["s5_log_step"][l][None, :], (128, 64))
    s5[:, 192:256] = np.tile(inp["s5_d"][l].T, (8, 1))
    br = inp["s5_b_re"][l].transpose(1, 0, 2).reshape(64, 1024)
    bi = inp["s5_b_im"][l].transpose(1, 0, 2).reshape(64, 1024)
    cr = inp["s5_c_re"][l].transpose(2, 0, 1).reshape(64, 1024)
    ci = inp["s5_c_im"][l].transpose(2, 0, 1).reshape(64, 1024)
    s5[:, 256:1280] = np.concatenate([br, bi], 0)
    s5[:, 1280:2304] = np.concatenate([bi, br], 0)
    s5[:, 2304:3328] = np.concatenate([cr, ci], 0)
    s5[:, 3328:4352] = np.concatenate([ci, cr], 0)
    return v, lamv, s5


def build(dbg=None, nlayers=DEPTH):
    nc = bass.Bass("TRN2", target_bir_lowering=False)
    P = Prog(nc)
    dbg = dbg or set()

    def dram_in(name, shape, dt=F32):
        return nc.dram_tensor(name, list(shape), dt, kind="ExternalInput")

    def scratch(name, shape, dt):
        kind = "ExternalOutput" if name in dbg else "Internal"
        return nc.dram_tensor(name, list(shape), dt, kind=kind)

    xT_in = dram_in("xT", [D, S])
    W = {}
    for nm, shp in [("w_in", [DEPTH, D, IN_W]), ("w_attn_out", [DEPTH, D, D]), ("s5_glu_w1", [DEPTH, D, D]),
                    ("s5_glu_w2", [DEPTH, D, D]), ("conv_w_out", [DEPTH, D, D]), ("w_out", [DEPTH, D, D]),
                    ("ffn_w_up", [DEPTH, D, 2 * FFN_H]), ("ffn_w_down", [DEPTH, FFN_H, D])]:
        W[nm] = dram_in(nm, shp)
    vecs_in = dram_in("vecs", [DEPTH, 128, NVEC])
    lamv_in = dram_in("lamv", [DEPTH, 128, 256])
    s5p_in = dram_in("s5p", [DEPTH, 128, 4352])
    relb_in = dram_in("rel_bias", [32, 8])
    W_names = list(W.keys())
    cst = {}
    for nm, shp in [("ident_f", [128, 128]), ("jswap_f", [128, 128]), ("jrev_f", [128, 128]), ("bdones_f", [128, 128]),
                    ("onehot", [32, 384]), ("bandmask", [128, 256]), ("mask_intra", [128, 128]), ("sig", [128, 2])]:
        cst[nm] = dram_in("c_" + nm, shp)
    yT_out = nc.dram_tensor("yT", [D, S], F32, kind="ExternalOutput")

    qT_s = scratch("qT_s", [D, S], BF16)
    kT_s = scratch("kT_s", [D, S], BF16)
    v_s = scratch("v_s", [S, D], BF16)
    u_s = scratch("u_s", [512, 8192], BF16)
    cT_s = scratch("cT_s", [D, S], BF16)
    gT_s = scratch("gT_s", [3 * D, S], BF16)
    br_s = scratch("br_s", [3 * D, S], BF16)
    x1T_s = scratch("x1T_s", [D, S], F32)
    xmidT_s = scratch("xmidT_s", [D, S], F32)
    act_s = scratch("act_s", [FFN_H, S], BF16)
    wr_d = scratch("wr_d", [8, 384], F32)
    R_qT, R_kT, R_v, R_u, R_cT, R_gT, R_br, R_x1, R_xmid, R_act, R_wr = [Res(n) for n in
        ("qT", "kT", "v", "u", "cT", "gT", "br", "x1", "xmid", "act", "wr")]

    sb = nc.alloc_sbuf_tensor
    R_c = Res("consts")
    ident_f = sb("ident_f", [128, 128], F32)
    jswap_f = sb("jswap_f", [128, 128], F32)
    jrev_f = sb("jrev_f", [128, 128], F32)
    mask_intra = sb("mask_intra", [128, 128], F32)
    bandmask = sb("bandmask", [128, 256], F32)
    sig = sb("sig", [128, 2], F32)
    ident_b = sb("ident_b", [128, 128], BF16)
    ones_b = sb("ones_b", [128, 128], BF16)
    bdones_b = sb("bdones_b", [128, 128], BF16)
    epsc = sb("epsc", [128, 1], F32)
    onehot = sb("onehot", [32, 384], F32)
    relb = sb("relb", [32, 8], F32)
    expB = sb("expB", [128, 8, 256], BF16)
    for t_, nm in [(ident_f, "ident_f"), (jswap_f, "jswap_f"), (jrev_f, "jrev_f"), (mask_intra, "mask_intra"),
                   (bandmask, "bandmask"), (sig, "sig"), (onehot, "onehot")]:
        P.dma(t_[:], cst[nm].ap(), writes=[R_c])
    P.dma(relb[:], relb_in.ap(), writes=[R_c])
    P.dma(bdones_b[:], cst["bdones_f"].ap(), writes=[R_c], q="pool")
    P.op("dve", lambda: nc.vector.tensor_copy(out=ident_b[:], in_=ident_f[:]), reads=[R_c], writes=[R_c])
    P.op("dve", lambda: nc.vector.memset(ones_b[:], 1.0), writes=[R_c])
    P.op("dve", lambda: nc.vector.memset(epsc[:], EPS), writes=[R_c])

    BIGA = sb("BIGA", [128, 8, S], BF16)
    R_A = [Res("A%d" % i) for i in range(8)]
    ARENA_N = 55296
    AR = Arena(sb("ARENA", [128, ARENA_N], BF16), ARENA_N)
    vec = sb("vec", [128, NVEC], F32)
    R_vec = Res("vec")
    lam4 = sb("lam4", [128, 256], F32)
    neglam = sb("neglam", [128, 1], F32)
    subsc = sb("subsc", [128, 1], F32)
    R_lam = Res("lam")

    PS = TPool(nc, "ps", [128, 512], F32, 8, psum=True)
    f32p = TPool(nc, "f32t", [128, 512], F32, 6)
    b16p = TPool(nc, "b16t", [128, 512], BF16, 8)
    wpool = TPool(nc, "wt", [128, 8, 512], BF16, 2, arena=AR)
    xin = TPool(nc, "xin", [128, 8, 512], F32, 2, arena=AR)
    sqp = TPool(nc, "sq", [128, 8, 512], BF16, 1, arena=AR)
    ustage = TPool(nc, "ust", [128, 32, 8, 16], BF16, 2, arena=AR)

    ps, rps = PS.get()
    P.op("pe", lambda: nc.tensor.matmul(ps[0:8, 0:384], lhsT=relb[:], rhs=onehot[:], start=True, stop=True),
         reads=[R_c], writes=[rps])
    wr_sb, r_wr_sb = f32p.get()
    P.op("act", lambda: nc.scalar.copy(out=wr_sb[0:8, 0:384], in_=ps[0:8, 0:384]), reads=[rps], writes=[r_wr_sb])
    P.dma(wr_d.ap(), wr_sb[0:8, 0:384], reads=[r_wr_sb], writes=[R_wr])
    for h in range(8):
        xt_, rxt = f32p.get()
        P.dma(xt_[:, 0:256], bass.AP(tensor=wr_d, offset=384 * h, ap=[[1, 128], [1, 256]]), reads=[R_wr], writes=[rxt])
        ps, rps = PS.get()
        P.op("pe", lambda: nc.tensor.matmul(ps[:, 0:256], lhsT=jrev_f[:], rhs=xt_[:, 0:256], start=True, stop=True),
             reads=[R_c, rxt], writes=[rps])
        et_, ret = f32p.get()
        P.op("act", lambda: nc.scalar.activation(out=et_[:, 0:256], in_=ps[:, 0:256], func=AF.Exp), reads=[rps], writes=[ret])
        P.op("dve", lambda: nc.vector.tensor_tensor(out=expB[:, h, :], in0=et_[:, 0:256], in1=bandmask[:], op=ALU.mult),
             reads=[ret, R_c], writes=[R_c])

    def rsqrt_from(ps_ap, scale, n=512, width=None):
        t, rt = f32p.get()
        return t, rt

    def norm_stage(src_dram, R_src, gcol, keep=None):
        for tb in range(NTB):
            xb, rxb = xin.get()
            P.dma(xb[:], src_dram.ap()[:, tb * TB:(tb + 1) * TB].rearrange("(kc p) t -> p kc t", p=128),
                  reads=[R_src], writes=[rxb])
            norm_block(xb, rxb, gcol, tb, sqp)

    def norm_block(xb, rxb, gcol, tb, sqpool):
        sq, rsq = sqpool.get()
        P.op("act", lambda: nc.scalar.activation(out=sq[:], in_=xb[:], func=AF.Square), reads=[rxb], writes=[rsq])
        ps, rps = PS.get()
        for kc in range(8):
            P.op("pe", lambda: nc.tensor.matmul(ps[:], lhsT=ones_b[:], rhs=sq[:, kc, :], start=(kc == 0), stop=(kc == 7)),
                 reads=[rsq, R_c], writes=[rps])
        rs, rrs = f32p.get()
        P.op("act", lambda: nc.scalar.activation(out=rs[:], in_=ps[:], func=AF.Sqrt, scale=1.0 / D, bias=epsc[:, 0:1]),
             reads=[rps, R_c], writes=[rrs])
        P.op("dve", lambda: nc.vector.reciprocal(out=rs[:], in_=rs[:]), reads=[rrs], writes=[rrs])
        for kc in range(8):
            P.op("dve", lambda: nc.vector.scalar_tensor_tensor(
                out=BIGA[:, kc, tb * TB:(tb + 1) * TB], in0=xb[:, kc, :], scalar=vec[:, gcol + kc:gcol + kc + 1],
                in1=rs[:], op0=ALU.mult, op1=ALU.mult), reads=[rxb, rrs, R_vec], writes=[R_A[kc]])

    def load_w(wdram2d, offs, KC=8):
        wt, rwt = wpool.get()
        pos = 0
        for off, n in offs:
            P.dma(wt[:, 0:KC, pos:pos + n], wdram2d[:, off:off + n].rearrange("(kc p) n -> p kc n", p=128),
                  writes=[rwt], q="pool")
            pos += n
        return wt, rwt

    def mm_fm(wt, rwt, ci, tb, KC=8, src=None, rsrc=None):
        ps, rps = PS.get()
        for kc in range(KC):
            P.op("pe", lambda: nc.tensor.matmul(ps[:], lhsT=wt[:, kc, ci * 128:(ci + 1) * 128],
                                                rhs=BIGA[:, kc, tb * TB:(tb + 1) * TB], start=(kc == 0), stop=(kc == KC - 1)),
                 reads=[rwt, R_A[kc]], writes=[rps])
        return ps, rps

    def qk_epilogue(ps, rps, gcol, dst, R_dst, row0, tb):
        sq, rsq = b16p.get()
        P.op("act", lambda: nc.scalar.activation(out=sq[:], in_=ps[:], func=AF.Square), reads=[rps], writes=[rsq])
        ps2, rps2 = PS.get()
        P.op("pe", lambda: nc.tensor.matmul(ps2[:], lhsT=bdones_b[:], rhs=sq[:], start=True, stop=True),
             reads=[rsq, R_c], writes=[rps2])
        rs, rrs = f32p.get()
        P.op("act", lambda: nc.scalar.activation(out=rs[:], in_=ps2[:], func=AF.Sqrt, scale=1.0 / 64, bias=epsc[:, 0:1]),
             reads=[rps2, R_c], writes=[rrs])
        P.op("dve", lambda: nc.vector.reciprocal(out=rs[:], in_=rs[:]), reads=[rrs], writes=[rrs])
        o, ro = b16p.get()
        P.op("dve", lambda: nc.vector.scalar_tensor_tensor(out=o[:], in0=ps[:], scalar=vec[:, gcol:gcol + 1], in1=rs[:],
                                                           op0=ALU.mult, op1=ALU.mult), reads=[rps, rrs, R_vec], writes=[ro])
        P.dma(dst.ap()[row0:row0 + 128, tb * TB:(tb + 1) * TB], o[:], reads=[ro], writes=[R_dst])


    y_s = scratch("y_s", [S, D], BF16)
    R_ys = Res("ys")

    def s5_stage(l):
        P.barrier()
        AR.reset()
        R_sp = Res("sp")
        sp = AR.alloc([128, 4352], F32)
        P.dma(sp[:], s5p_in.ap()[l], writes=[R_sp])
        lr, li, lstep, drep = sp[:, 0:64], sp[:, 64:128], sp[:, 128:192], sp[:, 192:256]
        T1, T2, CTa, CTb = sp[:, 256:1280], sp[:, 1280:2304], sp[:, 2304:3328], sp[:, 3328:4352]
        SL = AR.alloc([128, 20, 64], F32)
        KI = AR.alloc([128, 64], I32)
        PWr = AR.alloc([128, 16, 64], F32)
        PWi = AR.alloc([128, 16, 64], F32)
        AKr = AR.alloc([128, 9, 64], F32)
        AKi = AR.alloc([128, 9, 64], F32)
        AKs = AR.alloc([128, 9, 64], F32)
        PRr = AR.alloc([128, 15, 64], F32)
        PRi = AR.alloc([128, 15, 64], F32)
        QRr = AR.alloc([128, 9, 64], F32)
        QRi = AR.alloc([128, 9, 64], F32)
        BT1 = AR.alloc([128, 1024], F32)
        BT2 = AR.alloc([128, 1024], F32)
        tA = AR.alloc([128, 1024], F32)
        tB = AR.alloc([128, 1024], F32)
        rw = dict(reads=[R_sp, R_c], writes=[R_sp])

        def tt(o, a, b, op):
            P.op("dve", lambda: nc.vector.tensor_tensor(out=o, in0=a, in1=b, op=op), **rw)

        def ts(o, a, s1, op0, s2=None, op1=None):
            if op1 is None:
                P.op("dve", lambda: nc.vector.tensor_scalar(out=o, in0=a, scalar1=s1, scalar2=None, op0=op0), **rw)
            else:
                P.op("dve", lambda: nc.vector.tensor_scalar(out=o, in0=a, scalar1=s1, scalar2=s2, op0=op0, op1=op1), **rw)

        def act(o, a, func, scale=1.0):
            P.op("act", lambda: nc.scalar.activation(out=o, in_=a, func=func, scale=scale), **rw)

        def cp(o, a):
            P.op("dve", lambda: nc.vector.tensor_copy(out=o, in_=a), **rw)

        sl = lambda i: SL[:, i, :]
        dl, mag, ang, tu, kf, r1, rr, mm_, sinv, cosv, ar, ai, den, nr, fr, fi, t1, t2, sfi, nsfi = [sl(i) for i in range(20)]
        act(dl, lstep, AF.Exp)
        tt(t1, lr, dl, ALU.mult)
        act(mag, t1, AF.Exp)
        tt(ang, li, dl, ALU.mult)
        ts(tu, ang, 1.0 / (2 * math.pi), ALU.mult)
        cp(KI[:], tu)
        cp(kf, KI[:])
        tt(r1, tu, kf, ALU.subtract)

        def wrap_sin(dst, src, shift):
            ts(rr, src, shift, ALU.add)
            ts(mm_, rr, 0.5, ALU.is_gt)
            tt(rr, rr, mm_, ALU.subtract)
            ts(mm_, rr, 0.5, ALU.is_gt)
            tt(rr, rr, mm_, ALU.subtract)
            ts(mm_, rr, -0.5, ALU.is_lt)
            tt(rr, rr, mm_, ALU.add)
            ts(mm_, rr, -0.5, ALU.is_lt)
            tt(rr, rr, mm_, ALU.add)
            act(dst, rr, AF.Sin, scale=2 * math.pi)
        wrap_sin(sinv, r1, 0.0)
        wrap_sin(cosv, r1, 0.25)
        tt(ar, mag, cosv, ALU.mult)
        tt(ai, mag, sinv, ALU.mult)
        tt(t1, lr, lr, ALU.mult)
        tt(t2, li, li, ALU.mult)
        tt(den, t1, t2, ALU.add)
        P.op("dve", lambda: nc.vector.reciprocal(out=den, in_=den), **rw)
        ts(nr, ar, -1.0, ALU.add)
        tt(t1, nr, lr, ALU.mult)
        tt(t2, ai, li, ALU.mult)
        tt(t1, t1, t2, ALU.add)
        tt(fr, t1, den, ALU.mult)
        tt(t1, ai, lr, ALU.mult)
        tt(t2, nr, li, ALU.mult)
        tt(t1, t1, t2, ALU.subtract)
        tt(fi, t1, den, ALU.mult)
        ts(sfi, fi, sig[:, 0:1], ALU.mult)
        ts(nsfi, fi, sig[:, 1:2], ALU.mult)
        g3 = lambda a: a.rearrange("p (g h) -> p g h", h=16)
        bc3 = lambda a: a.unsqueeze(2).broadcast_to([128, 64, 16])
        tt(g3(tA[:]), bc3(fr), g3(T1), ALU.mult)
        tt(g3(tB[:]), bc3(sfi), g3(T2), ALU.mult)
        tt(BT1[:], tA[:], tB[:], ALU.add)
        tt(g3(tA[:]), bc3(fr), g3(T2), ALU.mult)
        tt(g3(tB[:]), bc3(nsfi), g3(T1), ALU.mult)
        tt(BT2[:], tA[:], tB[:], ALU.add)

        def cmul(orr, oi, xr, xi, yr, yi):
            tt(t1, xr, yr, ALU.mult)
            tt(t2, xi, yi, ALU.mult)
            tt(orr, t1, t2, ALU.subtract)
            tt(t1, xr, yi, ALU.mult)
            tt(t2, xi, yr, ALU.mult)
            tt(oi, t1, t2, ALU.add)
        P.op("dve", lambda: nc.vector.memset(PWr[:, 7, :], 1.0), **rw)
        P.op("dve", lambda: nc.vector.memset(PWi[:, 7, :], 0.0), **rw)
        cp(PWr[:, 8, :], ar)
        cp(PWi[:, 8, :], ai)
        for n in range(2, 9):
            cmul(PWr[:, 7 + n, :], PWi[:, 7 + n, :], PWr[:, 6 + n, :], PWi[:, 6 + n, :], PWr[:, 8, :], PWi[:, 8, :])
        tt(t1, ar, ar, ALU.mult)
        tt(t2, ai, ai, ALU.mult)
        tt(den, t1, t2, ALU.add)
        P.op("dve", lambda: nc.vector.reciprocal(out=den, in_=den), **rw)
        tt(PWr[:, 6, :], ar, den, ALU.mult)
        tt(t1, ai, den, ALU.mult)
        ts(PWi[:, 6, :], t1, -1.0, ALU.mult)
        for n in range(2, 8):
            cmul(PWr[:, 7 - n, :], PWi[:, 7 - n, :], PWr[:, 8 - n, :], PWi[:, 8 - n, :], PWr[:, 6, :], PWi[:, 6, :])
        cp(AKr[:, 0, :], PWr[:, 15, :])
        cp(AKi[:, 0, :], PWi[:, 15, :])
        for k in range(1, 9):
            cmul(AKr[:, k, :], AKi[:, k, :], AKr[:, k - 1, :], AKi[:, k - 1, :], AKr[:, k - 1, :], AKi[:, k - 1, :])
        ts(AKs[:], AKi[:], sig[:, 1:2], ALU.mult)
        for s_ in range(15):
            cp(PRr[:, s_, :], PWr[:, 14 - s_, :])
            ts(PRi[:, s_, :], PWi[:, 14 - s_, :], sig[:, 0:1], ALU.mult)
        ts(QRr[:], PWr[:, 7:16, :], sig[:, 1:2], ALU.mult)
        ts(QRi[:], PWi[:, 7:16, :], -1.0, ALU.mult)

        utok = AR.alloc([128, 8192], BF16)
        R_ut = Res("utok")
        PSr = SubPool(PS.t[4:8])
        for cb in range(4):
            P.dma(utok[:], u_s.ap()[cb * 128:(cb + 1) * 128, :], reads=[R_u], writes=[R_ut])
            for g8 in range(8):
                ps, rps = PSr.get()
                psb = ps.bitcast(BF16)
                for gg in range(8):
                    g = g8 * 8 + gg
                    P.op("pe", lambda: nc.tensor.transpose(psb[:, gg * 128:(gg + 1) * 128], utok[:, g * 128:(g + 1) * 128], ident_b[:]),
                         reads=[R_ut, R_c], writes=[rps])
                dst = BIGA[:, g8, :].rearrange("p (g c) -> p g c", c=512)[:, :, cb * 128:(cb + 1) * 128]
                src = psb[:, :].rearrange("p (g c) -> p g c", c=128)
                if g8 % 2 == 0:
                    P.op("act", lambda: nc.scalar.copy(out=dst, in_=src), reads=[rps], writes=[R_A[g8]])
                else:
                    P.op("dve", lambda: nc.vector.tensor_copy(out=dst, in_=src), reads=[rps], writes=[R_A[g8]])

        pfp = TPool(nc, "pf", [128, 4, 240], BF16, 2, arena=AR)
        qfp = TPool(nc, "qf", [128, 4, 144], BF16, 2, arena=AR)
        tP1 = AR.alloc([128, 4, 15, 16], F32)
        tP2 = AR.alloc([128, 4, 15, 16], F32)
        rkp = TPool(nc, "rk", [128, 9, 128], BF16, 2, arena=AR)
        minp = TPool(nc, "min", [128, 128], BF16, 2, arena=AR)
        mintp = TPool(nc, "mint", [128, 128], BF16, 4, arena=AR)
        xzp = TPool(nc, "xz", [128, 514], BF16, 5, arena=AR)
        ystp = TPool(nc, "yst", [128, 8, 64], BF16, 2, arena=AR)
        for xz_, rxz_ in xzp.t:
            P.op("pool", lambda: nc.gpsimd.memset(xz_[:, 0:1], 0.0), writes=[rxz_])
        psY = PS.t[0:4]
        BT1g, BT2g, CTag, CTbg = g3(BT1[:]), g3(BT2[:]), g3(CTa), g3(CTb)
        for bi in range(16):
            g0 = bi * 4
            pf, rpf = pfp.get()
            qf, rqf = qfp.get()
            pf4 = pf[:].rearrange("p g (b h) -> p g b h", h=16)
            qf4 = qf[:].rearrange("p g (b h) -> p g b h", h=16)
            e_ = lambda tab, nb: tab[:, :, g0:g0 + 4].rearrange("p b g -> p g b").unsqueeze(3).broadcast_to([128, 4, nb, 16])
            b_ = lambda tab, nb: tab[:, g0:g0 + 4, :].unsqueeze(2).broadcast_to([128, 4, nb, 16])
            rwp = dict(reads=[R_sp], writes=[R_sp])
            P.op("dve", lambda: nc.vector.tensor_tensor(out=tP1[:], in0=e_(PRr, 15), in1=b_(BT1g, 15), op=ALU.mult), **rwp)
            P.op("dve", lambda: nc.vector.tensor_tensor(out=tP2[:], in0=e_(PRi, 15), in1=b_(BT2g, 15), op=ALU.mult), **rwp)
            P.op("dve", lambda: nc.vector.tensor_tensor(out=pf4, in0=tP1[:], in1=tP2[:], op=ALU.add), reads=[R_sp], writes=[R_sp, rpf])
            P.op("dve", lambda: nc.vector.tensor_tensor(out=tP1[:, :, 0:9, :], in0=e_(QRr, 9), in1=b_(CTag, 9), op=ALU.mult), **rwp)
            P.op("dve", lambda: nc.vector.tensor_tensor(out=tP2[:, :, 0:9, :], in0=e_(QRi, 9), in1=b_(CTbg, 9), op=ALU.mult), **rwp)
            P.op("dve", lambda: nc.vector.tensor_tensor(out=qf4, in0=tP1[:, :, 0:9, :], in1=tP2[:, :, 0:9, :], op=ALU.add),
                 reads=[R_sp], writes=[R_sp, rqf])
            for gl in range(4):
                g = g0 + gl
                U_g = BIGA[:, g // 8, (g % 8) * 512:(g % 8 + 1) * 512]
                R_U = R_A[g // 8]
                rk, rrk = rkp.get()
                for k in range(9):
                    tf, rtf = f32p.get()
                    P.op("dve", lambda: nc.vector.tensor_scalar(out=tf[:, 0:128], in0=ident_f[:], scalar1=AKr[:, k, g:g + 1], scalar2=None,
                                                                op0=ALU.mult), reads=[R_sp, R_c], writes=[rtf])
                    P.op("dve", lambda: nc.vector.scalar_tensor_tensor(out=rk[:, k, :], in0=jswap_f[:], scalar=AKs[:, k, g:g + 1],
                                                                       in1=tf[:, 0:128], op0=ALU.mult, op1=ALU.add),
                         reads=[R_sp, R_c, rtf], writes=[rrk])
                ps, rps = PSr.get()
                psb = ps.bitcast(BF16)
                P.op("pe", lambda: nc.tensor.transpose(psb[:, 0:128], pf[:, gl, 0:128], ident_b[:]), reads=[rpf, R_c], writes=[rps])
                mi, rmi = minp.get()
                P.op("act", lambda: nc.scalar.copy(out=mi[:], in_=psb[:, 0:128]), reads=[rps], writes=[rmi])
                ps, rps = PSr.get()
                P.op("pe", lambda: nc.tensor.matmul(ps[:, 0:128], lhsT=pf[:, gl, 112:240], rhs=qf[:, gl, 0:128], start=True, stop=True),
                     reads=[rpf, rqf], writes=[rps])
                tf, rtf = f32p.get()
                P.op("dve", lambda: nc.vector.tensor_tensor(out=tf[:, 0:128], in0=ps[:, 0:128], in1=mask_intra[:], op=ALU.mult),
                     reads=[rps, R_c], writes=[rtf])
                mt, rmt = mintp.get()
                P.op("dve", lambda: nc.vector.scalar_tensor_tensor(out=mt[:], in0=ident_f[:], scalar=drep[:, g:g + 1], in1=tf[:, 0:128],
                                                                   op0=ALU.mult, op1=ALU.add), reads=[R_sp, R_c, rtf], writes=[rmt])
                xz, rxz = xzp.get()
                ps, rps = PSr.get()
                P.op("pe", lambda: nc.tensor.matmul(ps[:], lhsT=mi[:], rhs=U_g, start=True, stop=True), reads=[rmi, R_U], writes=[rps])
                P.op("act", lambda: nc.scalar.copy(out=xz[:, 1:513], in_=ps[:]), reads=[rps], writes=[rxz])
                for k in range(9):
                    sh = 1 << k
                    ps, rps = PSr.get()
                    P.op("pe", lambda: nc.tensor.matmul(ps[:, 0:512 - sh], lhsT=rk[:, k, :], rhs=xz[:, 1:513 - sh], start=True, stop=True),
                         reads=[rrk, rxz], writes=[rps])
                    P.op("dve", lambda: nc.vector.tensor_tensor(out=xz[:, 1 + sh:513], in0=ps[:, 0:512 - sh], in1=xz[:, 1 + sh:513],
                                                                op=ALU.add), reads=[rps], writes=[rxz])
                for cb in range(4):
                    P.op("pe", lambda: nc.tensor.matmul(psY[cb][0][:, gl * 128:(gl + 1) * 128], lhsT=U_g[:, cb * 128:(cb + 1) * 128], rhs=mt[:],
                                                        start=True, stop=False), reads=[R_U, rmt], writes=[psY[cb][1]])
                    P.op("pe", lambda: nc.tensor.matmul(psY[cb][0][:, gl * 128:(gl + 1) * 128], lhsT=xz[:, cb * 128:(cb + 1) * 128],
                                                        rhs=qf[:, gl, 16:144], start=False, stop=True), reads=[rxz, rqf], writes=[psY[cb][1]])
            for cb in range(4):
                yst, ryst = ystp.get()
                if "s5raw" in dbg:
                    P.op("act", lambda: nc.scalar.copy(out=yst[:].rearrange("p j (g h) -> p g j h", h=16),
                                                       in_=psY[cb][0][:].rearrange("p (g j h) -> p g j h", g=4, j=8)),
                         reads=[psY[cb][1]], writes=[ryst])
                else:
                    P.op("act", lambda: nc.scalar.activation(out=yst[:].rearrange("p j (g h) -> p g j h", h=16),
                                                             in_=psY[cb][0][:].rearrange("p (g j h) -> p g j h", g=4, j=8),
                                                             func=AF.Gelu_apprx_tanh), reads=[psY[cb][1]], writes=[ryst])
                P.dma(y_s.ap()[1024 * cb:1024 * (cb + 1), 64 * bi:64 * bi + 64].rearrange("(c j) n -> c j n", j=8), yst[:],
                      reads=[ryst], writes=[R_ys])

        P.barrier()
        AR.reset()
        ytp = TPool(nc, "ytk", [128, 1024], BF16, 2, arena=AR)
        for tt_ in range(32):
            yt, ryt = ytp.get()
            P.dma(yt[:], y_s.ap()[tt_ * 128:(tt_ + 1) * 128, :], reads=[R_ys], writes=[ryt])
            ps, rps = PSr.get()
            psb = ps.bitcast(BF16)
            for kc in range(8):
                P.op("pe", lambda: nc.tensor.transpose(psb[:, kc * 128:(kc + 1) * 128], yt[:, kc * 128:(kc + 1) * 128], ident_b[:]),
                     reads=[ryt, R_c], writes=[rps])
            dst = BIGA[:, :, tt_ * 128:(tt_ + 1) * 128]
            src = psb[:, :].rearrange("p (k c) -> p k c", c=128)
            if tt_ % 2 == 0:
                P.op("act", lambda: nc.scalar.copy(out=dst, in_=src), reads=[rps], writes=R_A)
            else:
                P.op("dve", lambda: nc.vector.tensor_copy(out=dst, in_=src), reads=[rps], writes=R_A)
        wp4 = TPool(nc, "wt4", [128, 8, 512], BF16, 4, arena=AR)
        for grp in range(2):
            w1t, rw1 = wp4.get()
            P.dma(w1t[:], W["s5_glu_w1"].ap()[l][:, grp * 512:(grp + 1) * 512].rearrange("(kc p) n -> p kc n", p=128), writes=[rw1], q="pool")
            w2t, rw2 = wp4.get()
            P.dma(w2t[:], W["s5_glu_w2"].ap()[l][:, grp * 512:(grp + 1) * 512].rearrange("(kc p) n -> p kc n", p=128), writes=[rw2], q="pool")
            for ci in range(4):
                for tb in range(NTB):
                    psa, rpsa = mm_fm(w1t, rw1, ci, tb)
                    psb_, rpsb = mm_fm(w2t, rw2, ci, tb)
                    sg_, rsg = f32p.get()
                    P.op("act", lambda: nc.scalar.activation(out=sg_[:], in_=psb_[:], func=AF.Sigmoid), reads=[rpsb], writes=[rsg])
                    o, ro = b16p.get()
                    P.op("dve", lambda: nc.vector.tensor_tensor(out=o[:], in0=psa[:], in1=sg_[:], op=ALU.mult), reads=[rpsa, rsg], writes=[ro])
                    r0_ = D + (grp * 4 + ci) * 128
                    P.dma(br_s.ap()[r0_:r0_ + 128, tb * TB:(tb + 1) * TB], o[:], reads=[ro], writes=[R_br])

    for l in range(nlayers):
        x_src, R_xsrc = (xT_in, Res("xin")) if l == 0 else (xmidT_s, R_xmid)
        x_dst, R_xdst = (yT_out, Res("yout")) if l == nlayers - 1 else (xmidT_s, R_xmid)
        P.barrier()
        P.dma(vec[:], vecs_in.ap()[l], writes=[R_vec])
        P.dma(lam4[:], lamv_in.ap()[l], writes=[R_lam])
        lam_init = 0.8 - 0.6 * math.exp(-0.3 * l)
        lt, rlt = f32p.get()
        P.op("dve", lambda: nc.vector.tensor_tensor(out=lt[:, 0:64], in0=lam4[:, 0:64], in1=lam4[:, 64:128], op=ALU.mult),
             reads=[R_lam], writes=[rlt])
        P.op("dve", lambda: nc.vector.tensor_tensor(out=lt[:, 64:128], in0=lam4[:, 128:192], in1=lam4[:, 192:256], op=ALU.mult),
             reads=[R_lam], writes=[rlt])
        P.op("dve", lambda: nc.vector.reduce_sum(out=lt[:, 128:130], in_=lt[:, 0:128].rearrange("p (a b) -> p a b", a=2),
                                                 axis=AX.X), reads=[rlt], writes=[rlt])
        P.op("act", lambda: nc.scalar.activation(out=lt[:, 130:132], in_=lt[:, 128:130], func=AF.Exp), reads=[rlt], writes=[rlt])
        P.op("dve", lambda: nc.vector.scalar_tensor_tensor(out=neglam[:], in0=lt[:, 131:132], scalar=-lam_init, in1=lt[:, 130:131],
                                                           op0=ALU.add, op1=ALU.subtract), reads=[rlt], writes=[R_lam])
        P.op("dve", lambda: nc.vector.tensor_scalar(out=subsc[:], in0=vec[:, V_SUB:V_SUB + 1], scalar1=(1.0 - lam_init), scalar2=None,
                                                    op0=ALU.mult), reads=[R_vec], writes=[R_lam])

        norm_stage(x_src, R_xsrc, V_GMIX)

        w_in2 = W["w_in"].ap()[l]
        for seg, (off0, gcol, dst, R_dst) in enumerate([(OFF_Q, V_QG, qT_s, R_qT), (OFF_K, V_KG, kT_s, R_kT)]):
            for grp in range(2):
                wt, rwt = load_w(w_in2, [(off0 + grp * 512, 512)])
                for ci in range(4):
                    for tb in range(NTB):
                        ps, rps = mm_fm(wt, rwt, ci, tb)
                        qk_epilogue(ps, rps, gcol, dst, R_dst, (grp * 4 + ci) * 128, tb)
        for grp in range(2):
            wt, rwt = load_w(w_in2, [(OFF_V + grp * 512, 512)])
            for tt in range(32):
                ps, rps = PS.get()
                for kc in range(8):
                    P.op("pe", lambda: nc.tensor.matmul(ps[:], lhsT=BIGA[:, kc, tt * 128:(tt + 1) * 128], rhs=wt[:, kc, :],
                                                        start=(kc == 0), stop=(kc == 7)), reads=[rwt, R_A[kc]], writes=[rps])
                o, ro = b16p.get()
                if tt % 2 == 0:
                    P.op("act", lambda: nc.scalar.copy(out=o[:], in_=ps[:]), reads=[rps], writes=[ro])
                else:
                    P.op("dve", lambda: nc.vector.tensor_copy(out=o[:], in_=ps[:]), reads=[rps], writes=[ro])
                P.dma(v_s.ap()[tt * 128:(tt + 1) * 128, grp * 512:(grp + 1) * 512], o[:], reads=[ro], writes=[R_v])
        for grp in range(2):
            wt, rwt = load_w(w_in2, [(OFF_U + grp * 512, 512)])
            for cb in range(4):
                us, rus = ustage.get()
                for j in range(8):
                    ps, rps = PS.get()
                    for kc in range(8):
                        P.op("pe", lambda: nc.tensor.matmul(ps[:], lhsT=BIGA[:, kc, 1024 * cb + j:1024 * (cb + 1):8], rhs=wt[:, kc, :],
                                                            start=(kc == 0), stop=(kc == 7)), reads=[rwt, R_A[kc]], writes=[rps])
                    src_v = ps[:].rearrange("p (g h) -> p g h", h=16)
                    if j % 2 == 0:
                        P.op("act", lambda: nc.scalar.copy(out=us[:, :, j, :], in_=src_v), reads=[rps], writes=[rus])
                    else:
                        P.op("dve", lambda: nc.vector.tensor_copy(out=us[:, :, j, :], in_=src_v), reads=[rps], writes=[rus])
                P.dma(u_s.ap()[cb * 128:(cb + 1) * 128, grp * 4096:(grp + 1) * 4096], us[:].rearrange("p g j h -> p (g j h)"),
                      reads=[rus], writes=[R_u])
        for grp in range(4):
            wt, rwt = load_w(w_in2, [(OFF_C + grp * 256, 256), (OFF_C + 1024 + grp * 256, 256)])
            for ci in range(2):
                for tb in range(NTB):
                    psa, rpsa = mm_fm(wt, rwt, ci, tb)
                    psb, rpsb = mm_fm(wt, rwt, ci + 2, tb)
                    sg_, rsg = f32p.get()
                    P.op("act", lambda: nc.scalar.activation(out=sg_[:], in_=psb[:], func=AF.Sigmoid), reads=[rpsb], writes=[rsg])
                    o, ro = b16p.get()
                    P.op("dve", lambda: nc.vector.tensor_tensor(out=o[:], in0=psa[:], in1=sg_[:], op=ALU.mult),
                         reads=[rpsa, rsg], writes=[ro])
                    r0 = (grp * 2 + ci) * 128
                    P.dma(cT_s.ap()[r0:r0 + 128, tb * TB:(tb + 1) * TB], o[:], reads=[ro], writes=[R_cT])
        for grp in range(6):
            wt, rwt = load_w(w_in2, [(OFF_G + grp * 512, 512)])
            for ci in range(4):
                for tb in range(NTB):
                    ps, rps = mm_fm(wt, rwt, ci, tb)
                    o, ro = b16p.get()
                    P.op("act", lambda: nc.scalar.activation(out=o[:], in_=ps[:], func=AF.Sigmoid), reads=[rps], writes=[ro])
                    r0 = (grp * 4 + ci) * 128
                    P.dma(gT_s.ap()[r0:r0 + 128, tb * TB:(tb + 1) * TB], o[:], reads=[ro], writes=[R_gT])
        if "stop_s2" in dbg:
            break

        P.barrier()
        AR.reset()
        kpool = TPool(nc, "kT", [128, S], BF16, 2, arena=AR)
        qpool = TPool(nc, "qT", [128, S], BF16, 2, arena=AR)
        vpool = TPool(nc, "vh", [128, 32, 128], BF16, 2, arena=AR)
        epool = TPool(nc, "eT", [128, 512], BF16, 6, arena=AR)
        o32 = TPool(nc, "o32", [128, 512], F32, 4, arena=AR)
        psO = [PS.t[0], PS.t[1]]
        psS = [PS.t[2], PS.t[3]]
        PSs = SubPool(PS.t[4:8])
        for h in range(8):
            kt, rkt = kpool.get()
            P.dma(kt[:], kT_s.ap()[h * 128:(h + 1) * 128, :], reads=[R_kT], writes=[rkt])
            qt, rqt = qpool.get()
            P.dma(qt[:], qT_s.ap()[h * 128:(h + 1) * 128, :], reads=[R_qT], writes=[rqt])
            vt, rvt = vpool.get()
            P.dma(vt[:], v_s.ap()[:, h * 128:(h + 1) * 128].rearrange("(j p) e -> p j e", p=128), reads=[R_v], writes=[rvt])
            for qb in range(NTB):
                q0 = qb * TB
                nj = 4 * qb + 4
                for j in range(nj):
                    lo = max(0, 128 * j - q0)
                    for c in range(2):
                        pss, rpss = PSs.get()
                        P.op("pe", lambda: nc.tensor.matmul(pss[:, lo:512], lhsT=kt[64 * c:64 * c + 64, 128 * j:128 * j + 128],
                                                            rhs=qt[64 * c:64 * c + 64, q0 + lo:q0 + 512], start=True, stop=True),
                             reads=[rkt, rqt], writes=[rpss])
                        et, ret = epool.get()
                        P.op("act", lambda: nc.scalar.activation(out=et[:, lo:512], in_=pss[:, lo:512], func=AF.Exp, scale=0.125),
                             reads=[rpss], writes=[ret])
                        if j >= 4 * qb:
                            a = 128 * (j - 4 * qb)
                            b = min(a + 256, 512)
                            ba = 0
                        elif j == 4 * qb - 1:
                            a, b, ba = 0, 128, 128
                        else:
                            a = None
                        if a is not None:
                            P.op("dve", lambda: nc.vector.tensor_tensor(out=et[:, a:b], in0=et[:, a:b], in1=expB[:, h, ba:ba + (b - a)],
                                                                        op=ALU.mult), reads=[ret, R_c], writes=[ret])
                        P.op("pe", lambda: nc.tensor.matmul(psO[c][0][:, lo:512], lhsT=vt[:, j, :], rhs=et[:, lo:512],
                                                            start=(j == 0), stop=(j == nj - 1)), reads=[rvt, ret], writes=[psO[c][1]])
                        P.op("pe", lambda: nc.tensor.matmul(psS[c][0][:, lo:512], lhsT=ones_b[:], rhs=et[:, lo:512],
                                                            start=(j == 0), stop=(j == nj - 1)), reads=[ret, R_c], writes=[psS[c][1]])
                r0, rr0 = o32.get()
                t0, rt0 = o32.get()
                t1, rt1 = o32.get()
                P.op("dve", lambda: nc.vector.reciprocal(out=r0[:], in_=psS[0][0][:]), reads=[psS[0][1]], writes=[rr0])
                P.op("dve", lambda: nc.vector.tensor_tensor(out=t0[:], in0=psO[0][0][:], in1=r0[:], op=ALU.mult),
                     reads=[psO[0][1], rr0], writes=[rt0])
                P.op("dve", lambda: nc.vector.reciprocal(out=r0[:], in_=psS[1][0][:]), reads=[psS[1][1]], writes=[rr0])
                P.op("dve", lambda: nc.vector.tensor_tensor(out=t1[:], in0=psO[1][0][:], in1=r0[:], op=ALU.mult),
                     reads=[psO[1][1], rr0], writes=[rt1])
                P.op("dve", lambda: nc.vector.scalar_tensor_tensor(out=t0[:], in0=t1[:], scalar=neglam[:, 0:1], in1=t0[:],
                                                                   op0=ALU.mult, op1=ALU.add), reads=[rt1, R_lam], writes=[rt0])
                sq, rsq = epool.get()
                P.op("act", lambda: nc.scalar.activation(out=sq[:], in_=t0[:], func=AF.Square), reads=[rt0], writes=[rsq])
                pss, rpss = PSs.get()
                P.op("pe", lambda: nc.tensor.matmul(pss[:], lhsT=ones_b[:], rhs=sq[:], start=True, stop=True),
                     reads=[rsq, R_c], writes=[rpss])
                P.op("act", lambda: nc.scalar.activation(out=r0[:], in_=pss[:], func=AF.Sqrt, scale=1.0 / 128, bias=epsc[:, 0:1]),
                     reads=[rpss, R_c], writes=[rr0])
                P.op("dve", lambda: nc.vector.reciprocal(out=r0[:], in_=r0[:]), reads=[rr0], writes=[rr0])
                P.op("dve", lambda: nc.vector.scalar_tensor_tensor(out=BIGA[:, h, q0:q0 + TB], in0=t0[:], scalar=subsc[:, 0:1], in1=r0[:],
                                                                   op0=ALU.mult, op1=ALU.mult), reads=[rt0, rr0, R_lam], writes=[R_A[h]])

        wpool3 = TPool(nc, "wt3", [128, 8, 512], BF16, 2, arena=AR)

        def proj_to_br(wname, row_base, wp):
            for grp in range(2):
                wt, rwt = wp.get()
                P.dma(wt[:], W[wname].ap()[l][:, grp * 512:(grp + 1) * 512].rearrange("(kc p) n -> p kc n", p=128),
                      writes=[rwt], q="pool")
                for ci in range(4):
                    for tb in range(NTB):
                        ps, rps = mm_fm(wt, rwt, ci, tb)
                        o, ro = b16p.get()
                        if tb % 2 == 0:
                            P.op("act", lambda: nc.scalar.copy(out=o[:], in_=ps[:]), reads=[rps], writes=[ro])
                        else:
                            P.op("dve", lambda: nc.vector.tensor_copy(out=o[:], in_=ps[:]), reads=[rps], writes=[ro])
                        r0_ = row_base + (grp * 4 + ci) * 128
                        P.dma(br_s.ap()[r0_:r0_ + 128, tb * TB:(tb + 1) * TB], o[:], reads=[ro], writes=[R_br])
        proj_to_br("w_attn_out", 0, wpool3)

        s5_stage(l)

        P.barrier()
        AR.reset()
        diagp = TPool(nc, "diag", [128, 31, 128], BF16, 3, arena=AR)
        wc = AR.alloc([128, 8, 1024], BF16)
        R_wc = Res("wc")
        P.dma(wc[:], W["conv_w_out"].ap()[l].rearrange("(kc p) n -> p kc n", p=128), writes=[R_wc], q="pool")
        cin = TPool(nc, "cin", [128, 544], BF16, 3, arena=AR)
        cvp = TPool(nc, "cv", [128, 8, 512], F32, 1, arena=AR)
        xbp = TPool(nc, "xb", [128, 8, 512], BF16, 1, arena=AR)
        sqp5 = TPool(nc, "sq5", [128, 8, 512], BF16, 1, arena=AR)
        ynp = TPool(nc, "yn", [128, 8, 512], BF16, 2, arena=AR)
        for tb in range(NTB):
            cv, rcv = cvp.get()
            xb, rxb = xbp.get()
            sq, rsq = sqp5.get()
            for kc in range(8):
                ct, rct = cin.get()
                if tb == 0:
                    P.op("pool", lambda: nc.gpsimd.memset(ct[:, 0:30], 0.0), writes=[rct])
                    P.dma(ct[:, 30:542], cT_s.ap()[kc * 128:(kc + 1) * 128, 0:512], reads=[R_cT], writes=[rct])
                else:
                    P.dma(ct[:, 0:542], cT_s.ap()[kc * 128:(kc + 1) * 128, tb * TB - 30:tb * TB + 512], reads=[R_cT], writes=[rct])
                diag, R_diag = diagp.get()
                for k in range(31):
                    P.op("pool", lambda: nc.gpsimd.tensor_scalar(out=diag[:, k, :], in0=ident_f[:],
                                                                 scalar1=vec[:, V_CW + kc * 31 + k:V_CW + kc * 31 + k + 1], scalar2=None,
                                                                 op0=ALU.mult), reads=[R_vec, R_c], writes=[R_diag])
                ps, rps = PS.get()
                for k in range(31):
                    P.op("pe", lambda: nc.tensor.matmul(ps[:], lhsT=diag[:, k, :], rhs=ct[:, k:k + 512], start=(k == 0), stop=(k == 30)),
                         reads=[R_diag, rct], writes=[rps])
                P.op("act", lambda: nc.scalar.activation(out=cv[:, kc, :], in_=ps[:], func=AF.Identity, bias=vec[:, V_CB + kc:V_CB + kc + 1]),
                     reads=[rps, R_vec], writes=[rcv])
                P.op("dve", lambda: nc.vector.tensor_copy(out=xb[:, kc, :], in_=cv[:, kc, :]), reads=[rcv], writes=[rxb])
                P.op("act", lambda: nc.scalar.activation(out=sq[:, kc, :], in_=cv[:, kc, :], func=AF.Square), reads=[rcv], writes=[rsq])
            ps1, rps1 = PS.get()
            ps2, rps2 = PS.get()
            for kc in range(8):
                P.op("pe", lambda: nc.tensor.matmul(ps1[:], lhsT=ones_b[:], rhs=xb[:, kc, :], start=(kc == 0), stop=(kc == 7)),
                     reads=[rxb, R_c], writes=[rps1])
            for kc in range(8):
                P.op("pe", lambda: nc.tensor.matmul(ps2[:], lhsT=ones_b[:], rhs=sq[:, kc, :], start=(kc == 0), stop=(kc == 7)),
                     reads=[rsq, R_c], writes=[rps2])
            mean, rmean = f32p.get()
            msq, rmsq = f32p.get()
            rs, rrs = f32p.get()
            P.op("act", lambda: nc.scalar.mul(out=mean[:], in_=ps1[:], mul=1.0 / D), reads=[rps1], writes=[rmean])
            P.op("dve", lambda: nc.vector.tensor_tensor(out=msq[:], in0=mean[:], in1=mean[:], op=ALU.mult), reads=[rmean], writes=[rmsq])
            P.op("dve", lambda: nc.vector.scalar_tensor_tensor(out=rs[:], in0=ps2[:], scalar=1.0 / D, in1=msq[:], op0=ALU.mult,
                                                               op1=ALU.subtract), reads=[rps2, rmsq], writes=[rrs])
            P.op("act", lambda: nc.scalar.activation(out=rs[:], in_=rs[:], func=AF.Sqrt, bias=epsc[:, 0:1]), reads=[rrs, R_c], writes=[rrs])
            P.op("dve", lambda: nc.vector.reciprocal(out=rs[:], in_=rs[:]), reads=[rrs], writes=[rrs])
            yn, ryn = ynp.get()
            for kc in range(8):
                P.op("dve", lambda: nc.vector.tensor_tensor(out=cv[:, kc, :], in0=cv[:, kc, :], in1=mean[:], op=ALU.subtract),
                     reads=[rmean], writes=[rcv])
                P.op("dve", lambda: nc.vector.tensor_tensor(out=cv[:, kc, :], in0=cv[:, kc, :], in1=rs[:], op=ALU.mult),
                     reads=[rrs], writes=[rcv])
                P.op("act", lambda: nc.scalar.activation(out=yn[:, kc, :], in_=cv[:, kc, :], func=AF.Silu,
                                                         scale=vec[:, V_LNG + kc:V_LNG + kc + 1], bias=vec[:, V_LNB + kc:V_LNB + kc + 1]),
                     reads=[rcv, R_vec], writes=[ryn])
            for co in range(8):
                ps, rps = PS.get()
                for kc in range(8):
                    P.op("pe", lambda: nc.tensor.matmul(ps[:], lhsT=wc[:, kc, co * 128:(co + 1) * 128], rhs=yn[:, kc, :],
                                                        start=(kc == 0), stop=(kc == 7)), reads=[R_wc, ryn], writes=[rps])
                o, ro = b16p.get()
                if co % 2 == 0:
                    P.op("act", lambda: nc.scalar.copy(out=o[:], in_=ps[:]), reads=[rps], writes=[ro])
                else:
                    P.op("dve", lambda: nc.vector.tensor_copy(out=o[:], in_=ps[:]), reads=[rps], writes=[ro])
                r0_ = 2 * D + co * 128
                P.dma(br_s.ap()[r0_:r0_ + 128, tb * TB:(tb + 1) * TB], o[:], reads=[ro], writes=[R_br])

        P.barrier()
        AR.reset()
        wo = AR.alloc([128, 8, 1024], BF16)
        R_wo = Res("wo")
        P.dma(wo[:], W["w_out"].ap()[l].rearrange("(kc p) n -> p kc n", p=128), writes=[R_wo], q="pool")
        inp6 = TPool(nc, "in6", [128, 512], BF16, 12, arena=AR)
        mixp = TPool(nc, "mix", [128, 8, 512], BF16, 2, arena=AR)
        xin6 = TPool(nc, "xin6", [128, 8, 512], F32, 1, arena=AR)
        x1p = TPool(nc, "x1p", [128, 8, 512], F32, 2, arena=AR)
        sqp6 = TPool(nc, "sq6", [128, 8, 512], BF16, 1, arena=AR)
        for tb in range(NTB):
            mix, rmix = mixp.get()
            for kc in range(8):
                tl = []
                for i in range(3):
                    bt, rbt = inp6.get()
                    P.dma(bt[:], br_s.ap()[i * D + kc * 128:i * D + (kc + 1) * 128, tb * TB:(tb + 1) * TB], reads=[R_br], writes=[rbt])
                    gt, rgt = inp6.get()
                    P.dma(gt[:], gT_s.ap()[i * D + kc * 128:i * D + (kc + 1) * 128, tb * TB:(tb + 1) * TB], reads=[R_gT], writes=[rgt])
                    tl.append((bt, rbt, gt, rgt))
                ta, rta = f32p.get()
                tb2, rtb2 = f32p.get()
                P.op("pool", lambda: nc.gpsimd.tensor_tensor(out=ta[:], in0=tl[0][0][:], in1=tl[0][2][:], op=ALU.mult),
                     reads=[tl[0][1], tl[0][3]], writes=[rta])
                P.op("pool", lambda: nc.gpsimd.tensor_tensor(out=tb2[:], in0=tl[1][0][:], in1=tl[1][2][:], op=ALU.mult),
                     reads=[tl[1][1], tl[1][3]], writes=[rtb2])
                P.op("dve", lambda: nc.vector.tensor_tensor(out=ta[:], in0=ta[:], in1=tb2[:], op=ALU.add), reads=[rtb2], writes=[rta])
                P.op("pool", lambda: nc.gpsimd.tensor_tensor(out=tb2[:], in0=tl[2][0][:], in1=tl[2][2][:], op=ALU.mult),
                     reads=[tl[2][1], tl[2][3]], writes=[rtb2])
                P.op("dve", lambda: nc.vector.tensor_tensor(out=mix[:, kc, :], in0=ta[:], in1=tb2[:], op=ALU.add),
                     reads=[rta, rtb2], writes=[rmix])
            xb, rxb = xin6.get()
            P.dma(xb[:], x_src.ap()[:, tb * TB:(tb + 1) * TB].rearrange("(kc p) t -> p kc t", p=128), reads=[R_xsrc], writes=[rxb])
            x1, rx1 = x1p.get()
            for co in range(8):
                ps, rps = PS.get()
                for kc in range(8):
                    P.op("pe", lambda: nc.tensor.matmul(ps[:], lhsT=wo[:, kc, co * 128:(co + 1) * 128], rhs=mix[:, kc, :],
                                                        start=(kc == 0), stop=(kc == 7)), reads=[R_wo, rmix], writes=[rps])
                P.op("dve", lambda: nc.vector.tensor_tensor(out=x1[:, co, :], in0=ps[:], in1=xb[:, co, :], op=ALU.add),
                     reads=[rps, rxb], writes=[rx1])
            P.dma(x1T_s.ap()[:, tb * TB:(tb + 1) * TB].rearrange("(kc p) t -> p kc t", p=128), x1[:], reads=[rx1], writes=[R_x1])
            norm_block(x1, rx1, V_GFFN, tb, sqp6)

        P.barrier()
        AR.reset()
        wpool8 = TPool(nc, "wt8", [128, 8, 512], BF16, 2, arena=AR)
        ufull = TPool(nc, "uf", [128, 2 + S], BF16, 8, arena=AR)
        dgp = TPool(nc, "dg", [128, 4, 3, 128], BF16, 2, arena=AR)
        w_up2 = W["ffn_w_up"].ap()[l]
        for grp in range(11):
            wt, rwt = wpool8.get()
            P.dma(wt[:, :, 0:256], w_up2[:, grp * 256:(grp + 1) * 256].rearrange("(kc p) n -> p kc n", p=128), writes=[rwt], q="pool")
            P.dma(wt[:, :, 256:512], w_up2[:, FFN_H + grp * 256:FFN_H + (grp + 1) * 256].rearrange("(kc p) n -> p kc n", p=128),
                  writes=[rwt], q="pool")
            dgt, rdg = dgp.get()
            for ci in range(4):
                gch = (2 * grp + ci) if ci < 2 else (22 + 2 * grp + ci - 2)
                for k in range(3):
                    P.op("pool", lambda: nc.gpsimd.tensor_scalar(out=dgt[:, ci, k, :], in0=ident_f[:],
                                                                 scalar1=vec[:, V_FW + gch * 3 + k:V_FW + gch * 3 + k + 1], scalar2=None,
                                                                 op0=ALU.mult), reads=[R_vec, R_c], writes=[rdg])
            ufs = [ufull.get() for _ in range(4)]
            for ci in range(4):
                P.op("pool", lambda: nc.gpsimd.memset(ufs[ci][0][:, 0:2], 0.0), writes=[ufs[ci][1]])
                for tb in range(NTB):
                    ps, rps = mm_fm(wt, rwt, ci, tb)
                    if tb % 2 == 0:
                        P.op("act", lambda: nc.scalar.copy(out=ufs[ci][0][:, 2 + tb * TB:2 + (tb + 1) * TB], in_=ps[:]),
                             reads=[rps], writes=[ufs[ci][1]])
                    else:
                        P.op("dve", lambda: nc.vector.tensor_copy(out=ufs[ci][0][:, 2 + tb * TB:2 + (tb + 1) * TB], in_=ps[:]),
                             reads=[rps], writes=[ufs[ci][1]])
            for pi in range(2):
                for tb in range(NTB):
                    psv, rpsv = PS.get()
                    psg, rpsg = PS.get()
                    for k in range(3):
                        P.op("pe", lambda: nc.tensor.matmul(psv[:], lhsT=dgt[:, pi, k, :], rhs=ufs[pi][0][:, tb * TB + k:tb * TB + k + TB],
                                                            start=(k == 0), stop=(k == 2)), reads=[rdg, ufs[pi][1]], writes=[rpsv])
                    for k in range(3):
                        P.op("pe", lambda: nc.tensor.matmul(psg[:], lhsT=dgt[:, pi + 2, k, :], rhs=ufs[pi + 2][0][:, tb * TB + k:tb * TB + k + TB],
                                                            start=(k == 0), stop=(k == 2)), reads=[rdg, ufs[pi + 2][1]], writes=[rpsg])
                    gl, rgl = f32p.get()
                    P.op("act", lambda: nc.scalar.activation(out=gl[:], in_=psg[:], func=AF.Gelu_apprx_tanh), reads=[rpsg], writes=[rgl])
                    o, ro = b16p.get()
                    P.op("dve", lambda: nc.vector.tensor_tensor(out=o[:], in0=psv[:], in1=gl[:], op=ALU.mult), reads=[rpsv, rgl], writes=[ro])
                    r0_ = (2 * grp + pi) * 128
                    P.dma(act_s.ap()[r0_:r0_ + 128, tb * TB:(tb + 1) * TB], o[:], reads=[ro], writes=[R_act])

        P.barrier()
        AR.reset()
        wd = AR.alloc([128, 22, 1024], BF16)
        R_wd = Res("wd")
        wdd = W["ffn_w_down"].ap()[l].rearrange("(kc p) n -> p kc n", p=128)
        P.dma(wd[:, 0:11, :], wdd[:, 0:11, :], writes=[R_wd], q="pool")
        P.dma(wd[:, 11:22, :], wdd[:, 11:22, :], writes=[R_wd], q="pool")
        actp = TPool(nc, "actp", [128, 22, 512], BF16, 2, arena=AR)
        xp9 = TPool(nc, "xp9", [128, 512], F32, 4, arena=AR)
        for tb in range(NTB):
            at, rat = actp.get()
            P.dma(at[:], act_s.ap()[:, tb * TB:(tb + 1) * TB].rearrange("(kc p) t -> p kc t", p=128), reads=[R_act], writes=[rat])
            for co in range(8):
                xr, rxr = xp9.get()
                P.dma(xr[:], x1T_s.ap()[co * 128:(co + 1) * 128, tb * TB:(tb + 1) * TB], reads=[R_x1], writes=[rxr])
                ps, rps = PS.get()
                for kc in range(22):
                    P.op("pe", lambda: nc.tensor.matmul(ps[:], lhsT=wd[:, kc, co * 128:(co + 1) * 128], rhs=at[:, kc, :],
                                                        start=(kc == 0), stop=(kc == 21)), reads=[R_wd, rat], writes=[rps])
                P.op("dve", lambda: nc.vector.tensor_tensor(out=xr[:], in0=ps[:], in1=xr[:], op=ALU.add), reads=[rps], writes=[rxr])
                P.dma(x_dst.ap()[co * 128:(co + 1) * 128, tb * TB:(tb + 1) * TB], xr[:], reads=[rxr], writes=[R_xdst])

    P.finish()
    return nc


_CACHE = {}


def make_in_maps(inputs, nb=8):
    common = {}
    for nm in ("w_in", "w_attn_out", "s5_glu_w1", "s5_glu_w2", "conv_w_out", "w_out", "ffn_w_up", "ffn_w_down"):
        common[nm] = np.ascontiguousarray(inputs[nm], dtype=np.float32)
    vs, ls, ss = [], [], []
    for l in range(DEPTH):
        v, lamv, s5 = host_layer_params(inputs, l)
        vs.append(v)
        ls.append(lamv)
        ss.append(s5)
    common["vecs"] = np.stack(vs)
    common["lamv"] = np.stack(ls)
    common["s5p"] = np.stack(ss)
    common["rel_bias"] = np.ascontiguousarray(inputs["rel_bias"], dtype=np.float32)
    common.update({"c_" + k: v for k, v in host_consts().items()})
    x = inputs["x"]
    in_maps = []
    for b in range(nb):
        m = dict(common)
        m["xT"] = np.ascontiguousarray(x[b].T)
        in_maps.append(m)
    return in_maps


def kernel(**inputs):
    inputs = {k: np.asarray(v) for k, v in inputs.items()}
    if "nc" not in _CACHE:
        _CACHE["nc"] = build()
    nc = _CACHE["nc"]
    in_maps = make_in_maps(inputs)
    res = run_bass_kernel_spmd(nc, in_maps, core_ids=list(range(8)))
    out = np.stack([np.ascontiguousarray(r["yT"].T) for r in res.results], 0)
    return out.astype(np.float32)
```

```python
import math
import numpy as np
import ml_dtypes
import concourse.bass as bass
import concourse.mybir as mybir
from concourse.bass_utils import run_bass_kernel_spmd

F32 = mybir.dt.float32
BF16 = mybir.dt.bfloat16
I32 = mybir.dt.int32
ALU = mybir.AluOpType
AF = mybir.ActivationFunctionType
AX = mybir.AxisListType

D = 1024
S = 4096
DEPTH = 2
TB = 512
NTB = S // TB
FFN_H = 2816
IN_W = 9216
OFF_Q, OFF_K, OFF_V, OFF_U, OFF_C, OFF_G = 0, 1024, 2048, 3072, 4096, 6144
EPS = 1e-6
GC = 1.5957691216057308


class Res:
    __slots__ = ("name", "lw", "rd")

    def __init__(self, name=""):
        self.name = name
        self.lw = None
        self.rd = []


class Prog:
    ENGS = ("pe", "dve", "act", "pool", "sp")

    def __init__(self, nc, n_dma_sems=32):
        self.nc = nc
        self.eng = {"pe": nc.tensor, "dve": nc.vector, "act": nc.scalar,
                    "pool": nc.gpsimd, "sp": nc.sync}
        self.sems = {}
        self.cnt = {}
        for e in self.ENGS:
            self.sems[e] = nc.alloc_semaphore("c_" + e)
            self.cnt[e] = 0
        self.n_dma = n_dma_sems
        for i in range(n_dma_sems):
            k = "d%d" % i
            self.sems[k] = nc.alloc_semaphore("s_" + k)
            self.cnt[k] = 0
        self.dma_rr = 0
        self.known = {e: {} for e in self.ENGS}
        self.ninstr = 0

    def _deps(self, reads, writes):
        deps = {}

        def add(t):
            if t is None:
                return
            k, v = t
            if deps.get(k, 0) < v:
                deps[k] = v
        for r in reads:
            add(r.lw)
        for w in writes:
            add(w.lw)
            for t in w.rd:
                add(t)
        return deps

    def _wait(self, e, deps):
        kn = self.known[e]
        for k, v in deps.items():
            if k == e and e == "pe":
                continue
            if kn.get(k, 0) >= v:
                continue
            self.eng[e].wait_ge(self.sems[k], v)
            kn[k] = v

    def _commit(self, tok, reads, writes):
        for r in reads:
            r.rd.append(tok)
            if len(r.rd) > 48:
                m = {}
                for k, v in r.rd:
                    if m.get(k, 0) < v:
                        m[k] = v
                r.rd = list(m.items())
        for w in writes:
            w.lw = tok
            w.rd = []

    def op(self, e, fn, reads=(), writes=()):
        deps = self._deps(reads, writes)
        self._wait(e, deps)
        ins = fn()
        self.cnt[e] += 1
        ins.then_inc(self.sems[e], 1)
        self._commit((e, self.cnt[e]), reads, writes)
        self.ninstr += 1
        return ins

    def dma(self, out, in_, reads=(), writes=(), q="sp", **kw):
        deps = self._deps(reads, writes)
        self._wait(q, deps)
        k = "d%d" % self.dma_rr
        self.dma_rr = (self.dma_rr + 1) % self.n_dma
        self._wait(q, {k: self.cnt[k]})
        ins = self.eng[q].dma_start(out=out, in_=in_, **kw)
        self.cnt[k] += 16
        ins.then_inc(self.sems[k], 16)
        self._commit((k, self.cnt[k]), reads, writes)
        self.ninstr += 1
        return ins

    def barrier(self):
        allv = {k: v for k, v in self.cnt.items() if v > 0}
        for e in self.ENGS:
            self._wait(e, allv)

    def finish(self, e="sp"):
        allv = {k: v for k, v in self.cnt.items() if v > 0}
        self._wait(e, allv)


class Arena:
    def __init__(self, t, nelem):
        self.t = t
        self.n = nelem
        self.off = 0

    def reset(self):
        self.off = 0

    def alloc(self, shape, dt):
        nel = 1
        for d in shape[1:]:
            nel *= d
        nb = nel * (2 if dt == BF16 else 4)
        nb = (nb + 31) // 32 * 32
        assert self.off + nb // 2 <= self.n, ("arena overflow", self.off, nb, self.n)
        ap = self.t[:, self.off:self.off + nb // 2]
        self.off += nb // 2
        if dt != BF16:
            ap = ap.bitcast(dt)
        ap = ap[:, 0:nel]
        if len(shape) == 3:
            ap = ap.rearrange("p (a b) -> p a b", a=shape[1])
        elif len(shape) == 4:
            ap = ap.rearrange("p (a b c) -> p a b c", a=shape[1], b=shape[2])
        return ap


class SubPool:
    def __init__(self, items):
        self.t = list(items)
        self.i = 0

    def get(self):
        r = self.t[self.i]
        self.i = (self.i + 1) % len(self.t)
        return r


class TPool:
    def __init__(self, nc, name, shape, dt, n, psum=False, arena=None):
        self.t = []
        for i in range(n):
            if psum:
                h = nc.alloc_psum_tensor("%s%d" % (name, i), shape, dt)
            elif arena is not None:
                h = arena.alloc(shape, dt)
            else:
                h = nc.alloc_sbuf_tensor("%s%d" % (name, i), shape, dt)
            self.t.append((h, Res("%s%d" % (name, i))))
        self.i = 0

    def get(self):
        r = self.t[self.i]
        self.i = (self.i + 1) % len(self.t)
        return r


def run_pipe(gens):
    active = []

    def step():
        nxt = []
        for a in active:
            try:
                next(a)
                nxt.append(a)
            except StopIteration:
                pass
        active[:] = nxt
    for g in gens:
        active.append(g)
        step()
    while active:
        step()


def t5_bucket_np(rel):
    nb = 16
    n = -rel
    ret = np.where(n < 0, nb, 0)
    n = np.abs(n)
    max_exact = nb // 2
    nf = np.maximum(n, 1).astype(np.float32)
    large = max_exact + (np.log(nf / max_exact) / math.log(128 / max_exact) * (nb - max_exact)).astype(np.int32)
    large = np.minimum(large, nb - 1)
    return ret + np.where(n < max_exact, n, large)


def host_consts():
    c = {}
    c["ident_f"] = np.eye(128, dtype=np.float32)
    jsw = np.zeros((128, 128), np.float32)
    for i in range(128):
        jsw[i, (i + 64) % 128] = 1.0
    c["jswap_f"] = jsw
    c["jrev_f"] = np.eye(128, dtype=np.float32)[::-1].copy()
    bd = np.zeros((128, 128), np.float32)
    bd[:64, :64] = 1.0
    bd[64:, 64:] = 1.0
    c["bdones_f"] = bd
    m = np.arange(384)
    rel = 127 - m
    bk = t5_bucket_np(rel)
    oh = np.zeros((32, 384), np.float32)
    oh[bk, m] = 1.0
    oh[15, :] -= 1.0
    oh[:, 383] = 0.0
    c["onehot"] = oh
    assert np.all(t5_bucket_np(-np.arange(129, 4096)) == 15)
    mk = np.ones((128, 256), np.float32)
    mk[64:, :64] = 0.0
    c["bandmask"] = mk
    ii = np.arange(128) // 16
    c["mask_intra"] = (ii[None, :] >= ii[:, None]).astype(np.float32)
    sg = np.zeros((128, 2), np.float32)
    sg[:64, 0] = -1.0
    sg[64:, 0] = 1.0
    sg[:64, 1] = 1.0
    sg[64:, 1] = -1.0
    c["sig"] = sg
    return c


NVEC = 8 + 8 + 1 + 1 + 1 + 8 + 8 + 8 + 8 * 31 + 44 * 3
V_GMIX, V_GFFN, V_QG, V_KG, V_SUB, V_CB, V_LNG, V_LNB, V_CW, V_FW = 0, 8, 16, 17, 18, 19, 27, 35, 43, 43 + 248


def host_layer_params(inp, l):
    v = np.zeros((128, NVEC), np.float32)
    v[:, V_GMIX:V_GMIX + 8] = inp["norm_mix"][l].reshape(8, 128).T
    v[:, V_GFFN:V_GFFN + 8] = inp["norm_ffn"][l].reshape(8, 128).T
    v[:, V_QG] = np.tile(inp["qk_gain_q"][l], 2)
    v[:, V_KG] = np.tile(inp["qk_gain_k"][l], 2)
    v[:, V_SUB] = inp["diff_subln"][l]
    v[:, V_CB:V_CB + 8] = inp["conv_dw_b"][l].reshape(8, 128).T
    v[:, V_LNG:V_LNG + 8] = inp["conv_ln_g"][l].reshape(8, 128).T
    v[:, V_LNB:V_LNB + 8] = inp["conv_ln_b"][l].reshape(8, 128).T
    v[:, V_CW:V_CW + 248] = inp["conv_dw_w"][l].reshape(31, 8, 128).transpose(2, 1, 0).reshape(128, 248)
    v[:, V_FW:V_FW + 132] = inp["ffn_dw_w"][l].reshape(3, 44, 128).transpose(2, 1, 0).reshape(128, 132)
    lamv = np.concatenate([inp["lambda_q1"][l], inp["lambda_k1"][l], inp["lambda_q2"][l], inp["lambda_k2"][l]])
    lamv = np.broadcast_to(lamv[None, :], (128, 256)).copy()
    s5 = np.zeros((128, 64 * 4 + 4 * 64 * 16), np.float32)
    lr = inp["s5_lambda_re"][l].T
    li = inp["s5_lambda_im"][l].T
    s5[:, 0:64] = np.concatenate([lr, lr], 0)
    s5[:, 64:128] = np.concatenate([li, li], 0)
    s5[:, 128:192] = np.broadcast_to(inp["s5_log_step"][l][None, :], (128, 64))
    s5[:, 192:256] = np.tile(inp["s5_d"][l].T, (8, 1))
    br = inp["s5_b_re"][l].transpose(1, 0, 2).reshape(64, 1024)
    bi = inp["s5_b_im"][l].transpose(1, 0, 2).reshape(64, 1024)
    cr = inp["s5_c_re"][l].transpose(2, 0, 1).reshape(64, 1024)
    ci = inp["s5_c_im"][l].transpose(2, 0, 1).reshape(64, 1024)
    s5[:, 256:1280] = np.concatenate([br, bi], 0)
    s5[:, 1280:2304] = np.concatenate([bi, br], 0)
    s5[:, 2304:3328] = np.concatenate([cr, ci], 0)
    s5[:, 3328:4352] = np.concatenate([ci, cr], 0)
    return v, lamv, s5


def build(dbg=None, nlayers=DEPTH):
    nc = bass.Bass("TRN2", target_bir_lowering=False)
    P = Prog(nc)
    dbg = dbg or set()

    def dram_in(name, shape, dt=F32):
        return nc.dram_tensor(name, list(shape), dt, kind="ExternalInput")

    def scratch(name, shape, dt):
        kind = "ExternalOutput" if name in dbg else "Internal"
        return nc.dram_tensor(name, list(shape), dt, kind=kind)

    xT_in = dram_in("xT", [D, S])
    W = {}
    for nm, shp in [("w_in", [DEPTH, D, IN_W]), ("w_attn_out", [DEPTH, D, D]), ("s5_glu_w1", [DEPTH, D, D]),
                    ("s5_glu_w2", [DEPTH, D, D]), ("conv_w_out", [DEPTH, D, D]), ("w_out", [DEPTH, D, D]),
                    ("ffn_w_up", [DEPTH, D, 2 * FFN_H]), ("ffn_w_down", [DEPTH, FFN_H, D])]:
        W[nm] = dram_in(nm, shp)
    vecs_in = dram_in("vecs", [DEPTH, 128, NVEC])
    lamv_in = dram_in("lamv", [DEPTH, 128, 256])
    s5p_in = dram_in("s5p", [DEPTH, 128, 4352])
    relb_in = dram_in("rel_bias", [32, 8])
    W_names = list(W.keys())
    cst = {}
    for nm, shp in [("ident_f", [128, 128]), ("jswap_f", [128, 128]), ("jrev_f", [128, 128]), ("bdones_f", [128, 128]),
                    ("onehot", [32, 384]), ("bandmask", [128, 256]), ("mask_intra", [128, 128]), ("sig", [128, 2])]:
        cst[nm] = dram_in("c_" + nm, shp)
    yT_out = nc.dram_tensor("yT", [D, S], F32, kind="ExternalOutput")

    qT_s = scratch("qT_s", [D, S], BF16)
    kT_s = scratch("kT_s", [D, S], BF16)
    v_s = scratch("v_s", [S, D], BF16)
    u_s = scratch("u_s", [512, 8192], BF16)
    cT_s = scratch("cT_s", [D, S], BF16)
    gT_s = scratch("gT_s", [3 * D, S], BF16)
    br_s = scratch("br_s", [3 * D, S], BF16)
    x1T_s = scratch("x1T_s", [D, S], F32)
    xmidT_s = scratch("xmidT_s", [D, S], F32)
    act_s = scratch("act_s", [FFN_H, S], BF16)
    wr_d = scratch("wr_d", [8, 384], F32)
    R_qT, R_kT, R_v, R_u, R_cT, R_gT, R_br, R_x1, R_xmid, R_act, R_wr = [Res(n) for n in
        ("qT", "kT", "v", "u", "cT", "gT", "br", "x1", "xmid", "act", "wr")]

    sb = nc.alloc_sbuf_tensor
    R_c = Res("consts")
    ident_f = sb("ident_f", [128, 128], F32)
    jswap_f = sb("jswap_f", [128, 128], F32)
    jrev_f = sb("jrev_f", [128, 128], F32)
    mask_intra = sb("mask_intra", [128, 128], F32)
    bandmask = sb("bandmask", [128, 256], F32)
    sig = sb("sig", [128, 2], F32)
    ident_b = sb("ident_b", [128, 128], BF16)
    ones_b = sb("ones_b", [128, 128], BF16)
    bdones_b = sb("bdones_b", [128, 128], BF16)
    epsc = sb("epsc", [128, 1], F32)
    onehot = sb("onehot", [32, 384], F32)
    relb = sb("relb", [32, 8], F32)
    expB = sb("expB", [128, 8, 256], BF16)
    for t_, nm in [(ident_f, "ident_f"), (jswap_f, "jswap_f"), (jrev_f, "jrev_f"), (mask_intra, "mask_intra"),
                   (bandmask, "bandmask"), (sig, "sig"), (onehot, "onehot")]:
        P.dma(t_[:], cst[nm].ap(), writes=[R_c])
    P.dma(relb[:], relb_in.ap(), writes=[R_c])
    P.dma(bdones_b[:], cst["bdones_f"].ap(), writes=[R_c], q="pool")
    P.op("dve", lambda: nc.vector.tensor_copy(out=ident_b[:], in_=ident_f[:]), reads=[R_c], writes=[R_c])
    P.op("dve", lambda: nc.vector.memset(ones_b[:], 1.0), writes=[R_c])
    P.op("dve", lambda: nc.vector.memset(epsc[:], EPS), writes=[R_c])

    BIGA = sb("BIGA", [128, 8, S], BF16)
    R_A = [Res("A%d" % i) for i in range(8)]
    ARENA_N = 55296
    AR = Arena(sb("ARENA", [128, ARENA_N], BF16), ARENA_N)
    vec = sb("vec", [128, NVEC], F32)
    R_vec = Res("vec")
    lam4 = sb("lam4", [128, 256], F32)
    neglam = sb("neglam", [128, 1], F32)
    subsc = sb("subsc", [128, 1], F32)
    R_lam = Res("lam")

    PS = TPool(nc, "ps", [128, 512], F32, 8, psum=True)
    f32p = TPool(nc, "f32t", [128, 512], F32, 6)
    b16p = TPool(nc, "b16t", [128, 512], BF16, 8)
    wpool = TPool(nc, "wt", [128, 8, 512], BF16, 2, arena=AR)
    xin = TPool(nc, "xin", [128, 8, 512], F32, 2, arena=AR)
    sqp = TPool(nc, "sq", [128, 8, 512], BF16, 1, arena=AR)
    ustage = TPool(nc, "ust", [128, 32, 8, 16], BF16, 2, arena=AR)

    ps, rps = PS.get()
    P.op("pe", lambda: nc.tensor.matmul(ps[0:8, 0:384], lhsT=relb[:], rhs=onehot[:], start=True, stop=True),
         reads=[R_c], writes=[rps])
    wr_sb, r_wr_sb = f32p.get()
    P.op("act", lambda: nc.scalar.copy(out=wr_sb[0:8, 0:384], in_=ps[0:8, 0:384]), reads=[rps], writes=[r_wr_sb])
    P.dma(wr_d.ap(), wr_sb[0:8, 0:384], reads=[r_wr_sb], writes=[R_wr])
    for h in range(8):
        xt_, rxt = f32p.get()
        P.dma(xt_[:, 0:256], bass.AP(tensor=wr_d, offset=384 * h, ap=[[1, 128], [1, 256]]), reads=[R_wr], writes=[rxt])
        ps, rps = PS.get()
        P.op("pe", lambda: nc.tensor.matmul(ps[:, 0:256], lhsT=jrev_f[:], rhs=xt_[:, 0:256], start=True, stop=True),
             reads=[R_c, rxt], writes=[rps])
        et_, ret = f32p.get()
        P.op("act", lambda: nc.scalar.activation(out=et_[:, 0:256], in_=ps[:, 0:256], func=AF.Exp), reads=[rps], writes=[ret])
        P.op("dve", lambda: nc.vector.tensor_tensor(out=expB[:, h, :], in0=et_[:, 0:256], in1=bandmask[:], op=ALU.mult),
             reads=[ret, R_c], writes=[R_c])

    def rsqrt_from(ps_ap, scale, n=512, width=None):
        t, rt = f32p.get()
        return t, rt

    def norm_stage(src_dram, R_src, gcol, keep=None):
        for tb in range(NTB):
            xb, rxb = xin.get()
            P.dma(xb[:], src_dram.ap()[:, tb * TB:(tb + 1) * TB].rearrange("(kc p) t -> p kc t", p=128),
                  reads=[R_src], writes=[rxb])
            norm_block(xb, rxb, gcol, tb, sqp)

    def norm_block(xb, rxb, gcol, tb, sqpool):
        sq, rsq = sqpool.get()
        P.op("act", lambda: nc.scalar.activation(out=sq[:], in_=xb[:], func=AF.Square), reads=[rxb], writes=[rsq])
        ps, rps = PS.get()
        for kc in range(8):
            P.op("pe", lambda: nc.tensor.matmul(ps[:], lhsT=ones_b[:], rhs=sq[:, kc, :], start=(kc == 0), stop=(kc == 7)),
                 reads=[rsq, R_c], writes=[rps])
        rs, rrs = f32p.get()
        P.op("act", lambda: nc.scalar.activation(out=rs[:], in_=ps[:], func=AF.Sqrt, scale=1.0 / D, bias=epsc[:, 0:1]),
             reads=[rps, R_c], writes=[rrs])
        P.op("dve", lambda: nc.vector.reciprocal(out=rs[:], in_=rs[:]), reads=[rrs], writes=[rrs])
        for kc in range(8):
            P.op("dve", lambda: nc.vector.scalar_tensor_tensor(
                out=BIGA[:, kc, tb * TB:(tb + 1) * TB], in0=xb[:, kc, :], scalar=vec[:, gcol + kc:gcol + kc + 1],
                in1=rs[:], op0=ALU.mult, op1=ALU.mult), reads=[rxb, rrs, R_vec], writes=[R_A[kc]])

    def load_w(wdram2d, offs, KC=8):
        wt, rwt = wpool.get()
        pos = 0
        for off, n in offs:
            P.dma(wt[:, 0:KC, pos:pos + n], wdram2d[:, off:off + n].rearrange("(kc p) n -> p kc n", p=128),
                  writes=[rwt], q="pool")
            pos += n
        return wt, rwt

    def mm_fm(wt, rwt, ci, tb, KC=8, src=None, rsrc=None):
        ps, rps = PS.get()
        for kc in range(KC):
            P.op("pe", lambda: nc.tensor.matmul(ps[:], lhsT=wt[:, kc, ci * 128:(ci + 1) * 128],
                                                rhs=BIGA[:, kc, tb * TB:(tb + 1) * TB], start=(kc == 0), stop=(kc == KC - 1)),
                 reads=[rwt, R_A[kc]], writes=[rps])
        return ps, rps

    def qk_tile(wt, rwt, ci, gcol, dst, R_dst, row0, tb):
        ps, rps = mm_fm(wt, rwt, ci, tb)
        sq, rsq = b16p.get()
        P.op("act", lambda: nc.scalar.activation(out=sq[:], in_=ps[:], func=AF.Square), reads=[rps], writes=[rsq])
        yield
        ps2, rps2 = PS.get()
        P.op("pe", lambda: nc.tensor.matmul(ps2[:], lhsT=bdones_b[:], rhs=sq[:], start=True, stop=True),
             reads=[rsq, R_c], writes=[rps2])
        rs, rrs = f32p.get()
        P.op("act", lambda: nc.scalar.activation(out=rs[:], in_=ps2[:], func=AF.Sqrt, scale=1.0 / 64, bias=epsc[:, 0:1]),
             reads=[rps2, R_c], writes=[rrs])
        P.op("dve", lambda: nc.vector.reciprocal(out=rs[:], in_=rs[:]), reads=[rrs], writes=[rrs])
        o, ro = b16p.get()
        P.op("dve", lambda: nc.vector.scalar_tensor_tensor(out=o[:], in0=ps[:], scalar=vec[:, gcol:gcol + 1], in1=rs[:],
                                                           op0=ALU.mult, op1=ALU.mult), reads=[rps, rrs, R_vec], writes=[ro])
        P.dma(dst.ap()[row0:row0 + 128, tb * TB:(tb + 1) * TB], o[:], reads=[ro], writes=[R_dst])


    y_s = scratch("y_s", [S, D], BF16)
    R_ys = Res("ys")

    def s5_stage(l):
        P.barrier()
        AR.reset()
        R_sp = Res("sp")
        sp = AR.alloc([128, 4352], F32)
        P.dma(sp[:], s5p_in.ap()[l], writes=[R_sp])
        lr, li, lstep, drep = sp[:, 0:64], sp[:, 64:128], sp[:, 128:192], sp[:, 192:256]
        T1, T2, CTa, CTb = sp[:, 256:1280], sp[:, 1280:2304], sp[:, 2304:3328], sp[:, 3328:4352]
        SL = AR.alloc([128, 20, 64], F32)
        KI = AR.alloc([128, 64], I32)
        PWr = AR.alloc([128, 16, 64], F32)
        PWi = AR.alloc([128, 16, 64], F32)
        AKr = AR.alloc([128, 9, 64], F32)
        AKi = AR.alloc([128, 9, 64], F32)
        AKs = AR.alloc([128, 9, 64], F32)
        PRr = AR.alloc([128, 15, 64], F32)
        PRi = AR.alloc([128, 15, 64], F32)
        QRr = AR.alloc([128, 9, 64], F32)
        QRi = AR.alloc([128, 9, 64], F32)
        BT1 = AR.alloc([128, 1024], F32)
        BT2 = AR.alloc([128, 1024], F32)
        tA = AR.alloc([128, 1024], F32)
        tB = AR.alloc([128, 1024], F32)
        rw = dict(reads=[R_sp, R_c], writes=[R_sp])

        def tt(o, a, b, op):
            P.op("dve", lambda: nc.vector.tensor_tensor(out=o, in0=a, in1=b, op=op), **rw)

        def ts(o, a, s1, op0, s2=None, op1=None):
            if op1 is None:
                P.op("dve", lambda: nc.vector.tensor_scalar(out=o, in0=a, scalar1=s1, scalar2=None, op0=op0), **rw)
            else:
                P.op("dve", lambda: nc.vector.tensor_scalar(out=o, in0=a, scalar1=s1, scalar2=s2, op0=op0, op1=op1), **rw)

        def act(o, a, func, scale=1.0):
            P.op("act", lambda: nc.scalar.activation(out=o, in_=a, func=func, scale=scale), **rw)

        def cp(o, a):
            P.op("dve", lambda: nc.vector.tensor_copy(out=o, in_=a), **rw)

        sl = lambda i: SL[:, i, :]
        dl, mag, ang, tu, kf, r1, rr, mm_, sinv, cosv, ar, ai, den, nr, fr, fi, t1, t2, sfi, nsfi = [sl(i) for i in range(20)]
        act(dl, lstep, AF.Exp)
        tt(t1, lr, dl, ALU.mult)
        act(mag, t1, AF.Exp)
        tt(ang, li, dl, ALU.mult)
        ts(tu, ang, 1.0 / (2 * math.pi), ALU.mult)
        cp(KI[:], tu)
        cp(kf, KI[:])
        tt(r1, tu, kf, ALU.subtract)

        def wrap_sin(dst, src, shift):
            ts(rr, src, shift, ALU.add)
            ts(mm_, rr, 0.5, ALU.is_gt)
            tt(rr, rr, mm_, ALU.subtract)
            ts(mm_, rr, 0.5, ALU.is_gt)
            tt(rr, rr, mm_, ALU.subtract)
            ts(mm_, rr, -0.5, ALU.is_lt)
            tt(rr, rr, mm_, ALU.add)
            ts(mm_, rr, -0.5, ALU.is_lt)
            tt(rr, rr, mm_, ALU.add)
            act(dst, rr, AF.Sin, scale=2 * math.pi)
        wrap_sin(sinv, r1, 0.0)
        wrap_sin(cosv, r1, 0.25)
        tt(ar, mag, cosv, ALU.mult)
        tt(ai, mag, sinv, ALU.mult)
        tt(t1, lr, lr, ALU.mult)
        tt(t2, li, li, ALU.mult)
        tt(den, t1, t2, ALU.add)
        P.op("dve", lambda: nc.vector.reciprocal(out=den, in_=den), **rw)
        ts(nr, ar, -1.0, ALU.add)
        tt(t1, nr, lr, ALU.mult)
        tt(t2, ai, li, ALU.mult)
        tt(t1, t1, t2, ALU.add)
        tt(fr, t1, den, ALU.mult)
        tt(t1, ai, lr, ALU.mult)
        tt(t2, nr, li, ALU.mult)
        tt(t1, t1, t2, ALU.subtract)
        tt(fi, t1, den, ALU.mult)
        ts(sfi, fi, sig[:, 0:1], ALU.mult)
        ts(nsfi, fi, sig[:, 1:2], ALU.mult)
        g3 = lambda a: a.rearrange("p (g h) -> p g h", h=16)
        bc3 = lambda a: a.unsqueeze(2).broadcast_to([128, 64, 16])
        tt(g3(tA[:]), bc3(fr), g3(T1), ALU.mult)
        tt(g3(tB[:]), bc3(sfi), g3(T2), ALU.mult)
        tt(BT1[:], tA[:], tB[:], ALU.add)
        tt(g3(tA[:]), bc3(fr), g3(T2), ALU.mult)
        tt(g3(tB[:]), bc3(nsfi), g3(T1), ALU.mult)
        tt(BT2[:], tA[:], tB[:], ALU.add)

        def cmul(orr, oi, xr, xi, yr, yi):
            tt(t1, xr, yr, ALU.mult)
            tt(t2, xi, yi, ALU.mult)
            tt(orr, t1, t2, ALU.subtract)
            tt(t1, xr, yi, ALU.mult)
            tt(t2, xi, yr, ALU.mult)
            tt(oi, t1, t2, ALU.add)
        P.op("dve", lambda: nc.vector.memset(PWr[:, 7, :], 1.0), **rw)
        P.op("dve", lambda: nc.vector.memset(PWi[:, 7, :], 0.0), **rw)
        cp(PWr[:, 8, :], ar)
        cp(PWi[:, 8, :], ai)
        for n in range(2, 9):
            cmul(PWr[:, 7 + n, :], PWi[:, 7 + n, :], PWr[:, 6 + n, :], PWi[:, 6 + n, :], PWr[:, 8, :], PWi[:, 8, :])
        tt(t1, ar, ar, ALU.mult)
        tt(t2, ai, ai, ALU.mult)
        tt(den, t1, t2, ALU.add)
        P.op("dve", lambda: nc.vector.reciprocal(out=den, in_=den), **rw)
        tt(PWr[:, 6, :], ar, den, ALU.mult)
        tt(t1, ai, den, ALU.mult)
        ts(PWi[:, 6, :], t1, -1.0, ALU.mult)
        for n in range(2, 8):
            cmul(PWr[:, 7 - n, :], PWi[:, 7 - n, :], PWr[:, 8 - n, :], PWi[:, 8 - n, :], PWr[:, 6, :], PWi[:, 6, :])
        cp(AKr[:, 0, :], PWr[:, 15, :])
        cp(AKi[:, 0, :], PWi[:, 15, :])
        for k in range(1, 9):
            cmul(AKr[:, k, :], AKi[:, k, :], AKr[:, k - 1, :], AKi[:, k - 1, :], AKr[:, k - 1, :], AKi[:, k - 1, :])
        ts(AKs[:], AKi[:], sig[:, 1:2], ALU.mult)
        for s_ in range(15):
            cp(PRr[:, s_, :], PWr[:, 14 - s_, :])
            ts(PRi[:, s_, :], PWi[:, 14 - s_, :], sig[:, 0:1], ALU.mult)
        ts(QRr[:], PWr[:, 7:16, :], sig[:, 1:2], ALU.mult)
        ts(QRi[:], PWi[:, 7:16, :], -1.0, ALU.mult)

        off_ut = AR.off
        utok = AR.alloc([128, 8192], BF16)
        R_ut = Res("utok")
        PSr = SubPool(PS.t[4:8])
        for cb in range(4):
            P.dma(utok[:], u_s.ap()[cb * 128:(cb + 1) * 128, :], reads=[R_u], writes=[R_ut])
            for g8 in range(8):
                ps, rps = PSr.get()
                psb = ps.bitcast(BF16)
                for gg in range(8):
                    g = g8 * 8 + gg
                    P.op("pe", lambda: nc.tensor.transpose(psb[:, gg * 128:(gg + 1) * 128], utok[:, g * 128:(g + 1) * 128], ident_b[:]),
                         reads=[R_ut, R_c], writes=[rps])
                dst = BIGA[:, g8, :].rearrange("p (g c) -> p g c", c=512)[:, :, cb * 128:(cb + 1) * 128]
                src = psb[:, :].rearrange("p (g c) -> p g c", c=128)
                if g8 % 2 == 0:
                    P.op("act", lambda: nc.scalar.copy(out=dst, in_=src), reads=[rps], writes=[R_A[g8]])
                else:
                    P.op("dve", lambda: nc.vector.tensor_copy(out=dst, in_=src), reads=[rps], writes=[R_A[g8]])

        P.barrier()
        AR.off = off_ut
        pfp = TPool(nc, "pf", [128, 4, 240], BF16, 2, arena=AR)
        qfp = TPool(nc, "qf", [128, 4, 144], BF16, 2, arena=AR)
        tP1 = AR.alloc([128, 4, 15, 16], F32)
        tP2 = AR.alloc([128, 4, 15, 16], F32)
        rkp = TPool(nc, "rk", [128, 9, 128], BF16, 4, arena=AR)
        minp = TPool(nc, "min", [128, 128], BF16, 4, arena=AR)
        PSm = SubPool(PS.t[4:8])
        mintp = TPool(nc, "mint", [128, 128], BF16, 4, arena=AR)
        xzp = TPool(nc, "xz", [128, 514], BF16, 5, arena=AR)
        ystp = TPool(nc, "yst", [128, 8, 64], BF16, 2, arena=AR)
        for xz_, rxz_ in xzp.t:
            P.op("pool", lambda: nc.gpsimd.memset(xz_[:, 0:1], 0.0), writes=[rxz_])
        psY = PS.t[0:4]
        BT1g, BT2g, CTag, CTbg = g3(BT1[:]), g3(BT2[:]), g3(CTa), g3(CTb)
        for bi in range(16):
            g0 = bi * 4
            pf, rpf = pfp.get()
            qf, rqf = qfp.get()
            pf4 = pf[:].rearrange("p g (b h) -> p g b h", h=16)
            qf4 = qf[:].rearrange("p g (b h) -> p g b h", h=16)
            e_ = lambda tab, nb: tab[:, :, g0:g0 + 4].rearrange("p b g -> p g b").unsqueeze(3).broadcast_to([128, 4, nb, 16])
            b_ = lambda tab, nb: tab[:, g0:g0 + 4, :].unsqueeze(2).broadcast_to([128, 4, nb, 16])
            rwp = dict(reads=[R_sp], writes=[R_sp])
            P.op("dve", lambda: nc.vector.tensor_tensor(out=tP1[:], in0=e_(PRr, 15), in1=b_(BT1g, 15), op=ALU.mult), **rwp)
            P.op("dve", lambda: nc.vector.tensor_tensor(out=tP2[:], in0=e_(PRi, 15), in1=b_(BT2g, 15), op=ALU.mult), **rwp)
            P.op("dve", lambda: nc.vector.tensor_tensor(out=pf4, in0=tP1[:], in1=tP2[:], op=ALU.add), reads=[R_sp], writes=[R_sp, rpf])
            P.op("dve", lambda: nc.vector.tensor_tensor(out=tP1[:, :, 0:9, :], in0=e_(QRr, 9), in1=b_(CTag, 9), op=ALU.mult), **rwp)
            P.op("dve", lambda: nc.vector.tensor_tensor(out=tP2[:, :, 0:9, :], in0=e_(QRi, 9), in1=b_(CTbg, 9), op=ALU.mult), **rwp)
            P.op("dve", lambda: nc.vector.tensor_tensor(out=qf4, in0=tP1[:, :, 0:9, :], in1=tP2[:, :, 0:9, :], op=ALU.add),
                 reads=[R_sp], writes=[R_sp, rqf])
            def grp_gen(gl):
                g = g0 + gl
                U_g = BIGA[:, g // 8, (g % 8) * 512:(g % 8 + 1) * 512]
                R_U = R_A[g // 8]
                rk, rrk = rkp.get()
                for k in range(9):
                    tf, rtf = f32p.get()
                    P.op("dve", lambda: nc.vector.tensor_scalar(out=tf[:, 0:128], in0=ident_f[:], scalar1=AKr[:, k, g:g + 1], scalar2=None,
                                                                op0=ALU.mult), reads=[R_sp, R_c], writes=[rtf])
                    P.op("dve", lambda: nc.vector.scalar_tensor_tensor(out=rk[:, k, :], in0=jswap_f[:], scalar=AKs[:, k, g:g + 1],
                                                                       in1=tf[:, 0:128], op0=ALU.mult, op1=ALU.add),
                         reads=[R_sp, R_c, rtf], writes=[rrk])
                ps, rps = PSr.get()
                psb = ps.bitcast(BF16)
                P.op("pe", lambda: nc.tensor.transpose(psb[:, 0:128], pf[:, gl, 0:128], ident_b[:]), reads=[rpf, R_c], writes=[rps])
                mi, rmi = minp.get()
                P.op("act", lambda: nc.scalar.copy(out=mi[:], in_=psb[:, 0:128]), reads=[rps], writes=[rmi])
                ps, rps = PSm.get()
                P.op("pe", lambda: nc.tensor.matmul(ps[:, 0:128], lhsT=pf[:, gl, 112:240], rhs=qf[:, gl, 0:128], start=True, stop=True),
                     reads=[rpf, rqf], writes=[rps])
                tf, rtf = f32p.get()
                P.op("dve", lambda: nc.vector.tensor_tensor(out=tf[:, 0:128], in0=ps[:, 0:128], in1=mask_intra[:], op=ALU.mult),
                     reads=[rps, R_c], writes=[rtf])
                mt, rmt = mintp.get()
                P.op("dve", lambda: nc.vector.scalar_tensor_tensor(out=mt[:], in0=ident_f[:], scalar=drep[:, g:g + 1], in1=tf[:, 0:128],
                                                                   op0=ALU.mult, op1=ALU.add), reads=[R_sp, R_c, rtf], writes=[rmt])
                yield
                xz, rxz = xzp.get()
                ps, rps = PSr.get()
                P.op("pe", lambda: nc.tensor.matmul(ps[:], lhsT=mi[:], rhs=U_g, start=True, stop=True), reads=[rmi, R_U], writes=[rps])
                P.op("act", lambda: nc.scalar.copy(out=xz[:, 1:513], in_=ps[:]), reads=[rps], writes=[rxz])
                for k in range(9):
                    yield
                    sh = 1 << k
                    ps, rps = PSr.get()
                    P.op("pe", lambda: nc.tensor.matmul(ps[:, 0:512 - sh], lhsT=rk[:, k, :], rhs=xz[:, 1:513 - sh], start=True, stop=True),
                         reads=[rrk, rxz], writes=[rps])
                    P.op("dve", lambda: nc.vector.tensor_tensor(out=xz[:, 1 + sh:513], in0=ps[:, 0:512 - sh], in1=xz[:, 1 + sh:513],
                                                                op=ALU.add), reads=[rps], writes=[rxz])
                yield
                for cb in range(4):
                    P.op("pe", lambda: nc.tensor.matmul(psY[cb][0][:, gl * 128:(gl + 1) * 128], lhsT=U_g[:, cb * 128:(cb + 1) * 128], rhs=mt[:],
                                                        start=True, stop=False), reads=[R_U, rmt], writes=[psY[cb][1]])
                    P.op("pe", lambda: nc.tensor.matmul(psY[cb][0][:, gl * 128:(gl + 1) * 128], lhsT=xz[:, cb * 128:(cb + 1) * 128],
                                                        rhs=qf[:, gl, 16:144], start=False, stop=True), reads=[rxz, rqf], writes=[psY[cb][1]])
            run_pipe(grp_gen(gl) for gl in range(4))
            for cb in range(4):
                yst, ryst = ystp.get()
                if "s5raw" in dbg:
                    P.op("act", lambda: nc.scalar.copy(out=yst[:].rearrange("p j (g h) -> p g j h", h=16),
                                                       in_=psY[cb][0][:].rearrange("p (g j h) -> p g j h", g=4, j=8)),
                         reads=[psY[cb][1]], writes=[ryst])
                else:
                    P.op("act", lambda: nc.scalar.activation(out=yst[:].rearrange("p j (g h) -> p g j h", h=16),
                                                             in_=psY[cb][0][:].rearrange("p (g j h) -> p g j h", g=4, j=8),
                                                             func=AF.Gelu_apprx_tanh), reads=[psY[cb][1]], writes=[ryst])
                P.dma(y_s.ap()[1024 * cb:1024 * (cb + 1), 64 * bi:64 * bi + 64].rearrange("(c j) n -> c j n", j=8), yst[:],
                      reads=[ryst], writes=[R_ys])

        P.barrier()
        AR.reset()
        ytp = TPool(nc, "ytk", [128, 1024], BF16, 2, arena=AR)
        for tt_ in range(32):
            yt, ryt = ytp.get()
            P.dma(yt[:], y_s.ap()[tt_ * 128:(tt_ + 1) * 128, :], reads=[R_ys], writes=[ryt])
            ps, rps = PSr.get()
            psb = ps.bitcast(BF16)
            for kc in range(8):
                P.op("pe", lambda: nc.tensor.transpose(psb[:, kc * 128:(kc + 1) * 128], yt[:, kc * 128:(kc + 1) * 128], ident_b[:]),
                     reads=[ryt, R_c], writes=[rps])
            dst = BIGA[:, :, tt_ * 128:(tt_ + 1) * 128]
            src = psb[:, :].rearrange("p (k c) -> p k c", c=128)
            if tt_ % 2 == 0:
                P.op("act", lambda: nc.scalar.copy(out=dst, in_=src), reads=[rps], writes=R_A)
            else:
                P.op("dve", lambda: nc.vector.tensor_copy(out=dst, in_=src), reads=[rps], writes=R_A)
        wp4 = TPool(nc, "wt4", [128, 8, 512], BF16, 4, arena=AR)
        for grp in range(2):
            w1t, rw1 = wp4.get()
            P.dma(w1t[:], W["s5_glu_w1"].ap()[l][:, grp * 512:(grp + 1) * 512].rearrange("(kc p) n -> p kc n", p=128), writes=[rw1], q="pool")
            w2t, rw2 = wp4.get()
            P.dma(w2t[:], W["s5_glu_w2"].ap()[l][:, grp * 512:(grp + 1) * 512].rearrange("(kc p) n -> p kc n", p=128), writes=[rw2], q="pool")
            for ci in range(4):
                for tb in range(NTB):
                    psa, rpsa = mm_fm(w1t, rw1, ci, tb)
                    psb_, rpsb = mm_fm(w2t, rw2, ci, tb)
                    sg_, rsg = f32p.get()
                    P.op("act", lambda: nc.scalar.activation(out=sg_[:], in_=psb_[:], func=AF.Sigmoid), reads=[rpsb], writes=[rsg])
                    o, ro = b16p.get()
                    P.op("dve", lambda: nc.vector.tensor_tensor(out=o[:], in0=psa[:], in1=sg_[:], op=ALU.mult), reads=[rpsa, rsg], writes=[ro])
                    r0_ = D + (grp * 4 + ci) * 128
                    P.dma(br_s.ap()[r0_:r0_ + 128, tb * TB:(tb + 1) * TB], o[:], reads=[ro], writes=[R_br])

    for l in range(nlayers):
        x_src, R_xsrc = (xT_in, Res("xin")) if l == 0 else (xmidT_s, R_xmid)
        x_dst, R_xdst = (yT_out, Res("yout")) if l == nlayers - 1 else (xmidT_s, R_xmid)
        P.barrier()
        P.dma(vec[:], vecs_in.ap()[l], writes=[R_vec])
        P.dma(lam4[:], lamv_in.ap()[l], writes=[R_lam])
        lam_init = 0.8 - 0.6 * math.exp(-0.3 * l)
        lt, rlt = f32p.get()
        P.op("dve", lambda: nc.vector.tensor_tensor(out=lt[:, 0:64], in0=lam4[:, 0:64], in1=lam4[:, 64:128], op=ALU.mult),
             reads=[R_lam], writes=[rlt])
        P.op("dve", lambda: nc.vector.tensor_tensor(out=lt[:, 64:128], in0=lam4[:, 128:192], in1=lam4[:, 192:256], op=ALU.mult),
             reads=[R_lam], writes=[rlt])
        P.op("dve", lambda: nc.vector.reduce_sum(out=lt[:, 128:130], in_=lt[:, 0:128].rearrange("p (a b) -> p a b", a=2),
                                                 axis=AX.X), reads=[rlt], writes=[rlt])
        P.op("act", lambda: nc.scalar.activation(out=lt[:, 130:132], in_=lt[:, 128:130], func=AF.Exp), reads=[rlt], writes=[rlt])
        P.op("dve", lambda: nc.vector.scalar_tensor_tensor(out=neglam[:], in0=lt[:, 131:132], scalar=-lam_init, in1=lt[:, 130:131],
                                                           op0=ALU.add, op1=ALU.subtract), reads=[rlt], writes=[R_lam])
        P.op("dve", lambda: nc.vector.tensor_scalar(out=subsc[:], in0=vec[:, V_SUB:V_SUB + 1], scalar1=(1.0 - lam_init), scalar2=None,
                                                    op0=ALU.mult), reads=[R_vec], writes=[R_lam])

        norm_stage(x_src, R_xsrc, V_GMIX)

        w_in2 = W["w_in"].ap()[l]
        for seg, (off0, gcol, dst, R_dst) in enumerate([(OFF_Q, V_QG, qT_s, R_qT), (OFF_K, V_KG, kT_s, R_kT)]):
            for grp in range(2):
                wt, rwt = load_w(w_in2, [(off0 + grp * 512, 512)])
                run_pipe(qk_tile(wt, rwt, ci, gcol, dst, R_dst, (grp * 4 + ci) * 128, tb) for ci in range(4) for tb in range(NTB))
        for grp in range(2):
            wt, rwt = load_w(w_in2, [(OFF_V + grp * 512, 512)])
            for tt in range(32):
                ps, rps = PS.get()
                for kc in range(8):
                    P.op("pe", lambda: nc.tensor.matmul(ps[:], lhsT=BIGA[:, kc, tt * 128:(tt + 1) * 128], rhs=wt[:, kc, :],
                                                        start=(kc == 0), stop=(kc == 7)), reads=[rwt, R_A[kc]], writes=[rps])
                o, ro = b16p.get()
                if tt % 2 == 0:
                    P.op("act", lambda: nc.scalar.copy(out=o[:], in_=ps[:]), reads=[rps], writes=[ro])
                else:
                    P.op("dve", lambda: nc.vector.tensor_copy(out=o[:], in_=ps[:]), reads=[rps], writes=[ro])
                P.dma(v_s.ap()[tt * 128:(tt + 1) * 128, grp * 512:(grp + 1) * 512], o[:], reads=[ro], writes=[R_v])
        for grp in range(2):
            wt, rwt = load_w(w_in2, [(OFF_U + grp * 512, 512)])
            for cb in range(4):
                us, rus = ustage.get()
                for j in range(8):
                    ps, rps = PS.get()
                    for kc in range(8):
                        P.op("pe", lambda: nc.tensor.matmul(ps[:], lhsT=BIGA[:, kc, 1024 * cb + j:1024 * (cb + 1):8], rhs=wt[:, kc, :],
                                                            start=(kc == 0), stop=(kc == 7)), reads=[rwt, R_A[kc]], writes=[rps])
                    src_v = ps[:].rearrange("p (g h) -> p g h", h=16)
                    if j % 2 == 0:
                        P.op("act", lambda: nc.scalar.copy(out=us[:, :, j, :], in_=src_v), reads=[rps], writes=[rus])
                    else:
                        P.op("dve", lambda: nc.vector.tensor_copy(out=us[:, :, j, :], in_=src_v), reads=[rps], writes=[rus])
                P.dma(u_s.ap()[cb * 128:(cb + 1) * 128, grp * 4096:(grp + 1) * 4096], us[:].rearrange("p g j h -> p (g j h)"),
                      reads=[rus], writes=[R_u])
        for grp in range(4):
            wt, rwt = load_w(w_in2, [(OFF_C + grp * 256, 256), (OFF_C + 1024 + grp * 256, 256)])
            for ci in range(2):
                for tb in range(NTB):
                    psa, rpsa = mm_fm(wt, rwt, ci, tb)
                    psb, rpsb = mm_fm(wt, rwt, ci + 2, tb)
                    sg_, rsg = f32p.get()
                    P.op("act", lambda: nc.scalar.activation(out=sg_[:], in_=psb[:], func=AF.Sigmoid), reads=[rpsb], writes=[rsg])
                    o, ro = b16p.get()
                    P.op("dve", lambda: nc.vector.tensor_tensor(out=o[:], in0=psa[:], in1=sg_[:], op=ALU.mult),
                         reads=[rpsa, rsg], writes=[ro])
                    r0 = (grp * 2 + ci) * 128
                    P.dma(cT_s.ap()[r0:r0 + 128, tb * TB:(tb + 1) * TB], o[:], reads=[ro], writes=[R_cT])
        for grp in range(6):
            wt, rwt = load_w(w_in2, [(OFF_G + grp * 512, 512)])
            for ci in range(4):
                for tb in range(NTB):
                    ps, rps = mm_fm(wt, rwt, ci, tb)
                    o, ro = b16p.get()
                    P.op("act", lambda: nc.scalar.activation(out=o[:], in_=ps[:], func=AF.Sigmoid), reads=[rps], writes=[ro])
                    r0 = (grp * 4 + ci) * 128
                    P.dma(gT_s.ap()[r0:r0 + 128, tb * TB:(tb + 1) * TB], o[:], reads=[ro], writes=[R_gT])
        if "stop_s2" in dbg:
            break

        P.barrier()
        AR.reset()
        kpool = TPool(nc, "kT", [128, S], BF16, 2, arena=AR)
        qpool = TPool(nc, "qT", [128, S], BF16, 2, arena=AR)
        vpool = TPool(nc, "vh", [128, 32, 128], BF16, 2, arena=AR)
        epool = TPool(nc, "eT", [128, 512], BF16, 8, arena=AR)
        o32 = TPool(nc, "o32", [128, 512], F32, 9, arena=AR)
        psO = [PS.t[0], PS.t[1]]
        psS = [PS.t[2], PS.t[3]]
        PSs = SubPool(PS.t[4:8])
        hbuf = {}

        def att_load(h):
            kt, rkt = kpool.get()
            P.dma(kt[:], kT_s.ap()[h * 128:(h + 1) * 128, :], reads=[R_kT], writes=[rkt])
            qt, rqt = qpool.get()
            P.dma(qt[:], qT_s.ap()[h * 128:(h + 1) * 128, :], reads=[R_qT], writes=[rqt])
            vt, rvt = vpool.get()
            P.dma(vt[:], v_s.ap()[:, h * 128:(h + 1) * 128].rearrange("(j p) e -> p j e", p=128), reads=[R_v], writes=[rvt])
            hbuf[h] = (kt, rkt, qt, rqt, vt, rvt)
            return
            yield

        def att_unit(h, qb, j, c):
            kt, rkt, qt, rqt, vt, rvt = hbuf[h]
            q0 = qb * TB
            nj = 4 * qb + 4
            lo = max(0, 128 * j - q0)
            pss, rpss = PSs.get()
            P.op("pe", lambda: nc.tensor.matmul(pss[:, lo:512], lhsT=kt[64 * c:64 * c + 64, 128 * j:128 * j + 128],
                                                rhs=qt[64 * c:64 * c + 64, q0 + lo:q0 + 512], start=True, stop=True),
                 reads=[rkt, rqt], writes=[rpss])
            et, ret = epool.get()
            P.op("act", lambda: nc.scalar.activation(out=et[:, lo:512], in_=pss[:, lo:512], func=AF.Exp, scale=0.125),
                 reads=[rpss], writes=[ret])
            if j >= 4 * qb:
                a = 128 * (j - 4 * qb)
                b = min(a + 256, 512)
                ba = 0
            elif j == 4 * qb - 1:
                a, b, ba = 0, 128, 128
            else:
                a = None
            if a is not None:
                P.op("dve", lambda: nc.vector.tensor_tensor(out=et[:, a:b], in0=et[:, a:b], in1=expB[:, h, ba:ba + (b - a)],
                                                            op=ALU.mult), reads=[ret, R_c], writes=[ret])
            yield
            yield
            P.op("pe", lambda: nc.tensor.matmul(psO[c][0][:, lo:512], lhsT=vt[:, j, :], rhs=et[:, lo:512],
                                                start=(j == 0), stop=(j == nj - 1)), reads=[rvt, ret], writes=[psO[c][1]])
            P.op("pe", lambda: nc.tensor.matmul(psS[c][0][:, lo:512], lhsT=ones_b[:], rhs=et[:, lo:512],
                                                start=(j == 0), stop=(j == nj - 1)), reads=[ret, R_c], writes=[psS[c][1]])

        def att_final(h, qb):
            q0 = qb * TB
            yield
            r0, rr0 = o32.get()
            t0, rt0 = o32.get()
            t1, rt1 = o32.get()
            P.op("dve", lambda: nc.vector.reciprocal(out=r0[:], in_=psS[0][0][:]), reads=[psS[0][1]], writes=[rr0])
            P.op("dve", lambda: nc.vector.tensor_tensor(out=t0[:], in0=psO[0][0][:], in1=r0[:], op=ALU.mult),
                 reads=[psO[0][1], rr0], writes=[rt0])
            P.op("dve", lambda: nc.vector.reciprocal(out=t1[:], in_=psS[1][0][:]), reads=[psS[1][1]], writes=[rt1])
            P.op("dve", lambda: nc.vector.tensor_tensor(out=t1[:], in0=psO[1][0][:], in1=t1[:], op=ALU.mult),
                 reads=[psO[1][1]], writes=[rt1])
            P.op("dve", lambda: nc.vector.scalar_tensor_tensor(out=t0[:], in0=t1[:], scalar=neglam[:, 0:1], in1=t0[:],
                                                               op0=ALU.mult, op1=ALU.add), reads=[rt1, R_lam], writes=[rt0])
            sq, rsq = epool.get()
            P.op("act", lambda: nc.scalar.activation(out=sq[:], in_=t0[:], func=AF.Square), reads=[rt0], writes=[rsq])
            yield
            yield
            pss, rpss = PSs.get()
            P.op("pe", lambda: nc.tensor.matmul(pss[:], lhsT=ones_b[:], rhs=sq[:], start=True, stop=True),
                 reads=[rsq, R_c], writes=[rpss])
            P.op("act", lambda: nc.scalar.activation(out=r0[:], in_=pss[:], func=AF.Sqrt, scale=1.0 / 128, bias=epsc[:, 0:1]),
                 reads=[rpss, R_c], writes=[rr0])
            P.op("dve", lambda: nc.vector.reciprocal(out=r0[:], in_=r0[:]), reads=[rr0], writes=[rr0])
            P.op("dve", lambda: nc.vector.scalar_tensor_tensor(out=BIGA[:, h, q0:q0 + TB], in0=t0[:], scalar=subsc[:, 0:1], in1=r0[:],
                                                               op0=ALU.mult, op1=ALU.mult), reads=[rt0, rr0, R_lam], writes=[R_A[h]])

        def att_gens():
            yield att_load(0)
            for h in range(8):
                for qb in range(NTB):
                    for j in range(4 * qb + 4):
                        for c in range(2):
                            yield att_unit(h, qb, j, c)
                    yield att_final(h, qb)
                    if qb == 1 and h + 1 < 8:
                        yield att_load(h + 1)
        run_pipe(att_gens())

        wpool3 = TPool(nc, "wt3", [128, 8, 512], BF16, 2, arena=AR)

        def proj_to_br(wname, row_base, wp):
            for grp in range(2):
                wt, rwt = wp.get()
                P.dma(wt[:], W[wname].ap()[l][:, grp * 512:(grp + 1) * 512].rearrange("(kc p) n -> p kc n", p=128),
                      writes=[rwt], q="pool")
                for ci in range(4):
                    for tb in range(NTB):
                        ps, rps = mm_fm(wt, rwt, ci, tb)
                        o, ro = b16p.get()
                        if tb % 2 == 0:
                            P.op("act", lambda: nc.scalar.copy(out=o[:], in_=ps[:]), reads=[rps], writes=[ro])
                        else:
                            P.op("dve", lambda: nc.vector.tensor_copy(out=o[:], in_=ps[:]), reads=[rps], writes=[ro])
                        r0_ = row_base + (grp * 4 + ci) * 128
                        P.dma(br_s.ap()[r0_:r0_ + 128, tb * TB:(tb + 1) * TB], o[:], reads=[ro], writes=[R_br])
        proj_to_br("w_attn_out", 0, wpool3)

        s5_stage(l)

        P.barrier()
        AR.reset()
        diag = BIGA[:, :, 0:31 * 128].rearrange("p a (k c) -> p a k c", c=128)
        R_diag = Res("diag")
        for kc in range(8):
            for k in range(31):
                P.op("dve", lambda: nc.vector.tensor_scalar(out=diag[:, kc, k, :], in0=ident_f[:],
                                                            scalar1=vec[:, V_CW + kc * 31 + k:V_CW + kc * 31 + k + 1], scalar2=None,
                                                            op0=ALU.mult), reads=[R_vec, R_c], writes=[R_diag])
        wc = AR.alloc([128, 8, 1024], BF16)
        R_wc = Res("wc")
        P.dma(wc[:], W["conv_w_out"].ap()[l].rearrange("(kc p) n -> p kc n", p=128), writes=[R_wc], q="pool")
        cin = TPool(nc, "cin", [128, 544], BF16, 3, arena=AR)
        cvp = TPool(nc, "cv", [128, 8, 512], F32, 2, arena=AR)
        xbp = TPool(nc, "xb", [128, 8, 512], BF16, 2, arena=AR)
        sqp5 = TPool(nc, "sq5", [128, 8, 512], BF16, 2, arena=AR)
        ynp = TPool(nc, "yn", [128, 8, 512], BF16, 2, arena=AR)
        def conv_tb(tb):
            cv, rcv = cvp.get()
            xb, rxb = xbp.get()
            sq, rsq = sqp5.get()
            for kc in range(8):
                ct, rct = cin.get()
                if tb == 0:
                    P.op("pool", lambda: nc.gpsimd.memset(ct[:, 0:30], 0.0), writes=[rct])
                    P.dma(ct[:, 30:542], cT_s.ap()[kc * 128:(kc + 1) * 128, 0:512], reads=[R_cT], writes=[rct])
                else:
                    P.dma(ct[:, 0:542], cT_s.ap()[kc * 128:(kc + 1) * 128, tb * TB - 30:tb * TB + 512], reads=[R_cT], writes=[rct])
                ps, rps = PS.get()
                for k in range(31):
                    P.op("pe", lambda: nc.tensor.matmul(ps[:], lhsT=diag[:, kc, k, :], rhs=ct[:, k:k + 512], start=(k == 0), stop=(k == 30)),
                         reads=[R_diag, rct], writes=[rps])
                P.op("act", lambda: nc.scalar.activation(out=cv[:, kc, :], in_=ps[:], func=AF.Identity, bias=vec[:, V_CB + kc:V_CB + kc + 1]),
                     reads=[rps, R_vec], writes=[rcv])
                P.op("dve", lambda: nc.vector.tensor_copy(out=xb[:, kc, :], in_=cv[:, kc, :]), reads=[rcv], writes=[rxb])
                P.op("act", lambda: nc.scalar.activation(out=sq[:, kc, :], in_=cv[:, kc, :], func=AF.Square), reads=[rcv], writes=[rsq])
            yield
            ps1, rps1 = PS.get()
            ps2, rps2 = PS.get()
            for kc in range(8):
                P.op("pe", lambda: nc.tensor.matmul(ps1[:], lhsT=ones_b[:], rhs=xb[:, kc, :], start=(kc == 0), stop=(kc == 7)),
                     reads=[rxb, R_c], writes=[rps1])
            for kc in range(8):
                P.op("pe", lambda: nc.tensor.matmul(ps2[:], lhsT=ones_b[:], rhs=sq[:, kc, :], start=(kc == 0), stop=(kc == 7)),
                     reads=[rsq, R_c], writes=[rps2])
            mean, rmean = f32p.get()
            msq, rmsq = f32p.get()
            rs, rrs = f32p.get()
            P.op("act", lambda: nc.scalar.mul(out=mean[:], in_=ps1[:], mul=1.0 / D), reads=[rps1], writes=[rmean])
            P.op("dve", lambda: nc.vector.tensor_tensor(out=msq[:], in0=mean[:], in1=mean[:], op=ALU.mult), reads=[rmean], writes=[rmsq])
            P.op("dve", lambda: nc.vector.scalar_tensor_tensor(out=rs[:], in0=ps2[:], scalar=1.0 / D, in1=msq[:], op0=ALU.mult,
                                                               op1=ALU.subtract), reads=[rps2, rmsq], writes=[rrs])
            P.op("act", lambda: nc.scalar.activation(out=rs[:], in_=rs[:], func=AF.Sqrt, bias=epsc[:, 0:1]), reads=[rrs, R_c], writes=[rrs])
            P.op("dve", lambda: nc.vector.reciprocal(out=rs[:], in_=rs[:]), reads=[rrs], writes=[rrs])
            yn, ryn = ynp.get()
            for kc in range(8):
                P.op("dve", lambda: nc.vector.tensor_tensor(out=cv[:, kc, :], in0=cv[:, kc, :], in1=mean[:], op=ALU.subtract),
                     reads=[rmean], writes=[rcv])
                P.op("dve", lambda: nc.vector.tensor_tensor(out=cv[:, kc, :], in0=cv[:, kc, :], in1=rs[:], op=ALU.mult),
                     reads=[rrs], writes=[rcv])
                P.op("act", lambda: nc.scalar.activation(out=yn[:, kc, :], in_=cv[:, kc, :], func=AF.Silu,
                                                         scale=vec[:, V_LNG + kc:V_LNG + kc + 1], bias=vec[:, V_LNB + kc:V_LNB + kc + 1]),
                     reads=[rcv, R_vec], writes=[ryn])
            yield
            for co in range(8):
                ps, rps = PS.get()
                for kc in range(8):
                    P.op("pe", lambda: nc.tensor.matmul(ps[:], lhsT=wc[:, kc, co * 128:(co + 1) * 128], rhs=yn[:, kc, :],
                                                        start=(kc == 0), stop=(kc == 7)), reads=[R_wc, ryn], writes=[rps])
                o, ro = b16p.get()
                if co % 2 == 0:
                    P.op("act", lambda: nc.scalar.copy(out=o[:], in_=ps[:]), reads=[rps], writes=[ro])
                else:
                    P.op("dve", lambda: nc.vector.tensor_copy(out=o[:], in_=ps[:]), reads=[rps], writes=[ro])
                r0_ = 2 * D + co * 128
                P.dma(br_s.ap()[r0_:r0_ + 128, tb * TB:(tb + 1) * TB], o[:], reads=[ro], writes=[R_br])

        run_pipe(conv_tb(tb) for tb in range(NTB))

        P.barrier()
        AR.reset()
        wo = AR.alloc([128, 8, 1024], BF16)
        R_wo = Res("wo")
        P.dma(wo[:], W["w_out"].ap()[l].rearrange("(kc p) n -> p kc n", p=128), writes=[R_wo], q="pool")
        inp6 = TPool(nc, "in6", [128, 512], BF16, 12, arena=AR)
        mixp = TPool(nc, "mix", [128, 8, 512], BF16, 2, arena=AR)
        xin6 = TPool(nc, "xin6", [128, 8, 512], F32, 1, arena=AR)
        x1p = TPool(nc, "x1p", [128, 8, 512], F32, 2, arena=AR)
        sqp6 = TPool(nc, "sq6", [128, 8, 512], BF16, 1, arena=AR)
        for tb in range(NTB):
            mix, rmix = mixp.get()
            for kc in range(8):
                tl = []
                for i in range(3):
                    bt, rbt = inp6.get()
                    P.dma(bt[:], br_s.ap()[i * D + kc * 128:i * D + (kc + 1) * 128, tb * TB:(tb + 1) * TB], reads=[R_br], writes=[rbt])
                    gt, rgt = inp6.get()
                    P.dma(gt[:], gT_s.ap()[i * D + kc * 128:i * D + (kc + 1) * 128, tb * TB:(tb + 1) * TB], reads=[R_gT], writes=[rgt])
                    tl.append((bt, rbt, gt, rgt))
                ta, rta = f32p.get()
                tb2, rtb2 = f32p.get()
                P.op("pool", lambda: nc.gpsimd.tensor_tensor(out=ta[:], in0=tl[0][0][:], in1=tl[0][2][:], op=ALU.mult),
                     reads=[tl[0][1], tl[0][3]], writes=[rta])
                P.op("pool", lambda: nc.gpsimd.tensor_tensor(out=tb2[:], in0=tl[1][0][:], in1=tl[1][2][:], op=ALU.mult),
                     reads=[tl[1][1], tl[1][3]], writes=[rtb2])
                P.op("dve", lambda: nc.vector.tensor_tensor(out=ta[:], in0=ta[:], in1=tb2[:], op=ALU.add), reads=[rtb2], writes=[rta])
                P.op("pool", lambda: nc.gpsimd.tensor_tensor(out=tb2[:], in0=tl[2][0][:], in1=tl[2][2][:], op=ALU.mult),
                     reads=[tl[2][1], tl[2][3]], writes=[rtb2])
                P.op("dve", lambda: nc.vector.tensor_tensor(out=mix[:, kc, :], in0=ta[:], in1=tb2[:], op=ALU.add),
                     reads=[rta, rtb2], writes=[rmix])
            xb, rxb = xin6.get()
            P.dma(xb[:], x_src.ap()[:, tb * TB:(tb + 1) * TB].rearrange("(kc p) t -> p kc t", p=128), reads=[R_xsrc], writes=[rxb])
            x1, rx1 = x1p.get()
            for co in range(8):
                ps, rps = PS.get()
                for kc in range(8):
                    P.op("pe", lambda: nc.tensor.matmul(ps[:], lhsT=wo[:, kc, co * 128:(co + 1) * 128], rhs=mix[:, kc, :],
                                                        start=(kc == 0), stop=(kc == 7)), reads=[R_wo, rmix], writes=[rps])
                P.op("dve", lambda: nc.vector.tensor_tensor(out=x1[:, co, :], in0=ps[:], in1=xb[:, co, :], op=ALU.add),
                     reads=[rps, rxb], writes=[rx1])
            P.dma(x1T_s.ap()[:, tb * TB:(tb + 1) * TB].rearrange("(kc p) t -> p kc t", p=128), x1[:], reads=[rx1], writes=[R_x1])
            norm_block(x1, rx1, V_GFFN, tb, sqp6)

        P.barrier()
        AR.reset()
        wpool8 = TPool(nc, "wt8", [128, 8, 512], BF16, 2, arena=AR)
        ufull = TPool(nc, "uf", [128, 2 + S], BF16, 8, arena=AR)
        dgp = TPool(nc, "dg", [128, 4, 3, 128], BF16, 2, arena=AR)
        w_up2 = W["ffn_w_up"].ap()[l]
        for grp in range(11):
            wt, rwt = wpool8.get()
            P.dma(wt[:, :, 0:256], w_up2[:, grp * 256:(grp + 1) * 256].rearrange("(kc p) n -> p kc n", p=128), writes=[rwt], q="pool")
            P.dma(wt[:, :, 256:512], w_up2[:, FFN_H + grp * 256:FFN_H + (grp + 1) * 256].rearrange("(kc p) n -> p kc n", p=128),
                  writes=[rwt], q="pool")
            dgt, rdg = dgp.get()
            for ci in range(4):
                gch = (2 * grp + ci) if ci < 2 else (22 + 2 * grp + ci - 2)
                for k in range(3):
                    P.op("pool", lambda: nc.gpsimd.tensor_scalar(out=dgt[:, ci, k, :], in0=ident_f[:],
                                                                 scalar1=vec[:, V_FW + gch * 3 + k:V_FW + gch * 3 + k + 1], scalar2=None,
                                                                 op0=ALU.mult), reads=[R_vec, R_c], writes=[rdg])
            ufs = [ufull.get() for _ in range(4)]
            for ci in range(4):
                P.op("pool", lambda: nc.gpsimd.memset(ufs[ci][0][:, 0:2], 0.0), writes=[ufs[ci][1]])
                for tb in range(NTB):
                    ps, rps = mm_fm(wt, rwt, ci, tb)
                    if tb % 2 == 0:
                        P.op("act", lambda: nc.scalar.copy(out=ufs[ci][0][:, 2 + tb * TB:2 + (tb + 1) * TB], in_=ps[:]),
                             reads=[rps], writes=[ufs[ci][1]])
                    else:
                        P.op("dve", lambda: nc.vector.tensor_copy(out=ufs[ci][0][:, 2 + tb * TB:2 + (tb + 1) * TB], in_=ps[:]),
                             reads=[rps], writes=[ufs[ci][1]])
            for pi in range(2):
                for tb in range(NTB):
                    psv, rpsv = PS.get()
                    psg, rpsg = PS.get()
                    for k in range(3):
                        P.op("pe", lambda: nc.tensor.matmul(psv[:], lhsT=dgt[:, pi, k, :], rhs=ufs[pi][0][:, tb * TB + k:tb * TB + k + TB],
                                                            start=(k == 0), stop=(k == 2)), reads=[rdg, ufs[pi][1]], writes=[rpsv])
                    for k in range(3):
                        P.op("pe", lambda: nc.tensor.matmul(psg[:], lhsT=dgt[:, pi + 2, k, :], rhs=ufs[pi + 2][0][:, tb * TB + k:tb * TB + k + TB],
                                                            start=(k == 0), stop=(k == 2)), reads=[rdg, ufs[pi + 2][1]], writes=[rpsg])
                    gl, rgl = f32p.get()
                    P.op("act", lambda: nc.scalar.activation(out=gl[:], in_=psg[:], func=AF.Gelu_apprx_tanh), reads=[rpsg], writes=[rgl])
                    o, ro = b16p.get()
                    P.op("dve", lambda: nc.vector.tensor_tensor(out=o[:], in0=psv[:], in1=gl[:], op=ALU.mult), reads=[rpsv, rgl], writes=[ro])
                    r0_ = (2 * grp + pi) * 128
                    P.dma(act_s.ap()[r0_:r0_ + 128, tb * TB:(tb + 1) * TB], o[:], reads=[ro], writes=[R_act])

        P.barrier()
        AR.reset()
        wd = AR.alloc([128, 22, 1024], BF16)
        R_wd = Res("wd")
        wdd = W["ffn_w_down"].ap()[l].rearrange("(kc p) n -> p kc n", p=128)
        P.dma(wd[:, 0:11, :], wdd[:, 0:11, :], writes=[R_wd], q="pool")
        P.dma(wd[:, 11:22, :], wdd[:, 11:22, :], writes=[R_wd], q="pool")
        actp = TPool(nc, "actp", [128, 22, 512], BF16, 2, arena=AR)
        xp9 = TPool(nc, "xp9", [128, 512], F32, 4, arena=AR)
        for tb in range(NTB):
            at, rat = actp.get()
            P.dma(at[:], act_s.ap()[:, tb * TB:(tb + 1) * TB].rearrange("(kc p) t -> p kc t", p=128), reads=[R_act], writes=[rat])
            for co in range(8):
                xr, rxr = xp9.get()
                P.dma(xr[:], x1T_s.ap()[co * 128:(co + 1) * 128, tb * TB:(tb + 1) * TB], reads=[R_x1], writes=[rxr])
                ps, rps = PS.get()
                for kc in range(22):
                    P.op("pe", lambda: nc.tensor.matmul(ps[:], lhsT=wd[:, kc, co * 128:(co + 1) * 128], rhs=at[:, kc, :],
                                                        start=(kc == 0), stop=(kc == 21)), reads=[R_wd, rat], writes=[rps])
                P.op("dve", lambda: nc.vector.tensor_tensor(out=xr[:], in0=ps[:], in1=xr[:], op=ALU.add), reads=[rps], writes=[rxr])
                P.dma(x_dst.ap()[co * 128:(co + 1) * 128, tb * TB:(tb + 1) * TB], xr[:], reads=[rxr], writes=[R_xdst])

    P.finish()
    return nc


_CACHE = {}


def make_in_maps(inputs, nb=8):
    common = {}
    for nm in ("w_in", "w_attn_out", "s5_glu_w1", "s5_glu_w2", "conv_w_out", "w_out", "ffn_w_up", "ffn_w_down"):
        common[nm] = np.ascontiguousarray(inputs[nm], dtype=np.float32)
    vs, ls, ss = [], [], []
    for l in range(DEPTH):
        v, lamv, s5 = host_layer_params(inputs, l)
        vs.append(v)
        ls.append(lamv)
        ss.append(s5)
    common["vecs"] = np.stack(vs)
    common["lamv"] = np.stack(ls)
    common["s5p"] = np.stack(ss)
    common["rel_bias"] = np.ascontiguousarray(inputs["rel_bias"], dtype=np.float32)
    common.update({"c_" + k: v for k, v in host_consts().items()})
    x = inputs["x"]
    in_maps = []
    for b in range(nb):
        m = dict(common)
        m["xT"] = np.ascontiguousarray(x[b].T)
        in_maps.append(m)
    return in_maps


def kernel(**inputs):
    inputs = {k: np.asarray(v) for k, v in inputs.items()}
    if "nc" not in _CACHE:
        _CACHE["nc"] = build()
    nc = _CACHE["nc"]
    in_maps = make_in_maps(inputs)
    res = run_bass_kernel_spmd(nc, in_maps, core_ids=list(range(8)))
    out = np.stack([np.ascontiguousarray(r["yT"].T) for r in res.results], 0)
    return out.astype(np.float32)
```

```python
import math
import numpy as np
import ml_dtypes
import concourse.bass as bass
import concourse.mybir as mybir
from concourse.bass_utils import run_bass_kernel_spmd

F32 = mybir.dt.float32
BF16 = mybir.dt.bfloat16
I32 = mybir.dt.int32
ALU = mybir.AluOpType
AF = mybir.ActivationFunctionType
AX = mybir.AxisListType

D = 1024
S = 4096
DEPTH = 2
TB = 512
NTB = S // TB
FFN_H = 2816
IN_W = 9216
OFF_Q, OFF_K, OFF_V, OFF_U, OFF_C, OFF_G = 0, 1024, 2048, 3072, 4096, 6144
EPS = 1e-6
GC = 1.5957691216057308


class Res:
    __slots__ = ("name", "lw", "rd")

    def __init__(self, name=""):
        self.name = name
        self.lw = None
        self.rd = []


class Prog:
    ENGS = ("pe", "dve", "act", "pool", "sp")

    def __init__(self, nc, n_dma_sems=32):
        self.nc = nc
        self.eng = {"pe": nc.tensor, "dve": nc.vector, "act": nc.scalar,
                    "pool": nc.gpsimd, "sp": nc.sync}
        self.sems = {}
        self.cnt = {}
        for e in self.ENGS:
            self.sems[e] = nc.alloc_semaphore("c_" + e)
            self.cnt[e] = 0
        self.n_dma = n_dma_sems
        for i in range(n_dma_sems):
            k = "d%d" % i
            self.sems[k] = nc.alloc_semaphore("s_" + k)
            self.cnt[k] = 0
        self.dma_rr = 0
        self.known = {e: {} for e in self.ENGS}
        self.ninstr = 0

    def _deps(self, reads, writes):
        deps = {}

        def add(t):
            if t is None:
                return
            k, v = t
            if deps.get(k, 0) < v:
                deps[k] = v
        for r in reads:
            add(r.lw)
        for w in writes:
            add(w.lw)
            for t in w.rd:
                add(t)
        return deps

    def _wait(self, e, deps):
        kn = self.known[e]
        for k, v in deps.items():
            if k == e and e == "pe":
                continue
            if kn.get(k, 0) >= v:
                continue
            self.eng[e].wait_ge(self.sems[k], v)
            kn[k] = v

    def _commit(self, tok, reads, writes):
        for r in reads:
            r.rd.append(tok)
            if len(r.rd) > 48:
                m = {}
                for k, v in r.rd:
                    if m.get(k, 0) < v:
                        m[k] = v
                r.rd = list(m.items())
        for w in writes:
            w.lw = tok
            w.rd = []

    def op(self, e, fn, reads=(), writes=()):
        deps = self._deps(reads, writes)
        self._wait(e, deps)
        ins = fn()
        self.cnt[e] += 1
        ins.then_inc(self.sems[e], 1)
        self._commit((e, self.cnt[e]), reads, writes)
        self.ninstr += 1
        return ins

    def dma(self, out, in_, reads=(), writes=(), q="sp", **kw):
        deps = self._deps(reads, writes)
        self._wait(q, deps)
        k = "d%d" % self.dma_rr
        self.dma_rr = (self.dma_rr + 1) % self.n_dma
        self._wait(q, {k: self.cnt[k]})
        ins = self.eng[q].dma_start(out=out, in_=in_, **kw)
        self.cnt[k] += 16
        ins.then_inc(self.sems[k], 16)
        self._commit((k, self.cnt[k]), reads, writes)
        self.ninstr += 1
        return ins

    def barrier(self):
        allv = {k: v for k, v in self.cnt.items() if v > 0}
        for e in self.ENGS:
            self._wait(e, allv)

    def finish(self, e="sp"):
        allv = {k: v for k, v in self.cnt.items() if v > 0}
        self._wait(e, allv)


class Arena:
    def __init__(self, t, nelem):
        self.t = t
        self.n = nelem
        self.off = 0

    def reset(self):
        self.off = 0

    def alloc(self, shape, dt):
        nel = 1
        for d in shape[1:]:
            nel *= d
        nb = nel * (2 if dt == BF16 else 4)
        nb = (nb + 31) // 32 * 32
        assert self.off + nb // 2 <= self.n, ("arena overflow", self.off, nb, self.n)
        ap = self.t[:, self.off:self.off + nb // 2]
        self.off += nb // 2
        if dt != BF16:
            ap = ap.bitcast(dt)
        ap = ap[:, 0:nel]
        if len(shape) == 3:
            ap = ap.rearrange("p (a b) -> p a b", a=shape[1])
        elif len(shape) == 4:
            ap = ap.rearrange("p (a b c) -> p a b c", a=shape[1], b=shape[2])
        return ap


class SubPool:
    def __init__(self, items):
        self.t = list(items)
        self.i = 0

    def get(self):
        r = self.t[self.i]
        self.i = (self.i + 1) % len(self.t)
        return r


class TPool:
    def __init__(self, nc, name, shape, dt, n, psum=False, arena=None):
        self.t = []
        for i in range(n):
            if psum:
                h = nc.alloc_psum_tensor("%s%d" % (name, i), shape, dt)
            elif arena is not None:
                h = arena.alloc(shape, dt)
            else:
                h = nc.alloc_sbuf_tensor("%s%d" % (name, i), shape, dt)
            self.t.append((h, Res("%s%d" % (name, i))))
        self.i = 0

    def get(self):
        r = self.t[self.i]
        self.i = (self.i + 1) % len(self.t)
        return r


def run_pipe(gens):
    active = []

    def step():
        nxt = []
        for a in active:
            try:
                next(a)
                nxt.append(a)
            except StopIteration:
                pass
        active[:] = nxt
    for g in gens:
        active.append(g)
        step()
    while active:
        step()


def t5_bucket_np(rel):
    nb = 16
    n = -rel
    ret = np.where(n < 0, nb, 0)
    n = np.abs(n)
    max_exact = nb // 2
    nf = np.maximum(n, 1).astype(np.float32)
    large = max_exact + (np.log(nf / max_exact) / math.log(128 / max_exact) * (nb - max_exact)).astype(np.int32)
    large = np.minimum(large, nb - 1)
    return ret + np.where(n < max_exact, n, large)


def host_consts():
    c = {}
    c["ident_f"] = np.eye(128, dtype=np.float32)
    jsw = np.zeros((128, 128), np.float32)
    for i in range(128):
        jsw[i, (i + 64) % 128] = 1.0
    c["jswap_f"] = jsw
    c["jrev_f"] = np.eye(128, dtype=np.float32)[::-1].copy()
    bd = np.zeros((128, 128), np.float32)
    bd[:64, :64] = 1.0
    bd[64:, 64:] = 1.0
    c["bdones_f"] = bd
    m = np.arange(384)
    rel = 127 - m
    bk = t5_bucket_np(rel)
    oh = np.zeros((32, 384), np.float32)
    oh[bk, m] = 1.0
    oh[15, :] -= 1.0
    oh[:, 383] = 0.0
    c["onehot"] = oh
    assert np.all(t5_bucket_np(-np.arange(129, 4096)) == 15)
    mk = np.ones((128, 256), np.float32)
    mk[64:, :64] = 0.0
    c["bandmask"] = mk
    ii = np.arange(128) // 16
    c["mask_intra"] = (ii[None, :] >= ii[:, None]).astype(np.float32)
    sg = np.zeros((128, 2), np.float32)
    sg[:64, 0] = -1.0
    sg[64:, 0] = 1.0
    sg[:64, 1] = 1.0
    sg[64:, 1] = -1.0
    c["sig"] = sg
    return c


NVEC = 8 + 8 + 1 + 1 + 1 + 8 + 8 + 8 + 8 * 31 + 44 * 3
V_GMIX, V_GFFN, V_QG, V_KG, V_SUB, V_CB, V_LNG, V_LNB, V_CW, V_FW = 0, 8, 16, 17, 18, 19, 27, 35, 43, 43 + 248


def host_layer_params(inp, l):
    v = np.zeros((128, NVEC), np.float32)
    v[:, V_GMIX:V_GMIX + 8] = inp["norm_mix"][l].reshape(8, 128).T
    v[:, V_GFFN:V_GFFN + 8] = inp["norm_ffn"][l].reshape(8, 128).T
    v[:, V_QG] = np.tile(inp["qk_gain_q"][l], 2)
    v[:, V_KG] = np.tile(inp["qk_gain_k"][l], 2)
    v[:, V_SUB] = inp["diff_subln"][l]
    v[:, V_CB:V_CB + 8] = inp["conv_dw_b"][l].reshape(8, 128).T
    v[:, V_LNG:V_LNG + 8] = inp["conv_ln_g"][l].reshape(8, 128).T
    v[:, V_LNB:V_LNB + 8] = inp["conv_ln_b"][l].reshape(8, 128).T
    v[:, V_CW:V_CW + 248] = inp["conv_dw_w"][l].reshape(31, 8, 128).transpose(2, 1, 0).reshape(128, 248)
    v[:, V_FW:V_FW + 132] = inp["ffn_dw_w"][l].reshape(3, 44, 128).transpose(2, 1, 0).reshape(128, 132)
    lamv = np.concatenate([inp["lambda_q1"][l], inp["lambda_k1"][l], inp["lambda_q2"][l], inp["lambda_k2"][l]])
    lamv = np.broadcast_to(lamv[None, :], (128, 256)).copy()
    s5 = np.zeros((128, 64 * 4 + 4 * 64 * 16), np.float32)
    lr = inp["s5_lambda_re"][l].T
    li = inp["s5_lambda_im"][l].T
    s5[:, 0:64] = np.concatenate([lr, lr], 0)
    s5[:, 64:128] = np.concatenate([li, li], 0)
    s5[:, 128:192] = np.broadcast_to(inp["s5_log_step"][l][None, :], (128, 64))
    s5[:, 192:256] = np.tile(inp["s5_d"][l].T, (8, 1))
    br = inp["s5_b_re"][l].transpose(1, 0, 2).reshape(64, 1024)
    bi = inp["s5_b_im"][l].transpose(1, 0, 2).reshape(64, 1024)
    cr = inp["s5_c_re"][l].transpose(2, 0, 1).reshape(64, 1024)
    ci = inp["s5_c_im"][l].transpose(2, 0, 1).reshape(64, 1024)
    s5[:, 256:1280] = np.concatenate([br, bi], 0)
    s5[:, 1280:2304] = np.concatenate([bi, br], 0)
    s5[:, 2304:3328] = np.concatenate([cr, ci], 0)
    s5[:, 3328:4352] = np.concatenate([ci, cr], 0)
    return v, lamv, s5


def build(dbg=None, nlayers=DEPTH):
    nc = bass.Bass("TRN2", target_bir_lowering=False)
    P = Prog(nc)
    dbg = dbg or set()

    def dram_in(name, shape, dt=F32):
        return nc.dram_tensor(name, list(shape), dt, kind="ExternalInput")

    def scratch(name, shape, dt):
        kind = "ExternalOutput" if name in dbg else "Internal"
        return nc.dram_tensor(name, list(shape), dt, kind=kind)

    xT_in = dram_in("xT", [D, S])
    W = {}
    for nm, shp in [("w_in", [DEPTH, D, IN_W]), ("w_attn_out", [DEPTH, D, D]), ("s5_glu_w1", [DEPTH, D, D]),
                    ("s5_glu_w2", [DEPTH, D, D]), ("conv_w_out", [DEPTH, D, D]), ("w_out", [DEPTH, D, D]),
                    ("ffn_w_up", [DEPTH, D, 2 * FFN_H]), ("ffn_w_down", [DEPTH, FFN_H, D])]:
        W[nm] = dram_in(nm, shp)
    vecs_in = dram_in("vecs", [DEPTH, 128, NVEC])
    lamv_in = dram_in("lamv", [DEPTH, 128, 256])
    s5p_in = dram_in("s5p", [DEPTH, 128, 4352])
    relb_in = dram_in("rel_bias", [32, 8])
    W_names = list(W.keys())
    cst = {}
    for nm, shp in [("ident_f", [128, 128]), ("jswap_f", [128, 128]), ("jrev_f", [128, 128]), ("bdones_f", [128, 128]),
                    ("onehot", [32, 384]), ("bandmask", [128, 256]), ("mask_intra", [128, 128]), ("sig", [128, 2])]:
        cst[nm] = dram_in("c_" + nm, shp)
    yT_out = nc.dram_tensor("yT", [D, S], F32, kind="ExternalOutput")

    qT_s = scratch("qT_s", [D, S], BF16)
    kT_s = scratch("kT_s", [D, S], BF16)
    v_s = scratch("v_s", [S, D], BF16)
    u_s = scratch("u_s", [512, 8192], BF16)
    cT_s = scratch("cT_s", [D, S], BF16)
    gT_s = scratch("gT_s", [3 * D, S], BF16)
    br_s = scratch("br_s", [3 * D, S], BF16)
    x1T_s = scratch("x1T_s", [D, S], F32)
    xmidT_s = scratch("xmidT_s", [D, S], F32)
    act_s = scratch("act_s", [FFN_H, S], BF16)
    wr_d = scratch("wr_d", [8, 384], F32)
    R_qT, R_kT, R_v, R_u, R_cT, R_gT, R_br, R_x1, R_xmid, R_act, R_wr = [Res(n) for n in
        ("qT", "kT", "v", "u", "cT", "gT", "br", "x1", "xmid", "act", "wr")]

    sb = nc.alloc_sbuf_tensor
    R_c = Res("consts")
    ident_f = sb("ident_f", [128, 128], F32)
    jswap_f = sb("jswap_f", [128, 128], F32)
    jrev_f = sb("jrev_f", [128, 128], F32)
    mask_intra = sb("mask_intra", [128, 128], F32)
    bandmask = sb("bandmask", [128, 256], F32)
    sig = sb("sig", [128, 2], F32)
    ident_b = sb("ident_b", [128, 128], BF16)
    ones_b = sb("ones_b", [128, 128], BF16)
    bdones_b = sb("bdones_b", [128, 128], BF16)
    epsc = sb("epsc", [128, 1], F32)
    onehot = sb("onehot", [32, 384], F32)
    relb = sb("relb", [32, 8], F32)
    expB = sb("expB", [128, 8, 256], BF16)
    for t_, nm in [(ident_f, "ident_f"), (jswap_f, "jswap_f"), (jrev_f, "jrev_f"), (mask_intra, "mask_intra"),
                   (bandmask, "bandmask"), (sig, "sig"), (onehot, "onehot")]:
        P.dma(t_[:], cst[nm].ap(), writes=[R_c])
    P.dma(relb[:], relb_in.ap(), writes=[R_c])
    P.dma(bdones_b[:], cst["bdones_f"].ap(), writes=[R_c], q="pool")
    P.op("dve", lambda: nc.vector.tensor_copy(out=ident_b[:], in_=ident_f[:]), reads=[R_c], writes=[R_c])
    P.op("dve", lambda: nc.vector.memset(ones_b[:], 1.0), writes=[R_c])
    P.op("dve", lambda: nc.vector.memset(epsc[:], EPS), writes=[R_c])

    BIGA = sb("BIGA", [128, 8, S], BF16)
    R_A = [Res("A%d" % i) for i in range(8)]
    ARENA_N = 55296
    AR = Arena(sb("ARENA", [128, ARENA_N], BF16), ARENA_N)
    vec = sb("vec", [128, NVEC], F32)
    R_vec = Res("vec")
    lam4 = sb("lam4", [128, 256], F32)
    neglam = sb("neglam", [128, 1], F32)
    subsc = sb("subsc", [128, 1], F32)
    R_lam = Res("lam")

    PSALL = nc.alloc_psum_tensor("psall", [128, 4096], F32)
    PS = SubPool([(PSALL[:, i * 512:(i + 1) * 512], Res("ps%d" % i)) for i in range(8)])
    PSPAIR = [(PSALL[:, 2048:3072].rearrange("p (c n) -> p c n", c=2), Res("pp0")),
              (PSALL[:, 3072:4096].rearrange("p (c n) -> p c n", c=2), Res("pp1"))]
    f32p = TPool(nc, "f32t", [128, 512], F32, 6)
    b16p = TPool(nc, "b16t", [128, 512], BF16, 8)
    wpool = TPool(nc, "wt", [128, 8, 512], BF16, 2, arena=AR)
    xin = TPool(nc, "xin", [128, 8, 512], F32, 2, arena=AR)
    sqp = TPool(nc, "sq", [128, 8, 512], BF16, 1, arena=AR)
    ustage = TPool(nc, "ust", [128, 32, 8, 16], BF16, 2, arena=AR)

    ps, rps = PS.get()
    P.op("pe", lambda: nc.tensor.matmul(ps[0:8, 0:384], lhsT=relb[:], rhs=onehot[:], start=True, stop=True),
         reads=[R_c], writes=[rps])
    wr_sb, r_wr_sb = f32p.get()
    P.op("act", lambda: nc.scalar.copy(out=wr_sb[0:8, 0:384], in_=ps[0:8, 0:384]), reads=[rps], writes=[r_wr_sb])
    P.dma(wr_d.ap(), wr_sb[0:8, 0:384], reads=[r_wr_sb], writes=[R_wr])
    for h in range(8):
        xt_, rxt = f32p.get()
        P.dma(xt_[:, 0:256], bass.AP(tensor=wr_d, offset=384 * h, ap=[[1, 128], [1, 256]]), reads=[R_wr], writes=[rxt])
        ps, rps = PS.get()
        P.op("pe", lambda: nc.tensor.matmul(ps[:, 0:256], lhsT=jrev_f[:], rhs=xt_[:, 0:256], start=True, stop=True),
             reads=[R_c, rxt], writes=[rps])
        et_, ret = f32p.get()
        P.op("act", lambda: nc.scalar.activation(out=et_[:, 0:256], in_=ps[:, 0:256], func=AF.Exp), reads=[rps], writes=[ret])
        P.op("dve", lambda: nc.vector.tensor_tensor(out=expB[:, h, :], in0=et_[:, 0:256], in1=bandmask[:], op=ALU.mult),
             reads=[ret, R_c], writes=[R_c])

    def rsqrt_from(ps_ap, scale, n=512, width=None):
        t, rt = f32p.get()
        return t, rt

    def norm_stage(src_dram, R_src, gcol, keep=None):
        for tb in range(NTB):
            xb, rxb = xin.get()
            P.dma(xb[:], src_dram.ap()[:, tb * TB:(tb + 1) * TB].rearrange("(kc p) t -> p kc t", p=128),
                  reads=[R_src], writes=[rxb])
            norm_block(xb, rxb, gcol, tb, sqp)

    def norm_block(xb, rxb, gcol, tb, sqpool):
        sq, rsq = sqpool.get()
        P.op("act", lambda: nc.scalar.activation(out=sq[:], in_=xb[:], func=AF.Square), reads=[rxb], writes=[rsq])
        ps, rps = PS.get()
        for kc in range(8):
            P.op("pe", lambda: nc.tensor.matmul(ps[:], lhsT=ones_b[:], rhs=sq[:, kc, :], start=(kc == 0), stop=(kc == 7)),
                 reads=[rsq, R_c], writes=[rps])
        rs, rrs = f32p.get()
        P.op("act", lambda: nc.scalar.activation(out=rs[:], in_=ps[:], func=AF.Sqrt, scale=1.0 / D, bias=epsc[:, 0:1]),
             reads=[rps, R_c], writes=[rrs])
        P.op("dve", lambda: nc.vector.reciprocal(out=rs[:], in_=rs[:]), reads=[rrs], writes=[rrs])
        for kc in range(8):
            P.op("dve", lambda: nc.vector.scalar_tensor_tensor(
                out=BIGA[:, kc, tb * TB:(tb + 1) * TB], in0=xb[:, kc, :], scalar=vec[:, gcol + kc:gcol + kc + 1],
                in1=rs[:], op0=ALU.mult, op1=ALU.mult), reads=[rxb, rrs, R_vec], writes=[R_A[kc]])

    def load_w(wdram2d, offs, KC=8):
        wt, rwt = wpool.get()
        pos = 0
        for off, n in offs:
            P.dma(wt[:, 0:KC, pos:pos + n], wdram2d[:, off:off + n].rearrange("(kc p) n -> p kc n", p=128),
                  writes=[rwt], q="pool")
            pos += n
        return wt, rwt

    def mm_fm(wt, rwt, ci, tb, KC=8, src=None, rsrc=None):
        ps, rps = PS.get()
        for kc in range(KC):
            P.op("pe", lambda: nc.tensor.matmul(ps[:], lhsT=wt[:, kc, ci * 128:(ci + 1) * 128],
                                                rhs=BIGA[:, kc, tb * TB:(tb + 1) * TB], start=(kc == 0), stop=(kc == KC - 1)),
                 reads=[rwt, R_A[kc]], writes=[rps])
        return ps, rps

    def qk_tile(wt, rwt, ci, gcol, dst, R_dst, row0, tb):
        ps, rps = mm_fm(wt, rwt, ci, tb)
        sq, rsq = b16p.get()
        P.op("act", lambda: nc.scalar.activation(out=sq[:], in_=ps[:], func=AF.Square), reads=[rps], writes=[rsq])
        yield
        ps2, rps2 = PS.get()
        P.op("pe", lambda: nc.tensor.matmul(ps2[:], lhsT=bdones_b[:], rhs=sq[:], start=True, stop=True),
             reads=[rsq, R_c], writes=[rps2])
        rs, rrs = f32p.get()
        P.op("act", lambda: nc.scalar.activation(out=rs[:], in_=ps2[:], func=AF.Sqrt, scale=1.0 / 64, bias=epsc[:, 0:1]),
             reads=[rps2, R_c], writes=[rrs])
        P.op("dve", lambda: nc.vector.reciprocal(out=rs[:], in_=rs[:]), reads=[rrs], writes=[rrs])
        o, ro = b16p.get()
        P.op("dve", lambda: nc.vector.scalar_tensor_tensor(out=o[:], in0=ps[:], scalar=vec[:, gcol:gcol + 1], in1=rs[:],
                                                           op0=ALU.mult, op1=ALU.mult), reads=[rps, rrs, R_vec], writes=[ro])
        P.dma(dst.ap()[row0:row0 + 128, tb * TB:(tb + 1) * TB], o[:], reads=[ro], writes=[R_dst])


    y_s = scratch("y_s", [S, D], BF16)
    R_ys = Res("ys")

    def s5_stage(l):
        P.barrier()
        AR.reset()
        R_sp = Res("sp")
        sp = AR.alloc([128, 4352], F32)
        P.dma(sp[:], s5p_in.ap()[l], writes=[R_sp])
        lr, li, lstep, drep = sp[:, 0:64], sp[:, 64:128], sp[:, 128:192], sp[:, 192:256]
        T1, T2, CTa, CTb = sp[:, 256:1280], sp[:, 1280:2304], sp[:, 2304:3328], sp[:, 3328:4352]
        SL = AR.alloc([128, 20, 64], F32)
        KI = AR.alloc([128, 64], I32)
        PWr = AR.alloc([128, 16, 64], F32)
        PWi = AR.alloc([128, 16, 64], F32)
        AKr = AR.alloc([128, 9, 64], F32)
        AKi = AR.alloc([128, 9, 64], F32)
        AKs = AR.alloc([128, 9, 64], F32)
        PRr = AR.alloc([128, 15, 64], F32)
        PRi = AR.alloc([128, 15, 64], F32)
        QRr = AR.alloc([128, 9, 64], F32)
        QRi = AR.alloc([128, 9, 64], F32)
        BT1 = AR.alloc([128, 1024], F32)
        BT2 = AR.alloc([128, 1024], F32)
        tA = AR.alloc([128, 1024], F32)
        tB = AR.alloc([128, 1024], F32)
        rw = dict(reads=[R_sp, R_c], writes=[R_sp])

        def tt(o, a, b, op):
            P.op("dve", lambda: nc.vector.tensor_tensor(out=o, in0=a, in1=b, op=op), **rw)

        def ts(o, a, s1, op0, s2=None, op1=None):
            if op1 is None:
                P.op("dve", lambda: nc.vector.tensor_scalar(out=o, in0=a, scalar1=s1, scalar2=None, op0=op0), **rw)
            else:
                P.op("dve", lambda: nc.vector.tensor_scalar(out=o, in0=a, scalar1=s1, scalar2=s2, op0=op0, op1=op1), **rw)

        def act(o, a, func, scale=1.0):
            P.op("act", lambda: nc.scalar.activation(out=o, in_=a, func=func, scale=scale), **rw)

        def cp(o, a):
            P.op("dve", lambda: nc.vector.tensor_copy(out=o, in_=a), **rw)

        sl = lambda i: SL[:, i, :]
        dl, mag, ang, tu, kf, r1, rr, mm_, sinv, cosv, ar, ai, den, nr, fr, fi, t1, t2, sfi, nsfi = [sl(i) for i in range(20)]
        act(dl, lstep, AF.Exp)
        tt(t1, lr, dl, ALU.mult)
        act(mag, t1, AF.Exp)
        tt(ang, li, dl, ALU.mult)
        ts(tu, ang, 1.0 / (2 * math.pi), ALU.mult)
        cp(KI[:], tu)
        cp(kf, KI[:])
        tt(r1, tu, kf, ALU.subtract)

        def wrap_sin(dst, src, shift):
            ts(rr, src, shift, ALU.add)
            ts(mm_, rr, 0.5, ALU.is_gt)
            tt(rr, rr, mm_, ALU.subtract)
            ts(mm_, rr, 0.5, ALU.is_gt)
            tt(rr, rr, mm_, ALU.subtract)
            ts(mm_, rr, -0.5, ALU.is_lt)
            tt(rr, rr, mm_, ALU.add)
            ts(mm_, rr, -0.5, ALU.is_lt)
            tt(rr, rr, mm_, ALU.add)
            act(dst, rr, AF.Sin, scale=2 * math.pi)
        wrap_sin(sinv, r1, 0.0)
        wrap_sin(cosv, r1, 0.25)
        tt(ar, mag, cosv, ALU.mult)
        tt(ai, mag, sinv, ALU.mult)
        tt(t1, lr, lr, ALU.mult)
        tt(t2, li, li, ALU.mult)
        tt(den, t1, t2, ALU.add)
        P.op("dve", lambda: nc.vector.reciprocal(out=den, in_=den), **rw)
        ts(nr, ar, -1.0, ALU.add)
        tt(t1, nr, lr, ALU.mult)
        tt(t2, ai, li, ALU.mult)
        tt(t1, t1, t2, ALU.add)
        tt(fr, t1, den, ALU.mult)
        tt(t1, ai, lr, ALU.mult)
        tt(t2, nr, li, ALU.mult)
        tt(t1, t1, t2, ALU.subtract)
        tt(fi, t1, den, ALU.mult)
        ts(sfi, fi, sig[:, 0:1], ALU.mult)
        ts(nsfi, fi, sig[:, 1:2], ALU.mult)
        g3 = lambda a: a.rearrange("p (g h) -> p g h", h=16)
        bc3 = lambda a: a.unsqueeze(2).broadcast_to([128, 64, 16])
        tt(g3(tA[:]), bc3(fr), g3(T1), ALU.mult)
        tt(g3(tB[:]), bc3(sfi), g3(T2), ALU.mult)
        tt(BT1[:], tA[:], tB[:], ALU.add)
        tt(g3(tA[:]), bc3(fr), g3(T2), ALU.mult)
        tt(g3(tB[:]), bc3(nsfi), g3(T1), ALU.mult)
        tt(BT2[:], tA[:], tB[:], ALU.add)

        def cmul(orr, oi, xr, xi, yr, yi):
            tt(t1, xr, yr, ALU.mult)
            tt(t2, xi, yi, ALU.mult)
            tt(orr, t1, t2, ALU.subtract)
            tt(t1, xr, yi, ALU.mult)
            tt(t2, xi, yr, ALU.mult)
            tt(oi, t1, t2, ALU.add)
        P.op("dve", lambda: nc.vector.memset(PWr[:, 7, :], 1.0), **rw)
        P.op("dve", lambda: nc.vector.memset(PWi[:, 7, :], 0.0), **rw)
        cp(PWr[:, 8, :], ar)
        cp(PWi[:, 8, :], ai)
        for n in range(2, 9):
            cmul(PWr[:, 7 + n, :], PWi[:, 7 + n, :], PWr[:, 6 + n, :], PWi[:, 6 + n, :], PWr[:, 8, :], PWi[:, 8, :])
        tt(t1, ar, ar, ALU.mult)
        tt(t2, ai, ai, ALU.mult)
        tt(den, t1, t2, ALU.add)
        P.op("dve", lambda: nc.vector.reciprocal(out=den, in_=den), **rw)
        tt(PWr[:, 6, :], ar, den, ALU.mult)
        tt(t1, ai, den, ALU.mult)
        ts(PWi[:, 6, :], t1, -1.0, ALU.mult)
        for n in range(2, 8):
            cmul(PWr[:, 7 - n, :], PWi[:, 7 - n, :], PWr[:, 8 - n, :], PWi[:, 8 - n, :], PWr[:, 6, :], PWi[:, 6, :])
        cp(AKr[:, 0, :], PWr[:, 15, :])
        cp(AKi[:, 0, :], PWi[:, 15, :])
        for k in range(1, 9):
            cmul(AKr[:, k, :], AKi[:, k, :], AKr[:, k - 1, :], AKi[:, k - 1, :], AKr[:, k - 1, :], AKi[:, k - 1, :])
        ts(AKs[:], AKi[:], sig[:, 1:2], ALU.mult)
        for s_ in range(15):
            cp(PRr[:, s_, :], PWr[:, 14 - s_, :])
            ts(PRi[:, s_, :], PWi[:, 14 - s_, :], sig[:, 0:1], ALU.mult)
        ts(QRr[:], PWr[:, 7:16, :], sig[:, 1:2], ALU.mult)
        ts(QRi[:], PWi[:, 7:16, :], -1.0, ALU.mult)

        off_ut = AR.off
        utok = AR.alloc([128, 8192], BF16)
        R_ut = Res("utok")
        PSr = SubPool(PS.t[4:8])
        for cb in range(4):
            P.dma(utok[:], u_s.ap()[cb * 128:(cb + 1) * 128, :], reads=[R_u], writes=[R_ut])
            for g8 in range(8):
                ps, rps = PSr.get()
                psb = ps.bitcast(BF16)
                for gg in range(8):
                    g = g8 * 8 + gg
                    P.op("pe", lambda: nc.tensor.transpose(psb[:, gg * 128:(gg + 1) * 128], utok[:, g * 128:(g + 1) * 128], ident_b[:]),
                         reads=[R_ut, R_c], writes=[rps])
                dst = BIGA[:, g8, :].rearrange("p (g c) -> p g c", c=512)[:, :, cb * 128:(cb + 1) * 128]
                src = psb[:, :].rearrange("p (g c) -> p g c", c=128)
                if g8 % 2 == 0:
                    P.op("act", lambda: nc.scalar.copy(out=dst, in_=src), reads=[rps], writes=[R_A[g8]])
                else:
                    P.op("dve", lambda: nc.vector.tensor_copy(out=dst, in_=src), reads=[rps], writes=[R_A[g8]])

        P.barrier()
        AR.off = off_ut
        pfp = TPool(nc, "pf", [128, 4, 240], BF16, 2, arena=AR)
        qfp = TPool(nc, "qf", [128, 4, 144], BF16, 2, arena=AR)
        tP1 = AR.alloc([128, 4, 15, 16], F32)
        tP2 = AR.alloc([128, 4, 15, 16], F32)
        rkp = TPool(nc, "rk", [128, 9, 128], BF16, 4, arena=AR)
        minp = TPool(nc, "min", [128, 128], BF16, 4, arena=AR)
        PSm = SubPool(PS.t[4:8])
        mintp = TPool(nc, "mint", [128, 128], BF16, 4, arena=AR)
        xzp = TPool(nc, "xz", [128, 514], BF16, 5, arena=AR)
        ystp = TPool(nc, "yst", [128, 8, 64], BF16, 2, arena=AR)
        for xz_, rxz_ in xzp.t:
            P.op("pool", lambda: nc.gpsimd.memset(xz_[:, 0:1], 0.0), writes=[rxz_])
        psY = PS.t[0:4]
        BT1g, BT2g, CTag, CTbg = g3(BT1[:]), g3(BT2[:]), g3(CTa), g3(CTb)
        for bi in range(16):
            g0 = bi * 4
            pf, rpf = pfp.get()
            qf, rqf = qfp.get()
            pf4 = pf[:].rearrange("p g (b h) -> p g b h", h=16)
            qf4 = qf[:].rearrange("p g (b h) -> p g b h", h=16)
            e_ = lambda tab, nb: tab[:, :, g0:g0 + 4].rearrange("p b g -> p g b").unsqueeze(3).broadcast_to([128, 4, nb, 16])
            b_ = lambda tab, nb: tab[:, g0:g0 + 4, :].unsqueeze(2).broadcast_to([128, 4, nb, 16])
            rwp = dict(reads=[R_sp], writes=[R_sp])
            P.op("dve", lambda: nc.vector.tensor_tensor(out=tP1[:], in0=e_(PRr, 15), in1=b_(BT1g, 15), op=ALU.mult), **rwp)
            P.op("dve", lambda: nc.vector.tensor_tensor(out=tP2[:], in0=e_(PRi, 15), in1=b_(BT2g, 15), op=ALU.mult), **rwp)
            P.op("dve", lambda: nc.vector.tensor_tensor(out=pf4, in0=tP1[:], in1=tP2[:], op=ALU.add), reads=[R_sp], writes=[R_sp, rpf])
            P.op("dve", lambda: nc.vector.tensor_tensor(out=tP1[:, :, 0:9, :], in0=e_(QRr, 9), in1=b_(CTag, 9), op=ALU.mult), **rwp)
            P.op("dve", lambda: nc.vector.tensor_tensor(out=tP2[:, :, 0:9, :], in0=e_(QRi, 9), in1=b_(CTbg, 9), op=ALU.mult), **rwp)
            P.op("dve", lambda: nc.vector.tensor_tensor(out=qf4, in0=tP1[:, :, 0:9, :], in1=tP2[:, :, 0:9, :], op=ALU.add),
                 reads=[R_sp], writes=[R_sp, rqf])
            def grp_gen(gl):
                g = g0 + gl
                U_g = BIGA[:, g // 8, (g % 8) * 512:(g % 8 + 1) * 512]
                R_U = R_A[g // 8]
                rk, rrk = rkp.get()
                for k in range(9):
                    tf, rtf = f32p.get()
                    P.op("dve", lambda: nc.vector.tensor_scalar(out=tf[:, 0:128], in0=ident_f[:], scalar1=AKr[:, k, g:g + 1], scalar2=None,
                                                                op0=ALU.mult), reads=[R_sp, R_c], writes=[rtf])
                    P.op("dve", lambda: nc.vector.scalar_tensor_tensor(out=rk[:, k, :], in0=jswap_f[:], scalar=AKs[:, k, g:g + 1],
                                                                       in1=tf[:, 0:128], op0=ALU.mult, op1=ALU.add),
                         reads=[R_sp, R_c, rtf], writes=[rrk])
                ps, rps = PSr.get()
                psb = ps.bitcast(BF16)
                P.op("pe", lambda: nc.tensor.transpose(psb[:, 0:128], pf[:, gl, 0:128], ident_b[:]), reads=[rpf, R_c], writes=[rps])
                mi, rmi = minp.get()
                P.op("act", lambda: nc.scalar.copy(out=mi[:], in_=psb[:, 0:128]), reads=[rps], writes=[rmi])
                ps, rps = PSm.get()
                P.op("pe", lambda: nc.tensor.matmul(ps[:, 0:128], lhsT=pf[:, gl, 112:240], rhs=qf[:, gl, 0:128], start=True, stop=True),
                     reads=[rpf, rqf], writes=[rps])
                tf, rtf = f32p.get()
                P.op("dve", lambda: nc.vector.tensor_tensor(out=tf[:, 0:128], in0=ps[:, 0:128], in1=mask_intra[:], op=ALU.mult),
                     reads=[rps, R_c], writes=[rtf])
                mt, rmt = mintp.get()
                P.op("dve", lambda: nc.vector.scalar_tensor_tensor(out=mt[:], in0=ident_f[:], scalar=drep[:, g:g + 1], in1=tf[:, 0:128],
                                                                   op0=ALU.mult, op1=ALU.add), reads=[R_sp, R_c, rtf], writes=[rmt])
                yield
                xz, rxz = xzp.get()
                ps, rps = PSr.get()
                P.op("pe", lambda: nc.tensor.matmul(ps[:], lhsT=mi[:], rhs=U_g, start=True, stop=True), reads=[rmi, R_U], writes=[rps])
                P.op("act", lambda: nc.scalar.copy(out=xz[:, 1:513], in_=ps[:]), reads=[rps], writes=[rxz])
                for k in range(9):
                    yield
                    sh = 1 << k
                    ps, rps = PSr.get()
                    P.op("pe", lambda: nc.tensor.matmul(ps[:, 0:512 - sh], lhsT=rk[:, k, :], rhs=xz[:, 1:513 - sh], start=True, stop=True),
                         reads=[rrk, rxz], writes=[rps])
                    P.op("dve", lambda: nc.vector.tensor_tensor(out=xz[:, 1 + sh:513], in0=ps[:, 0:512 - sh], in1=xz[:, 1 + sh:513],
                                                                op=ALU.add), reads=[rps], writes=[rxz])
                yield
                for cb in range(4):
                    P.op("pe", lambda: nc.tensor.matmul(psY[cb][0][:, gl * 128:(gl + 1) * 128], lhsT=U_g[:, cb * 128:(cb + 1) * 128], rhs=mt[:],
                                                        start=True, stop=False), reads=[R_U, rmt], writes=[psY[cb][1]])
                    P.op("pe", lambda: nc.tensor.matmul(psY[cb][0][:, gl * 128:(gl + 1) * 128], lhsT=xz[:, cb * 128:(cb + 1) * 128],
                                                        rhs=qf[:, gl, 16:144], start=False, stop=True), reads=[rxz, rqf], writes=[psY[cb][1]])
            run_pipe(grp_gen(gl) for gl in range(4))
            for cb in range(4):
                yst, ryst = ystp.get()
                if "s5raw" in dbg:
                    P.op("act", lambda: nc.scalar.copy(out=yst[:].rearrange("p j (g h) -> p g j h", h=16),
                                                       in_=psY[cb][0][:].rearrange("p (g j h) -> p g j h", g=4, j=8)),
                         reads=[psY[cb][1]], writes=[ryst])
                else:
                    P.op("act", lambda: nc.scalar.activation(out=yst[:].rearrange("p j (g h) -> p g j h", h=16),
                                                             in_=psY[cb][0][:].rearrange("p (g j h) -> p g j h", g=4, j=8),
                                                             func=AF.Gelu_apprx_tanh), reads=[psY[cb][1]], writes=[ryst])
                P.dma(y_s.ap()[1024 * cb:1024 * (cb + 1), 64 * bi:64 * bi + 64].rearrange("(c j) n -> c j n", j=8), yst[:],
                      reads=[ryst], writes=[R_ys])

        P.barrier()
        AR.reset()
        ytp = TPool(nc, "ytk", [128, 1024], BF16, 2, arena=AR)
        for tt_ in range(32):
            yt, ryt = ytp.get()
            P.dma(yt[:], y_s.ap()[tt_ * 128:(tt_ + 1) * 128, :], reads=[R_ys], writes=[ryt])
            ps, rps = PSr.get()
            psb = ps.bitcast(BF16)
            for kc in range(8):
                P.op("pe", lambda: nc.tensor.transpose(psb[:, kc * 128:(kc + 1) * 128], yt[:, kc * 128:(kc + 1) * 128], ident_b[:]),
                     reads=[ryt, R_c], writes=[rps])
            dst = BIGA[:, :, tt_ * 128:(tt_ + 1) * 128]
            src = psb[:, :].rearrange("p (k c) -> p k c", c=128)
            if tt_ % 2 == 0:
                P.op("act", lambda: nc.scalar.copy(out=dst, in_=src), reads=[rps], writes=R_A)
            else:
                P.op("dve", lambda: nc.vector.tensor_copy(out=dst, in_=src), reads=[rps], writes=R_A)
        wp4 = TPool(nc, "wt4", [128, 8, 512], BF16, 4, arena=AR)
        for grp in range(2):
            w1t, rw1 = wp4.get()
            P.dma(w1t[:], W["s5_glu_w1"].ap()[l][:, grp * 512:(grp + 1) * 512].rearrange("(kc p) n -> p kc n", p=128), writes=[rw1], q="pool")
            w2t, rw2 = wp4.get()
            P.dma(w2t[:], W["s5_glu_w2"].ap()[l][:, grp * 512:(grp + 1) * 512].rearrange("(kc p) n -> p kc n", p=128), writes=[rw2], q="pool")
            for ci in range(4):
                for tb in range(NTB):
                    psa, rpsa = mm_fm(w1t, rw1, ci, tb)
                    psb_, rpsb = mm_fm(w2t, rw2, ci, tb)
                    sg_, rsg = f32p.get()
                    P.op("act", lambda: nc.scalar.activation(out=sg_[:], in_=psb_[:], func=AF.Sigmoid), reads=[rpsb], writes=[rsg])
                    o, ro = b16p.get()
                    P.op("dve", lambda: nc.vector.tensor_tensor(out=o[:], in0=psa[:], in1=sg_[:], op=ALU.mult), reads=[rpsa, rsg], writes=[ro])
                    r0_ = D + (grp * 4 + ci) * 128
                    P.dma(br_s.ap()[r0_:r0_ + 128, tb * TB:(tb + 1) * TB], o[:], reads=[ro], writes=[R_br])

    for l in range(nlayers):
        x_src, R_xsrc = (xT_in, Res("xin")) if l == 0 else (xmidT_s, R_xmid)
        x_dst, R_xdst = (yT_out, Res("yout")) if l == nlayers - 1 else (xmidT_s, R_xmid)
        P.barrier()
        P.dma(vec[:], vecs_in.ap()[l], writes=[R_vec])
        P.dma(lam4[:], lamv_in.ap()[l], writes=[R_lam])
        lam_init = 0.8 - 0.6 * math.exp(-0.3 * l)
        lt, rlt = f32p.get()
        P.op("dve", lambda: nc.vector.tensor_tensor(out=lt[:, 0:64], in0=lam4[:, 0:64], in1=lam4[:, 64:128], op=ALU.mult),
             reads=[R_lam], writes=[rlt])
        P.op("dve", lambda: nc.vector.tensor_tensor(out=lt[:, 64:128], in0=lam4[:, 128:192], in1=lam4[:, 192:256], op=ALU.mult),
             reads=[R_lam], writes=[rlt])
        P.op("dve", lambda: nc.vector.reduce_sum(out=lt[:, 128:130], in_=lt[:, 0:128].rearrange("p (a b) -> p a b", a=2),
                                                 axis=AX.X), reads=[rlt], writes=[rlt])
        P.op("act", lambda: nc.scalar.activation(out=lt[:, 130:132], in_=lt[:, 128:130], func=AF.Exp), reads=[rlt], writes=[rlt])
        P.op("dve", lambda: nc.vector.scalar_tensor_tensor(out=neglam[:], in0=lt[:, 131:132], scalar=-lam_init, in1=lt[:, 130:131],
                                                           op0=ALU.add, op1=ALU.subtract), reads=[rlt], writes=[R_lam])
        P.op("dve", lambda: nc.vector.tensor_scalar(out=subsc[:], in0=vec[:, V_SUB:V_SUB + 1], scalar1=(1.0 - lam_init), scalar2=None,
                                                    op0=ALU.mult), reads=[R_vec], writes=[R_lam])

        norm_stage(x_src, R_xsrc, V_GMIX)

        w_in2 = W["w_in"].ap()[l]
        for seg, (off0, gcol, dst, R_dst) in enumerate([(OFF_Q, V_QG, qT_s, R_qT), (OFF_K, V_KG, kT_s, R_kT)]):
            for grp in range(2):
                wt, rwt = load_w(w_in2, [(off0 + grp * 512, 512)])
                run_pipe(qk_tile(wt, rwt, ci, gcol, dst, R_dst, (grp * 4 + ci) * 128, tb) for ci in range(4) for tb in range(NTB))
        for grp in range(2):
            wt, rwt = load_w(w_in2, [(OFF_V + grp * 512, 512)])
            for tt in range(32):
                ps, rps = PS.get()
                for kc in range(8):
                    P.op("pe", lambda: nc.tensor.matmul(ps[:], lhsT=BIGA[:, kc, tt * 128:(tt + 1) * 128], rhs=wt[:, kc, :],
                                                        start=(kc == 0), stop=(kc == 7)), reads=[rwt, R_A[kc]], writes=[rps])
                o, ro = b16p.get()
                if tt % 2 == 0:
                    P.op("act", lambda: nc.scalar.copy(out=o[:], in_=ps[:]), reads=[rps], writes=[ro])
                else:
                    P.op("dve", lambda: nc.vector.tensor_copy(out=o[:], in_=ps[:]), reads=[rps], writes=[ro])
                P.dma(v_s.ap()[tt * 128:(tt + 1) * 128, grp * 512:(grp + 1) * 512], o[:], reads=[ro], writes=[R_v])
        for grp in range(2):
            wt, rwt = load_w(w_in2, [(OFF_U + grp * 512, 512)])
            for cb in range(4):
                us, rus = ustage.get()
                for j in range(8):
                    ps, rps = PS.get()
                    for kc in range(8):
                        P.op("pe", lambda: nc.tensor.matmul(ps[:], lhsT=BIGA[:, kc, 1024 * cb + j:1024 * (cb + 1):8], rhs=wt[:, kc, :],
                                                            start=(kc == 0), stop=(kc == 7)), reads=[rwt, R_A[kc]], writes=[rps])
                    src_v = ps[:].rearrange("p (g h) -> p g h", h=16)
                    if j % 2 == 0:
                        P.op("act", lambda: nc.scalar.copy(out=us[:, :, j, :], in_=src_v), reads=[rps], writes=[rus])
                    else:
                        P.op("dve", lambda: nc.vector.tensor_copy(out=us[:, :, j, :], in_=src_v), reads=[rps], writes=[rus])
                P.dma(u_s.ap()[cb * 128:(cb + 1) * 128, grp * 4096:(grp + 1) * 4096], us[:].rearrange("p g j h -> p (g j h)"),
                      reads=[rus], writes=[R_u])
        for grp in range(4):
            wt, rwt = load_w(w_in2, [(OFF_C + grp * 256, 256), (OFF_C + 1024 + grp * 256, 256)])
            for ci in range(2):
                for tb in range(NTB):
                    psa, rpsa = mm_fm(wt, rwt, ci, tb)
                    psb, rpsb = mm_fm(wt, rwt, ci + 2, tb)
                    sg_, rsg = f32p.get()
                    P.op("act", lambda: nc.scalar.activation(out=sg_[:], in_=psb[:], func=AF.Sigmoid), reads=[rpsb], writes=[rsg])
                    o, ro = b16p.get()
                    P.op("dve", lambda: nc.vector.tensor_tensor(out=o[:], in0=psa[:], in1=sg_[:], op=ALU.mult),
                         reads=[rpsa, rsg], writes=[ro])
                    r0 = (grp * 2 + ci) * 128
                    P.dma(cT_s.ap()[r0:r0 + 128, tb * TB:(tb + 1) * TB], o[:], reads=[ro], writes=[R_cT])
        for grp in range(6):
            wt, rwt = load_w(w_in2, [(OFF_G + grp * 512, 512)])
            for ci in range(4):
                for tb in range(NTB):
                    ps, rps = mm_fm(wt, rwt, ci, tb)
                    o, ro = b16p.get()
                    P.op("act", lambda: nc.scalar.activation(out=o[:], in_=ps[:], func=AF.Sigmoid), reads=[rps], writes=[ro])
                    r0 = (grp * 4 + ci) * 128
                    P.dma(gT_s.ap()[r0:r0 + 128, tb * TB:(tb + 1) * TB], o[:], reads=[ro], writes=[R_gT])
        if "stop_s2" in dbg:
            break

        P.barrier()
        AR.reset()
        kpool = TPool(nc, "kT", [128, S], BF16, 2, arena=AR)
        qpool = TPool(nc, "qT", [128, S], BF16, 2, arena=AR)
        vpool = TPool(nc, "vh", [128, 32, 128], BF16, 2, arena=AR)
        epool = TPool(nc, "eT", [128, 512], BF16, 3, arena=AR)
        epool2 = TPool(nc, "eT2", [128, 2, 512], BF16, 6, arena=AR)
        PSp = SubPool(PSPAIR)
        o32 = TPool(nc, "o32", [128, 512], F32, 9, arena=AR)
        psO = [PS.t[0], PS.t[1]]
        psS = [PS.t[2], PS.t[3]]
        PSs = SubPool(PS.t[4:8])
        hbuf = {}

        def att_load(h):
            kt, rkt = kpool.get()
            P.dma(kt[:], kT_s.ap()[h * 128:(h + 1) * 128, :], reads=[R_kT], writes=[rkt])
            qt, rqt = qpool.get()
            P.dma(qt[:], qT_s.ap()[h * 128:(h + 1) * 128, :], reads=[R_qT], writes=[rqt])
            vt, rvt = vpool.get()
            P.dma(vt[:], v_s.ap()[:, h * 128:(h + 1) * 128].rearrange("(j p) e -> p j e", p=128), reads=[R_v], writes=[rvt])
            hbuf[h] = (kt, rkt, qt, rqt, vt, rvt)
            return
            yield

        def att_unit(h, qb, j):
            kt, rkt, qt, rqt, vt, rvt = hbuf[h]
            q0 = qb * TB
            nj = 4 * qb + 4
            lo = max(0, 128 * j - q0)
            pp, rpp = PSp.get()
            for c in range(2):
                P.op("pe", lambda: nc.tensor.matmul(pp[:, c, lo:512], lhsT=kt[64 * c:64 * c + 64, 128 * j:128 * j + 128],
                                                    rhs=qt[64 * c:64 * c + 64, q0 + lo:q0 + 512], start=True, stop=True),
                     reads=[rkt, rqt], writes=[rpp])
            et, ret = epool2.get()
            P.op("act", lambda: nc.scalar.activation(out=et[:, :, lo:512], in_=pp[:, :, lo:512], func=AF.Exp, scale=0.125),
                 reads=[rpp], writes=[ret])
            if j >= 4 * qb:
                a = 128 * (j - 4 * qb)
                b = min(a + 256, 512)
                ba = 0
            elif j == 4 * qb - 1:
                a, b, ba = 0, 128, 128
            else:
                a = None
            if a is not None:
                P.op("dve", lambda: nc.vector.tensor_tensor(out=et[:, :, a:b], in0=et[:, :, a:b],
                                                            in1=expB[:, h, ba:ba + (b - a)].unsqueeze(1).broadcast_to([128, 2, b - a]),
                                                            op=ALU.mult), reads=[ret, R_c], writes=[ret])
            yield
            yield
            for c in range(2):
                P.op("pe", lambda: nc.tensor.matmul(psO[c][0][:, lo:512], lhsT=vt[:, j, :], rhs=et[:, c, lo:512],
                                                    start=(j == 0), stop=(j == nj - 1)), reads=[rvt, ret], writes=[psO[c][1]])
                P.op("pe", lambda: nc.tensor.matmul(psS[c][0][:, lo:512], lhsT=ones_b[:], rhs=et[:, c, lo:512],
                                                    start=(j == 0), stop=(j == nj - 1)), reads=[ret, R_c], writes=[psS[c][1]])

        def att_final(h, qb):
            q0 = qb * TB
            yield
            r0, rr0 = o32.get()
            t0, rt0 = o32.get()
            t1, rt1 = o32.get()
            P.op("dve", lambda: nc.vector.reciprocal(out=r0[:], in_=psS[0][0][:]), reads=[psS[0][1]], writes=[rr0])
            P.op("dve", lambda: nc.vector.tensor_tensor(out=t0[:], in0=psO[0][0][:], in1=r0[:], op=ALU.mult),
                 reads=[psO[0][1], rr0], writes=[rt0])
            P.op("dve", lambda: nc.vector.reciprocal(out=t1[:], in_=psS[1][0][:]), reads=[psS[1][1]], writes=[rt1])
            P.op("dve", lambda: nc.vector.tensor_tensor(out=t1[:], in0=psO[1][0][:], in1=t1[:], op=ALU.mult),
                 reads=[psO[1][1]], writes=[rt1])
            P.op("dve", lambda: nc.vector.scalar_tensor_tensor(out=t0[:], in0=t1[:], scalar=neglam[:, 0:1], in1=t0[:],
                                                               op0=ALU.mult, op1=ALU.add), reads=[rt1, R_lam], writes=[rt0])
            sq, rsq = epool.get()
            P.op("act", lambda: nc.scalar.activation(out=sq[:], in_=t0[:], func=AF.Square), reads=[rt0], writes=[rsq])
            yield
            yield
            pp_, rpss = PSp.get()
            pss = pp_[:, 0, :]
            P.op("pe", lambda: nc.tensor.matmul(pss, lhsT=ones_b[:], rhs=sq[:], start=True, stop=True),
                 reads=[rsq, R_c], writes=[rpss])
            P.op("act", lambda: nc.scalar.activation(out=r0[:], in_=pss, func=AF.Sqrt, scale=1.0 / 128, bias=epsc[:, 0:1]),
                 reads=[rpss, R_c], writes=[rr0])
            P.op("dve", lambda: nc.vector.reciprocal(out=r0[:], in_=r0[:]), reads=[rr0], writes=[rr0])
            P.op("dve", lambda: nc.vector.scalar_tensor_tensor(out=BIGA[:, h, q0:q0 + TB], in0=t0[:], scalar=subsc[:, 0:1], in1=r0[:],
                                                               op0=ALU.mult, op1=ALU.mult), reads=[rt0, rr0, R_lam], writes=[R_A[h]])

        def att_gens():
            yield att_load(0)
            for h in range(8):
                for qb in range(NTB):
                    for j in range(4 * qb + 4):
                        yield att_unit(h, qb, j)
                    yield att_final(h, qb)
                    if qb == 1 and h + 1 < 8:
                        yield att_load(h + 1)
        run_pipe(att_gens())

        wpool3 = TPool(nc, "wt3", [128, 8, 512], BF16, 2, arena=AR)

        def proj_to_br(wname, row_base, wp):
            for grp in range(2):
                wt, rwt = wp.get()
                P.dma(wt[:], W[wname].ap()[l][:, grp * 512:(grp + 1) * 512].rearrange("(kc p) n -> p kc n", p=128),
                      writes=[rwt], q="pool")
                for ci in range(4):
                    for tb in range(NTB):
                        ps, rps = mm_fm(wt, rwt, ci, tb)
                        o, ro = b16p.get()
                        if tb % 2 == 0:
                            P.op("act", lambda: nc.scalar.copy(out=o[:], in_=ps[:]), reads=[rps], writes=[ro])
                        else:
                            P.op("dve", lambda: nc.vector.tensor_copy(out=o[:], in_=ps[:]), reads=[rps], writes=[ro])
                        r0_ = row_base + (grp * 4 + ci) * 128
                        P.dma(br_s.ap()[r0_:r0_ + 128, tb * TB:(tb + 1) * TB], o[:], reads=[ro], writes=[R_br])
        proj_to_br("w_attn_out", 0, wpool3)

        s5_stage(l)

        P.barrier()
        AR.reset()
        diag = BIGA[:, :, 0:31 * 128].rearrange("p a (k c) -> p a k c", c=128)
        R_diag = Res("diag")
        for kc in range(8):
            for k in range(31):
                P.op("dve", lambda: nc.vector.tensor_scalar(out=diag[:, kc, k, :], in0=ident_f[:],
                                                            scalar1=vec[:, V_CW + kc * 31 + k:V_CW + kc * 31 + k + 1], scalar2=None,
                                                            op0=ALU.mult), reads=[R_vec, R_c], writes=[R_diag])
        wc = AR.alloc([128, 8, 1024], BF16)
        R_wc = Res("wc")
        P.dma(wc[:], W["conv_w_out"].ap()[l].rearrange("(kc p) n -> p kc n", p=128), writes=[R_wc], q="pool")
        cin = TPool(nc, "cin", [128, 544], BF16, 3, arena=AR)
        cvp = TPool(nc, "cv", [128, 8, 512], F32, 2, arena=AR)
        xbp = TPool(nc, "xb", [128, 8, 512], BF16, 2, arena=AR)
        sqp5 = TPool(nc, "sq5", [128, 8, 512], BF16, 2, arena=AR)
        ynp = TPool(nc, "yn", [128, 8, 512], BF16, 2, arena=AR)
        def conv_tb(tb):
            cv, rcv = cvp.get()
            xb, rxb = xbp.get()
            sq, rsq = sqp5.get()
            for kc in range(8):
                ct, rct = cin.get()
                if tb == 0:
                    P.op("pool", lambda: nc.gpsimd.memset(ct[:, 0:30], 0.0), writes=[rct])
                    P.dma(ct[:, 30:542], cT_s.ap()[kc * 128:(kc + 1) * 128, 0:512], reads=[R_cT], writes=[rct])
                else:
                    P.dma(ct[:, 0:542], cT_s.ap()[kc * 128:(kc + 1) * 128, tb * TB - 30:tb * TB + 512], reads=[R_cT], writes=[rct])
                ps, rps = PS.get()
                for k in range(31):
                    P.op("pe", lambda: nc.tensor.matmul(ps[:], lhsT=diag[:, kc, k, :], rhs=ct[:, k:k + 512], start=(k == 0), stop=(k == 30)),
                         reads=[R_diag, rct], writes=[rps])
                P.op("act", lambda: nc.scalar.activation(out=cv[:, kc, :], in_=ps[:], func=AF.Identity, bias=vec[:, V_CB + kc:V_CB + kc + 1]),
                     reads=[rps, R_vec], writes=[rcv])
                P.op("dve", lambda: nc.vector.tensor_copy(out=xb[:, kc, :], in_=cv[:, kc, :]), reads=[rcv], writes=[rxb])
                P.op("act", lambda: nc.scalar.activation(out=sq[:, kc, :], in_=cv[:, kc, :], func=AF.Square), reads=[rcv], writes=[rsq])
            yield
            ps1, rps1 = PS.get()
            ps2, rps2 = PS.get()
            for kc in range(8):
                P.op("pe", lambda: nc.tensor.matmul(ps1[:], lhsT=ones_b[:], rhs=xb[:, kc, :], start=(kc == 0), stop=(kc == 7)),
                     reads=[rxb, R_c], writes=[rps1])
            for kc in range(8):
                P.op("pe", lambda: nc.tensor.matmul(ps2[:], lhsT=ones_b[:], rhs=sq[:, kc, :], start=(kc == 0), stop=(kc == 7)),
                     reads=[rsq, R_c], writes=[rps2])
            mean, rmean = f32p.get()
            msq, rmsq = f32p.get()
            rs, rrs = f32p.get()
            P.op("act", lambda: nc.scalar.mul(out=mean[:], in_=ps1[:], mul=1.0 / D), reads=[rps1], writes=[rmean])
            P.op("dve", lambda: nc.vector.tensor_tensor(out=msq[:], in0=mean[:], in1=mean[:], op=ALU.mult), reads=[rmean], writes=[rmsq])
            P.op("dve", lambda: nc.vector.scalar_tensor_tensor(out=rs[:], in0=ps2[:], scalar=1.0 / D, in1=msq[:], op0=ALU.mult,
                                                               op1=ALU.subtract), reads=[rps2, rmsq], writes=[rrs])
            P.op("act", lambda: nc.scalar.activation(out=rs[:], in_=rs[:], func=AF.Sqrt, bias=epsc[:, 0:1]), reads=[rrs, R_c], writes=[rrs])
            P.op("dve", lambda: nc.vector.reciprocal(out=rs[:], in_=rs[:]), reads=[rrs], writes=[rrs])
            yn, ryn = ynp.get()
            for kc in range(8):
                P.op("dve", lambda: nc.vector.tensor_tensor(out=cv[:, kc, :], in0=cv[:, kc, :], in1=mean[:], op=ALU.subtract),
                     reads=[rmean], writes=[rcv])
                P.op("dve", lambda: nc.vector.tensor_tensor(out=cv[:, kc, :], in0=cv[:, kc, :], in1=rs[:], op=ALU.mult),
                     reads=[rrs], writes=[rcv])
                P.op("act", lambda: nc.scalar.activation(out=yn[:, kc, :], in_=cv[:, kc, :], func=AF.Silu,
                                                         scale=vec[:, V_LNG + kc:V_LNG + kc + 1], bias=vec[:, V_LNB + kc:V_LNB + kc + 1]),
                     reads=[rcv, R_vec], writes=[ryn])
            yield
            for co in range(8):
                ps, rps = PS.get()
                for kc in range(8):
                    P.op("pe", lambda: nc.tensor.matmul(ps[:], lhsT=wc[:, kc, co * 128:(co + 1) * 128], rhs=yn[:, kc, :],
                                                        start=(kc == 0), stop=(kc == 7)), reads=[R_wc, ryn], writes=[rps])
                o, ro = b16p.get()
                if co % 2 == 0:
                    P.op("act", lambda: nc.scalar.copy(out=o[:], in_=ps[:]), reads=[rps], writes=[ro])
                else:
                    P.op("dve", lambda: nc.vector.tensor_copy(out=o[:], in_=ps[:]), reads=[rps], writes=[ro])
                r0_ = 2 * D + co * 128
                P.dma(br_s.ap()[r0_:r0_ + 128, tb * TB:(tb + 1) * TB], o[:], reads=[ro], writes=[R_br])

        run_pipe(conv_tb(tb) for tb in range(NTB))

        P.barrier()
        AR.reset()
        wo = AR.alloc([128, 8, 1024], BF16)
        R_wo = Res("wo")
        P.dma(wo[:], W["w_out"].ap()[l].rearrange("(kc p) n -> p kc n", p=128), writes=[R_wo], q="pool")
        inp6 = TPool(nc, "in6", [128, 512], BF16, 12, arena=AR)
        mixp = TPool(nc, "mix", [128, 8, 512], BF16, 2, arena=AR)
        xin6 = TPool(nc, "xin6", [128, 8, 512], F32, 1, arena=AR)
        x1p = TPool(nc, "x1p", [128, 8, 512], F32, 2, arena=AR)
        sqp6 = TPool(nc, "sq6", [128, 8, 512], BF16, 1, arena=AR)
        for tb in range(NTB):
            mix, rmix = mixp.get()
            for kc in range(8):
                tl = []
                for i in range(3):
                    bt, rbt = inp6.get()
                    P.dma(bt[:], br_s.ap()[i * D + kc * 128:i * D + (kc + 1) * 128, tb * TB:(tb + 1) * TB], reads=[R_br], writes=[rbt])
                    gt, rgt = inp6.get()
                    P.dma(gt[:], gT_s.ap()[i * D + kc * 128:i * D + (kc + 1) * 128, tb * TB:(tb + 1) * TB], reads=[R_gT], writes=[rgt])
                    tl.append((bt, rbt, gt, rgt))
                ta, rta = f32p.get()
                tb2, rtb2 = f32p.get()
                P.op("pool", lambda: nc.gpsimd.tensor_tensor(out=ta[:], in0=tl[0][0][:], in1=tl[0][2][:], op=ALU.mult),
                     reads=[tl[0][1], tl[0][3]], writes=[rta])
                P.op("pool", lambda: nc.gpsimd.tensor_tensor(out=tb2[:], in0=tl[1][0][:], in1=tl[1][2][:], op=ALU.mult),
                     reads=[tl[1][1], tl[1][3]], writes=[rtb2])
                P.op("dve", lambda: nc.vector.tensor_tensor(out=ta[:], in0=ta[:], in1=tb2[:], op=ALU.add), reads=[rtb2], writes=[rta])
                P.op("pool", lambda: nc.gpsimd.tensor_tensor(out=tb2[:], in0=tl[2][0][:], in1=tl[2][2][:], op=ALU.mult),
                     reads=[tl[2][1], tl[2][3]], writes=[rtb2])
                P.op("dve", lambda: nc.vector.tensor_tensor(out=mix[:, kc, :], in0=ta[:], in1=tb2[:], op=ALU.add),
                     reads=[rta, rtb2], writes=[rmix])
            xb, rxb = xin6.get()
            P.dma(xb[:], x_src.ap()[:, tb * TB:(tb + 1) * TB].rearrange("(kc p) t -> p kc t", p=128), reads=[R_xsrc], writes=[rxb])
            x1, rx1 = x1p.get()
            for co in range(8):
                ps, rps = PS.get()
                for kc in range(8):
                    P.op("pe", lambda: nc.tensor.matmul(ps[:], lhsT=wo[:, kc, co * 128:(co + 1) * 128], rhs=mix[:, kc, :],
                                                        start=(kc == 0), stop=(kc == 7)), reads=[R_wo, rmix], writes=[rps])
                P.op("dve", lambda: nc.vector.tensor_tensor(out=x1[:, co, :], in0=ps[:], in1=xb[:, co, :], op=ALU.add),
                     reads=[rps, rxb], writes=[rx1])
            P.dma(x1T_s.ap()[:, tb * TB:(tb + 1) * TB].rearrange("(kc p) t -> p kc t", p=128), x1[:], reads=[rx1], writes=[R_x1])
            norm_block(x1, rx1, V_GFFN, tb, sqp6)

        P.barrier()
        AR.reset()
        wpool8 = TPool(nc, "wt8", [128, 8, 512], BF16, 2, arena=AR)
        ufull = TPool(nc, "uf", [128, 2 + S], BF16, 8, arena=AR)
        dgp = TPool(nc, "dg", [128, 4, 3, 128], BF16, 2, arena=AR)
        w_up2 = W["ffn_w_up"].ap()[l]
        for grp in range(11):
            wt, rwt = wpool8.get()
            P.dma(wt[:, :, 0:256], w_up2[:, grp * 256:(grp + 1) * 256].rearrange("(kc p) n -> p kc n", p=128), writes=[rwt], q="pool")
            P.dma(wt[:, :, 256:512], w_up2[:, FFN_H + grp * 256:FFN_H + (grp + 1) * 256].rearrange("(kc p) n -> p kc n", p=128),
                  writes=[rwt], q="pool")
            dgt, rdg = dgp.get()
            for ci in range(4):
                gch = (2 * grp + ci) if ci < 2 else (22 + 2 * grp + ci - 2)
                for k in range(3):
                    P.op("pool", lambda: nc.gpsimd.tensor_scalar(out=dgt[:, ci, k, :], in0=ident_f[:],
                                                                 scalar1=vec[:, V_FW + gch * 3 + k:V_FW + gch * 3 + k + 1], scalar2=None,
                                                                 op0=ALU.mult), reads=[R_vec, R_c], writes=[rdg])
            ufs = [ufull.get() for _ in range(4)]
            for ci in range(4):
                P.op("pool", lambda: nc.gpsimd.memset(ufs[ci][0][:, 0:2], 0.0), writes=[ufs[ci][1]])
                for tb in range(NTB):
                    ps, rps = mm_fm(wt, rwt, ci, tb)
                    if tb % 2 == 0:
                        P.op("act", lambda: nc.scalar.copy(out=ufs[ci][0][:, 2 + tb * TB:2 + (tb + 1) * TB], in_=ps[:]),
                             reads=[rps], writes=[ufs[ci][1]])
                    else:
                        P.op("dve", lambda: nc.vector.tensor_copy(out=ufs[ci][0][:, 2 + tb * TB:2 + (tb + 1) * TB], in_=ps[:]),
                             reads=[rps], writes=[ufs[ci][1]])
            for pi in range(2):
                for tb in range(NTB):
                    psv, rpsv = PS.get()
                    psg, rpsg = PS.get()
                    for k in range(3):
                        P.op("pe", lambda: nc.tensor.matmul(psv[:], lhsT=dgt[:, pi, k, :], rhs=ufs[pi][0][:, tb * TB + k:tb * TB + k + TB],
                                                            start=(k == 0), stop=(k == 2)), reads=[rdg, ufs[pi][1]], writes=[rpsv])
                    for k in range(3):
                        P.op("pe", lambda: nc.tensor.matmul(psg[:], lhsT=dgt[:, pi + 2, k, :], rhs=ufs[pi + 2][0][:, tb * TB + k:tb * TB + k + TB],
                                                            start=(k == 0), stop=(k == 2)), reads=[rdg, ufs[pi + 2][1]], writes=[rpsg])
                    gl, rgl = f32p.get()
                    P.op("act", lambda: nc.scalar.activation(out=gl[:], in_=psg[:], func=AF.Gelu_apprx_tanh), reads=[rpsg], writes=[rgl])
                    o, ro = b16p.get()
                    P.op("dve", lambda: nc.vector.tensor_tensor(out=o[:], in0=psv[:], in1=gl[:], op=ALU.mult), reads=[rpsv, rgl], writes=[ro])
                    r0_ = (2 * grp + pi) * 128
                    P.dma(act_s.ap()[r0_:r0_ + 128, tb * TB:(tb + 1) * TB], o[:], reads=[ro], writes=[R_act])

        P.barrier()
        AR.reset()
        wd = AR.alloc([128, 22, 1024], BF16)
        R_wd = Res("wd")
        wdd = W["ffn_w_down"].ap()[l].rearrange("(kc p) n -> p kc n", p=128)
        P.dma(wd[:, 0:11, :], wdd[:, 0:11, :], writes=[R_wd], q="pool")
        P.dma(wd[:, 11:22, :], wdd[:, 11:22, :], writes=[R_wd], q="pool")
        actp = TPool(nc, "actp", [128, 22, 512], BF16, 2, arena=AR)
        xp9 = TPool(nc, "xp9", [128, 512], F32, 4, arena=AR)
        for tb in range(NTB):
            at, rat = actp.get()
            P.dma(at[:], act_s.ap()[:, tb * TB:(tb + 1) * TB].rearrange("(kc p) t -> p kc t", p=128), reads=[R_act], writes=[rat])
            for co in range(8):
                xr, rxr = xp9.get()
                P.dma(xr[:], x1T_s.ap()[co * 128:(co + 1) * 128, tb * TB:(tb + 1) * TB], reads=[R_x1], writes=[rxr])
                ps, rps = PS.get()
                for kc in range(22):
                    P.op("pe", lambda: nc.tensor.matmul(ps[:], lhsT=wd[:, kc, co * 128:(co + 1) * 128], rhs=at[:, kc, :],
                                                        start=(kc == 0), stop=(kc == 21)), reads=[R_wd, rat], writes=[rps])
                P.op("dve", lambda: nc.vector.tensor_tensor(out=xr[:], in0=ps[:], in1=xr[:], op=ALU.add), reads=[rps], writes=[rxr])
                P.dma(x_dst.ap()[co * 128:(co + 1) * 128, tb * TB:(tb + 1) * TB], xr[:], reads=[rxr], writes=[R_xdst])

    P.finish()
    return nc


_CACHE = {}


def make_in_maps(inputs, nb=8):
    common = {}
    for nm in ("w_in", "w_attn_out", "s5_glu_w1", "s5_glu_w2", "conv_w_out", "w_out", "ffn_w_up", "ffn_w_down"):
        common[nm] = np.ascontiguousarray(inputs[nm], dtype=np.float32)
    vs, ls, ss = [], [], []
    for l in range(DEPTH):
        v, lamv, s5 = host_layer_params(inputs, l)
        vs.append(v)
        ls.append(lamv)
        ss.append(s5)
    common["vecs"] = np.stack(vs)
    common["lamv"] = np.stack(ls)
    common["s5p"] = np.stack(ss)
    common["rel_bias"] = np.ascontiguousarray(inputs["rel_bias"], dtype=np.float32)
    common.update({"c_" + k: v for k, v in host_consts().items()})
    x = inputs["x"]
    in_maps = []
    for b in range(nb):
        m = dict(common)
        m["xT"] = np.ascontiguousarray(x[b].T)
        in_maps.append(m)
    return in_maps


def kernel(**inputs):
    inputs = {k: np.asarray(v) for k, v in inputs.items()}
    if "nc" not in _CACHE:
        _CACHE["nc"] = build()
    nc = _CACHE["nc"]
    in_maps = make_in_maps(inputs)
    res = run_bass_kernel_spmd(nc, in_maps, core_ids=list(range(8)))
    out = np.stack([np.ascontiguousarray(r["yT"].T) for r in res.results], 0)
    return out.astype(np.float32)
```

```python
import math
import numpy as np
import ml_dtypes
import concourse.bass as bass
import concourse.mybir as mybir
from concourse.bass_utils import run_bass_kernel_spmd

F32 = mybir.dt.float32
BF16 = mybir.dt.bfloat16
I32 = mybir.dt.int32
ALU = mybir.AluOpType
AF = mybir.ActivationFunctionType
AX = mybir.AxisListType

D = 1024
S = 4096
DEPTH = 2
TB = 512
NTB = S // TB
FFN_H = 2816
IN_W = 9216
OFF_Q, OFF_K, OFF_V, OFF_U, OFF_C, OFF_G = 0, 1024, 2048, 3072, 4096, 6144
EPS = 1e-6
GC = 1.5957691216057308


class Res:
    __slots__ = ("name", "lw", "rd")

    def __init__(self, name=""):
        self.name = name
        self.lw = None
        self.rd = []


class Prog:
    ENGS = ("pe", "dve", "act", "pool", "sp")

    def __init__(self, nc, n_dma_sems=32):
        self.nc = nc
        self.eng = {"pe": nc.tensor, "dve": nc.vector, "act": nc.scalar,
                    "pool": nc.gpsimd, "sp": nc.sync}
        self.sems = {}
        self.cnt = {}
        for e in self.ENGS:
            self.sems[e] = nc.alloc_semaphore("c_" + e)
            self.cnt[e] = 0
        self.n_dma = n_dma_sems
        for i in range(n_dma_sems):
            k = "d%d" % i
            self.sems[k] = nc.alloc_semaphore("s_" + k)
            self.cnt[k] = 0
        self.dma_rr = 0
        self.known = {e: {} for e in self.ENGS}
        self.ninstr = 0

    def _deps(self, reads, writes):
        deps = {}

        def add(t):
            if t is None:
                return
            k, v = t
            if deps.get(k, 0) < v:
                deps[k] = v
        for r in reads:
            add(r.lw)
        for w in writes:
            add(w.lw)
            for t in w.rd:
                add(t)
        return deps

    def _wait(self, e, deps):
        kn = self.known[e]
        for k, v in deps.items():
            if k == e and e == "pe":
                continue
            if kn.get(k, 0) >= v:
                continue
            self.eng[e].wait_ge(self.sems[k], v)
            kn[k] = v

    def _commit(self, tok, reads, writes):
        for r in reads:
            r.rd.append(tok)
            if len(r.rd) > 48:
                m = {}
                for k, v in r.rd:
                    if m.get(k, 0) < v:
                        m[k] = v
                r.rd = list(m.items())
        for w in writes:
            w.lw = tok
            w.rd = []

    def op(self, e, fn, reads=(), writes=()):
        deps = self._deps(reads, writes)
        self._wait(e, deps)
        ins = fn()
        self.cnt[e] += 1
        ins.then_inc(self.sems[e], 1)
        self._commit((e, self.cnt[e]), reads, writes)
        self.ninstr += 1
        return ins

    def dma(self, out, in_, reads=(), writes=(), q="sp", **kw):
        deps = self._deps(reads, writes)
        self._wait(q, deps)
        k = "d%d" % self.dma_rr
        self.dma_rr = (self.dma_rr + 1) % self.n_dma
        self._wait(q, {k: self.cnt[k]})
        ins = self.eng[q].dma_start(out=out, in_=in_, **kw)
        self.cnt[k] += 16
        ins.then_inc(self.sems[k], 16)
        self._commit((k, self.cnt[k]), reads, writes)
        self.ninstr += 1
        return ins

    def barrier(self):
        allv = {k: v for k, v in self.cnt.items() if v > 0}
        for e in self.ENGS:
            self._wait(e, allv)

    def finish(self, e="sp"):
        allv = {k: v for k, v in self.cnt.items() if v > 0}
        self._wait(e, allv)


class Arena:
    def __init__(self, t, nelem):
        self.t = t
        self.n = nelem
        self.off = 0

    def reset(self):
        self.off = 0

    def alloc(self, shape, dt):
        nel = 1
        for d in shape[1:]:
            nel *= d
        nb = nel * (2 if dt == BF16 else 4)
        nb = (nb + 31) // 32 * 32
        assert self.off + nb // 2 <= self.n, ("arena overflow", self.off, nb, self.n)
        ap = self.t[:, self.off:self.off + nb // 2]
        self.off += nb // 2
        if dt != BF16:
            ap = ap.bitcast(dt)
        ap = ap[:, 0:nel]
        if len(shape) == 3:
            ap = ap.rearrange("p (a b) -> p a b", a=shape[1])
        elif len(shape) == 4:
            ap = ap.rearrange("p (a b c) -> p a b c", a=shape[1], b=shape[2])
        return ap


class SubPool:
    def __init__(self, items):
        self.t = list(items)
        self.i = 0

    def get(self):
        r = self.t[self.i]
        self.i = (self.i + 1) % len(self.t)
        return r


class TPool:
    def __init__(self, nc, name, shape, dt, n, psum=False, arena=None):
        self.t = []
        for i in range(n):
            if psum:
                h = nc.alloc_psum_tensor("%s%d" % (name, i), shape, dt)
            elif arena is not None:
                h = arena.alloc(shape, dt)
            else:
                h = nc.alloc_sbuf_tensor("%s%d" % (name, i), shape, dt)
            self.t.append((h, Res("%s%d" % (name, i))))
        self.i = 0

    def get(self):
        r = self.t[self.i]
        self.i = (self.i + 1) % len(self.t)
        return r


def run_pipe(gens):
    active = []

    def step():
        nxt = []
        for a in active:
            try:
                next(a)
                nxt.append(a)
            except StopIteration:
                pass
        active[:] = nxt
    for g in gens:
        active.append(g)
        step()
    while active:
        step()


def t5_bucket_np(rel):
    nb = 16
    n = -rel
    ret = np.where(n < 0, nb, 0)
    n = np.abs(n)
    max_exact = nb // 2
    nf = np.maximum(n, 1).astype(np.float32)
    large = max_exact + (np.log(nf / max_exact) / math.log(128 / max_exact) * (nb - max_exact)).astype(np.int32)
    large = np.minimum(large, nb - 1)
    return ret + np.where(n < max_exact, n, large)


def host_consts():
    c = {}
    c["ident_f"] = np.eye(128, dtype=np.float32)
    jsw = np.zeros((128, 128), np.float32)
    for i in range(128):
        jsw[i, (i + 64) % 128] = 1.0
    c["jswap_f"] = jsw
    c["jrev_f"] = np.eye(128, dtype=np.float32)[::-1].copy()
    bd = np.zeros((128, 128), np.float32)
    bd[:64, :64] = 1.0
    bd[64:, 64:] = 1.0
    c["bdones_f"] = bd
    m = np.arange(384)
    rel = 127 - m
    bk = t5_bucket_np(rel)
    oh = np.zeros((32, 384), np.float32)
    oh[bk, m] = 1.0
    oh[15, :] -= 1.0
    oh[:, 383] = 0.0
    c["onehot"] = oh
    assert np.all(t5_bucket_np(-np.arange(129, 4096)) == 15)
    mk = np.ones((128, 256), np.float32)
    mk[64:, :64] = 0.0
    c["bandmask"] = mk
    ii = np.arange(128) // 16
    c["mask_intra"] = (ii[None, :] >= ii[:, None]).astype(np.float32)
    sg = np.zeros((128, 2), np.float32)
    sg[:64, 0] = -1.0
    sg[64:, 0] = 1.0
    sg[:64, 1] = 1.0
    sg[64:, 1] = -1.0
    c["sig"] = sg
    return c


NVEC = 8 + 8 + 1 + 1 + 1 + 8 + 8 + 8 + 8 * 31 + 44 * 3
V_GMIX, V_GFFN, V_QG, V_KG, V_SUB, V_CB, V_LNG, V_LNB, V_CW, V_FW = 0, 8, 16, 17, 18, 19, 27, 35, 43, 43 + 248


def host_layer_params(inp, l):
    v = np.zeros((128, NVEC), np.float32)
    v[:, V_GMIX:V_GMIX + 8] = inp["norm_mix"][l].reshape(8, 128).T
    v[:, V_GFFN:V_GFFN + 8] = inp["norm_ffn"][l].reshape(8, 128).T
    v[:, V_QG] = np.tile(inp["qk_gain_q"][l], 2)
    v[:, V_KG] = np.tile(inp["qk_gain_k"][l], 2)
    v[:, V_SUB] = inp["diff_subln"][l]
    v[:, V_CB:V_CB + 8] = inp["conv_dw_b"][l].reshape(8, 128).T
    v[:, V_LNG:V_LNG + 8] = inp["conv_ln_g"][l].reshape(8, 128).T
    v[:, V_LNB:V_LNB + 8] = inp["conv_ln_b"][l].reshape(8, 128).T
    v[:, V_CW:V_CW + 248] = inp["conv_dw_w"][l].reshape(31, 8, 128).transpose(2, 1, 0).reshape(128, 248)
    v[:, V_FW:V_FW + 132] = inp["ffn_dw_w"][l].reshape(3, 44, 128).transpose(2, 1, 0).reshape(128, 132)
    lamv = np.concatenate([inp["lambda_q1"][l], inp["lambda_k1"][l], inp["lambda_q2"][l], inp["lambda_k2"][l]])
    lamv = np.broadcast_to(lamv[None, :], (128, 256)).copy()
    s5 = np.zeros((128, 64 * 4 + 4 * 64 * 16), np.float32)
    lr = inp["s5_lambda_re"][l].T
    li = inp["s5_lambda_im"][l].T
    s5[:, 0:64] = np.concatenate([lr, lr], 0)
    s5[:, 64:128] = np.concatenate([li, li], 0)
    s5[:, 128:192] = np.broadcast_to(inp["s5_log_step"][l][None, :], (128, 64))
    s5[:, 192:256] = np.tile(inp["s5_d"][l].T, (8, 1))
    br = inp["s5_b_re"][l].transpose(1, 0, 2).reshape(64, 1024)
    bi = inp["s5_b_im"][l].transpose(1, 0, 2).reshape(64, 1024)
    cr = inp["s5_c_re"][l].transpose(2, 0, 1).reshape(64, 1024)
    ci = inp["s5_c_im"][l].transpose(2, 0, 1).reshape(64, 1024)
    s5[:, 256:1280] = np.concatenate([br, bi], 0)
    s5[:, 1280:2304] = np.concatenate([bi, br], 0)
    s5[:, 2304:3328] = np.concatenate([cr, ci], 0)
    s5[:, 3328:4352] = np.concatenate([ci, cr], 0)
    return v, lamv, s5


def build(dbg=None, nlayers=DEPTH):
    nc = bass.Bass("TRN2", target_bir_lowering=False)
    P = Prog(nc)
    dbg = dbg or set()

    def dram_in(name, shape, dt=F32):
        return nc.dram_tensor(name, list(shape), dt, kind="ExternalInput")

    def scratch(name, shape, dt):
        kind = "ExternalOutput" if name in dbg else "Internal"
        return nc.dram_tensor(name, list(shape), dt, kind=kind)

    xT_in = dram_in("xT", [D, S])
    W = {}
    for nm, shp in [("w_in", [DEPTH, D, IN_W]), ("w_attn_out", [DEPTH, D, D]), ("s5_glu_w1", [DEPTH, D, D]),
                    ("s5_glu_w2", [DEPTH, D, D]), ("conv_w_out", [DEPTH, D, D]), ("w_out", [DEPTH, D, D]),
                    ("ffn_w_up", [DEPTH, D, 2 * FFN_H]), ("ffn_w_down", [DEPTH, FFN_H, D])]:
        W[nm] = dram_in(nm, shp)
    vecs_in = dram_in("vecs", [DEPTH, 128, NVEC])
    lamv_in = dram_in("lamv", [DEPTH, 128, 256])
    s5p_in = dram_in("s5p", [DEPTH, 128, 4352])
    relb_in = dram_in("rel_bias", [32, 8])
    W_names = list(W.keys())
    cst = {}
    for nm, shp in [("ident_f", [128, 128]), ("jswap_f", [128, 128]), ("jrev_f", [128, 128]), ("bdones_f", [128, 128]),
                    ("onehot", [32, 384]), ("bandmask", [128, 256]), ("mask_intra", [128, 128]), ("sig", [128, 2])]:
        cst[nm] = dram_in("c_" + nm, shp)
    yT_out = nc.dram_tensor("yT", [D, S], F32, kind="ExternalOutput")

    qT_s = scratch("qT_s", [D, S], BF16)
    kT_s = scratch("kT_s", [D, S], BF16)
    v_s = scratch("v_s", [S, D], BF16)
    u_s = scratch("u_s", [512, 8192], BF16)
    cT_s = scratch("cT_s", [D, S], BF16)
    gT_s = scratch("gT_s", [3 * D, S], BF16)
    br_s = scratch("br_s", [3 * D, S], BF16)
    x1T_s = scratch("x1T_s", [D, S], F32)
    xmidT_s = scratch("xmidT_s", [D, S], F32)
    act_s = scratch("act_s", [FFN_H, S], BF16)
    wr_d = scratch("wr_d", [8, 384], F32)
    R_qT, R_kT, R_v, R_u, R_cT, R_gT, R_br, R_x1, R_xmid, R_act, R_wr = [Res(n) for n in
        ("qT", "kT", "v", "u", "cT", "gT", "br", "x1", "xmid", "act", "wr")]

    sb = nc.alloc_sbuf_tensor
    R_c = Res("consts")
    ident_f = sb("ident_f", [128, 128], F32)
    jswap_f = sb("jswap_f", [128, 128], F32)
    jrev_f = sb("jrev_f", [128, 128], F32)
    mask_intra = sb("mask_intra", [128, 128], F32)
    bandmask = sb("bandmask", [128, 256], F32)
    sig = sb("sig", [128, 2], F32)
    ident_b = sb("ident_b", [128, 128], BF16)
    ones_b = sb("ones_b", [128, 128], BF16)
    bdones_b = sb("bdones_b", [128, 128], BF16)
    epsc = sb("epsc", [128, 1], F32)
    onehot = sb("onehot", [32, 384], F32)
    relb = sb("relb", [32, 8], F32)
    expB = sb("expB", [128, 8, 256], BF16)
    for t_, nm in [(ident_f, "ident_f"), (jswap_f, "jswap_f"), (jrev_f, "jrev_f"), (mask_intra, "mask_intra"),
                   (bandmask, "bandmask"), (sig, "sig"), (onehot, "onehot")]:
        P.dma(t_[:], cst[nm].ap(), writes=[R_c])
    P.dma(relb[:], relb_in.ap(), writes=[R_c])
    P.dma(bdones_b[:], cst["bdones_f"].ap(), writes=[R_c], q="pool")
    P.op("dve", lambda: nc.vector.tensor_copy(out=ident_b[:], in_=ident_f[:]), reads=[R_c], writes=[R_c])
    P.op("dve", lambda: nc.vector.memset(ones_b[:], 1.0), writes=[R_c])
    P.op("dve", lambda: nc.vector.memset(epsc[:], EPS), writes=[R_c])

    BIGA = sb("BIGA", [128, 8, S], BF16)
    R_A = [Res("A%d" % i) for i in range(8)]
    ARENA_N = 55296
    AR = Arena(sb("ARENA", [128, ARENA_N], BF16), ARENA_N)
    vec = sb("vec", [128, NVEC], F32)
    R_vec = Res("vec")
    lam4 = sb("lam4", [128, 256], F32)
    neglam = sb("neglam", [128, 1], F32)
    subsc = sb("subsc", [128, 1], F32)
    R_lam = Res("lam")

    PSALL = nc.alloc_psum_tensor("psall", [128, 4096], F32)
    PS = SubPool([(PSALL[:, i * 512:(i + 1) * 512], Res("ps%d" % i)) for i in range(8)])
    PSPAIR = [(PSALL[:, 2048:3072].rearrange("p (c n) -> p c n", c=2), Res("pp0")),
              (PSALL[:, 3072:4096].rearrange("p (c n) -> p c n", c=2), Res("pp1"))]
    f32p = TPool(nc, "f32t", [128, 512], F32, 6)
    b16p = TPool(nc, "b16t", [128, 512], BF16, 8)
    wpool = TPool(nc, "wt", [128, 8, 512], BF16, 2, arena=AR)
    xin = TPool(nc, "xin", [128, 8, 512], F32, 2, arena=AR)
    sqp = TPool(nc, "sq", [128, 8, 512], BF16, 1, arena=AR)
    ustage = TPool(nc, "ust", [128, 32, 8, 16], BF16, 2, arena=AR)

    ps, rps = PS.get()
    P.op("pe", lambda: nc.tensor.matmul(ps[0:8, 0:384], lhsT=relb[:], rhs=onehot[:], start=True, stop=True),
         reads=[R_c], writes=[rps])
    wr_sb, r_wr_sb = f32p.get()
    P.op("act", lambda: nc.scalar.copy(out=wr_sb[0:8, 0:384], in_=ps[0:8, 0:384]), reads=[rps], writes=[r_wr_sb])
    P.dma(wr_d.ap(), wr_sb[0:8, 0:384], reads=[r_wr_sb], writes=[R_wr])
    for h in range(8):
        xt_, rxt = f32p.get()
        P.dma(xt_[:, 0:256], bass.AP(tensor=wr_d, offset=384 * h, ap=[[1, 128], [1, 256]]), reads=[R_wr], writes=[rxt])
        ps, rps = PS.get()
        P.op("pe", lambda: nc.tensor.matmul(ps[:, 0:256], lhsT=jrev_f[:], rhs=xt_[:, 0:256], start=True, stop=True),
             reads=[R_c, rxt], writes=[rps])
        et_, ret = f32p.get()
        P.op("act", lambda: nc.scalar.activation(out=et_[:, 0:256], in_=ps[:, 0:256], func=AF.Exp), reads=[rps], writes=[ret])
        P.op("dve", lambda: nc.vector.tensor_tensor(out=expB[:, h, :], in0=et_[:, 0:256], in1=bandmask[:], op=ALU.mult),
             reads=[ret, R_c], writes=[R_c])

    def rsqrt_from(ps_ap, scale, n=512, width=None):
        t, rt = f32p.get()
        return t, rt

    def norm_stage(src_dram, R_src, gcol, keep=None):
        for tb in range(NTB):
            xb, rxb = xin.get()
            P.dma(xb[:], src_dram.ap()[:, tb * TB:(tb + 1) * TB].rearrange("(kc p) t -> p kc t", p=128),
                  reads=[R_src], writes=[rxb])
            norm_block(xb, rxb, gcol, tb, sqp)

    def norm_block(xb, rxb, gcol, tb, sqpool):
        sq, rsq = sqpool.get()
        P.op("act", lambda: nc.scalar.activation(out=sq[:], in_=xb[:], func=AF.Square), reads=[rxb], writes=[rsq])
        ps, rps = PS.get()
        for kc in range(8):
            P.op("pe", lambda: nc.tensor.matmul(ps[:], lhsT=ones_b[:], rhs=sq[:, kc, :], start=(kc == 0), stop=(kc == 7)),
                 reads=[rsq, R_c], writes=[rps])
        rs, rrs = f32p.get()
        P.op("act", lambda: nc.scalar.activation(out=rs[:], in_=ps[:], func=AF.Sqrt, scale=1.0 / D, bias=epsc[:, 0:1]),
             reads=[rps, R_c], writes=[rrs])
        P.op("dve", lambda: nc.vector.reciprocal(out=rs[:], in_=rs[:]), reads=[rrs], writes=[rrs])
        for kc in range(8):
            P.op("dve", lambda: nc.vector.scalar_tensor_tensor(
                out=BIGA[:, kc, tb * TB:(tb + 1) * TB], in0=xb[:, kc, :], scalar=vec[:, gcol + kc:gcol + kc + 1],
                in1=rs[:], op0=ALU.mult, op1=ALU.mult), reads=[rxb, rrs, R_vec], writes=[R_A[kc]])

    def load_w(wdram2d, offs, KC=8):
        wt, rwt = wpool.get()
        pos = 0
        for off, n in offs:
            P.dma(wt[:, 0:KC, pos:pos + n], wdram2d[:, off:off + n].rearrange("(kc p) n -> p kc n", p=128),
                  writes=[rwt], q="pool")
            pos += n
        return wt, rwt

    def mm_fm(wt, rwt, ci, tb, KC=8, src=None, rsrc=None):
        ps, rps = PS.get()
        for kc in range(KC):
            P.op("pe", lambda: nc.tensor.matmul(ps[:], lhsT=wt[:, kc, ci * 128:(ci + 1) * 128],
                                                rhs=BIGA[:, kc, tb * TB:(tb + 1) * TB], start=(kc == 0), stop=(kc == KC - 1)),
                 reads=[rwt, R_A[kc]], writes=[rps])
        return ps, rps

    def qk_tile(wt, rwt, ci, gcol, dst, R_dst, row0, tb):
        ps, rps = mm_fm(wt, rwt, ci, tb)
        sq, rsq = b16p.get()
        P.op("act", lambda: nc.scalar.activation(out=sq[:], in_=ps[:], func=AF.Square), reads=[rps], writes=[rsq])
        yield
        ps2, rps2 = PS.get()
        P.op("pe", lambda: nc.tensor.matmul(ps2[:], lhsT=bdones_b[:], rhs=sq[:], start=True, stop=True),
             reads=[rsq, R_c], writes=[rps2])
        rs, rrs = f32p.get()
        P.op("act", lambda: nc.scalar.activation(out=rs[:], in_=ps2[:], func=AF.Sqrt, scale=1.0 / 64, bias=epsc[:, 0:1]),
             reads=[rps2, R_c], writes=[rrs])
        P.op("dve", lambda: nc.vector.reciprocal(out=rs[:], in_=rs[:]), reads=[rrs], writes=[rrs])
        o, ro = b16p.get()
        P.op("dve", lambda: nc.vector.scalar_tensor_tensor(out=o[:], in0=ps[:], scalar=vec[:, gcol:gcol + 1], in1=rs[:],
                                                           op0=ALU.mult, op1=ALU.mult), reads=[rps, rrs, R_vec], writes=[ro])
        P.dma(dst.ap()[row0:row0 + 128, tb * TB:(tb + 1) * TB], o[:], reads=[ro], writes=[R_dst])


    y_s = scratch("y_s", [S, D], BF16)
    R_ys = Res("ys")

    def s5_stage(l):
        P.barrier()
        AR.reset()
        R_sp = Res("sp")
        sp = AR.alloc([128, 4352], F32)
        P.dma(sp[:], s5p_in.ap()[l], writes=[R_sp])
        lr, li, lstep, drep = sp[:, 0:64], sp[:, 64:128], sp[:, 128:192], sp[:, 192:256]
        T1, T2, CTa, CTb = sp[:, 256:1280], sp[:, 1280:2304], sp[:, 2304:3328], sp[:, 3328:4352]
        SL = AR.alloc([128, 20, 64], F32)
        KI = AR.alloc([128, 64], I32)
        PWr = AR.alloc([128, 16, 64], F32)
        PWi = AR.alloc([128, 16, 64], F32)
        AKr = AR.alloc([128, 9, 64], F32)
        AKi = AR.alloc([128, 9, 64], F32)
        AKs = AR.alloc([128, 9, 64], F32)
        PRr = AR.alloc([128, 15, 64], F32)
        PRi = AR.alloc([128, 15, 64], F32)
        QRr = AR.alloc([128, 9, 64], F32)
        QRi = AR.alloc([128, 9, 64], F32)
        BT1 = AR.alloc([128, 1024], F32)
        BT2 = AR.alloc([128, 1024], F32)
        tA = AR.alloc([128, 1024], F32)
        tB = AR.alloc([128, 1024], F32)
        rw = dict(reads=[R_sp, R_c], writes=[R_sp])

        def tt(o, a, b, op):
            P.op("dve", lambda: nc.vector.tensor_tensor(out=o, in0=a, in1=b, op=op), **rw)

        def ts(o, a, s1, op0, s2=None, op1=None):
            if op1 is None:
                P.op("dve", lambda: nc.vector.tensor_scalar(out=o, in0=a, scalar1=s1, scalar2=None, op0=op0), **rw)
            else:
                P.op("dve", lambda: nc.vector.tensor_scalar(out=o, in0=a, scalar1=s1, scalar2=s2, op0=op0, op1=op1), **rw)

        def act(o, a, func, scale=1.0):
            P.op("act", lambda: nc.scalar.activation(out=o, in_=a, func=func, scale=scale), **rw)

        def cp(o, a):
            P.op("dve", lambda: nc.vector.tensor_copy(out=o, in_=a), **rw)

        sl = lambda i: SL[:, i, :]
        dl, mag, ang, tu, kf, r1, rr, mm_, sinv, cosv, ar, ai, den, nr, fr, fi, t1, t2, sfi, nsfi = [sl(i) for i in range(20)]
        act(dl, lstep, AF.Exp)
        tt(t1, lr, dl, ALU.mult)
        act(mag, t1, AF.Exp)
        tt(ang, li, dl, ALU.mult)
        ts(tu, ang, 1.0 / (2 * math.pi), ALU.mult)
        cp(KI[:], tu)
        cp(kf, KI[:])
        tt(r1, tu, kf, ALU.subtract)

        def wrap_sin(dst, src, shift):
            ts(rr, src, shift, ALU.add)
            ts(mm_, rr, 0.5, ALU.is_gt)
            tt(rr, rr, mm_, ALU.subtract)
            ts(mm_, rr, 0.5, ALU.is_gt)
            tt(rr, rr, mm_, ALU.subtract)
            ts(mm_, rr, -0.5, ALU.is_lt)
            tt(rr, rr, mm_, ALU.add)
            ts(mm_, rr, -0.5, ALU.is_lt)
            tt(rr, rr, mm_, ALU.add)
            act(dst, rr, AF.Sin, scale=2 * math.pi)
        wrap_sin(sinv, r1, 0.0)
        wrap_sin(cosv, r1, 0.25)
        tt(ar, mag, cosv, ALU.mult)
        tt(ai, mag, sinv, ALU.mult)
        tt(t1, lr, lr, ALU.mult)
        tt(t2, li, li, ALU.mult)
        tt(den, t1, t2, ALU.add)
        P.op("dve", lambda: nc.vector.reciprocal(out=den, in_=den), **rw)
        ts(nr, ar, -1.0, ALU.add)
        tt(t1, nr, lr, ALU.mult)
        tt(t2, ai, li, ALU.mult)
        tt(t1, t1, t2, ALU.add)
        tt(fr, t1, den, ALU.mult)
        tt(t1, ai, lr, ALU.mult)
        tt(t2, nr, li, ALU.mult)
        tt(t1, t1, t2, ALU.subtract)
        tt(fi, t1, den, ALU.mult)
        ts(sfi, fi, sig[:, 0:1], ALU.mult)
        ts(nsfi, fi, sig[:, 1:2], ALU.mult)
        g3 = lambda a: a.rearrange("p (g h) -> p g h", h=16)
        bc3 = lambda a: a.unsqueeze(2).broadcast_to([128, 64, 16])
        tt(g3(tA[:]), bc3(fr), g3(T1), ALU.mult)
        tt(g3(tB[:]), bc3(sfi), g3(T2), ALU.mult)
        tt(BT1[:], tA[:], tB[:], ALU.add)
        tt(g3(tA[:]), bc3(fr), g3(T2), ALU.mult)
        tt(g3(tB[:]), bc3(nsfi), g3(T1), ALU.mult)
        tt(BT2[:], tA[:], tB[:], ALU.add)

        def cmul(orr, oi, xr, xi, yr, yi):
            tt(t1, xr, yr, ALU.mult)
            tt(t2, xi, yi, ALU.mult)
            tt(orr, t1, t2, ALU.subtract)
            tt(t1, xr, yi, ALU.mult)
            tt(t2, xi, yr, ALU.mult)
            tt(oi, t1, t2, ALU.add)
        P.op("dve", lambda: nc.vector.memset(PWr[:, 7, :], 1.0), **rw)
        P.op("dve", lambda: nc.vector.memset(PWi[:, 7, :], 0.0), **rw)
        cp(PWr[:, 8, :], ar)
        cp(PWi[:, 8, :], ai)
        for n in range(2, 9):
            cmul(PWr[:, 7 + n, :], PWi[:, 7 + n, :], PWr[:, 6 + n, :], PWi[:, 6 + n, :], PWr[:, 8, :], PWi[:, 8, :])
        tt(t1, ar, ar, ALU.mult)
        tt(t2, ai, ai, ALU.mult)
        tt(den, t1, t2, ALU.add)
        P.op("dve", lambda: nc.vector.reciprocal(out=den, in_=den), **rw)
        tt(PWr[:, 6, :], ar, den, ALU.mult)
        tt(t1, ai, den, ALU.mult)
        ts(PWi[:, 6, :], t1, -1.0, ALU.mult)
        for n in range(2, 8):
            cmul(PWr[:, 7 - n, :], PWi[:, 7 - n, :], PWr[:, 8 - n, :], PWi[:, 8 - n, :], PWr[:, 6, :], PWi[:, 6, :])
        cp(AKr[:, 0, :], PWr[:, 15, :])
        cp(AKi[:, 0, :], PWi[:, 15, :])
        for k in range(1, 9):
            cmul(AKr[:, k, :], AKi[:, k, :], AKr[:, k - 1, :], AKi[:, k - 1, :], AKr[:, k - 1, :], AKi[:, k - 1, :])
        ts(AKs[:], AKi[:], sig[:, 1:2], ALU.mult)
        for s_ in range(15):
            cp(PRr[:, s_, :], PWr[:, 14 - s_, :])
            ts(PRi[:, s_, :], PWi[:, 14 - s_, :], sig[:, 0:1], ALU.mult)
        ts(QRr[:], PWr[:, 7:16, :], sig[:, 1:2], ALU.mult)
        ts(QRi[:], PWi[:, 7:16, :], -1.0, ALU.mult)

        off_ut = AR.off
        utok = AR.alloc([128, 8192], BF16)
        R_ut = Res("utok")
        PSr = SubPool(PS.t[4:8])
        for cb in range(4):
            P.dma(utok[:], u_s.ap()[cb * 128:(cb + 1) * 128, :], reads=[R_u], writes=[R_ut])
            for g8 in range(8):
                ps, rps = PSr.get()
                psb = ps.bitcast(BF16)
                for gg in range(8):
                    g = g8 * 8 + gg
                    P.op("pe", lambda: nc.tensor.transpose(psb[:, gg * 128:(gg + 1) * 128], utok[:, g * 128:(g + 1) * 128], ident_b[:]),
                         reads=[R_ut, R_c], writes=[rps])
                dst = BIGA[:, g8, :].rearrange("p (g c) -> p g c", c=512)[:, :, cb * 128:(cb + 1) * 128]
                src = psb[:, :].rearrange("p (g c) -> p g c", c=128)
                if g8 % 2 == 0:
                    P.op("act", lambda: nc.scalar.copy(out=dst, in_=src), reads=[rps], writes=[R_A[g8]])
                else:
                    P.op("dve", lambda: nc.vector.tensor_copy(out=dst, in_=src), reads=[rps], writes=[R_A[g8]])

        P.barrier()
        AR.off = off_ut
        pfp = TPool(nc, "pf", [128, 4, 240], BF16, 2, arena=AR)
        qfp = TPool(nc, "qf", [128, 4, 144], BF16, 2, arena=AR)
        tP1 = AR.alloc([128, 4, 15, 16], F32)
        tP2 = AR.alloc([128, 4, 15, 16], F32)
        rkp = TPool(nc, "rk", [128, 9, 128], BF16, 4, arena=AR)
        minp = TPool(nc, "min", [128, 128], BF16, 4, arena=AR)
        PSm = SubPool(PS.t[4:8])
        mintp = TPool(nc, "mint", [128, 128], BF16, 4, arena=AR)
        xzp = TPool(nc, "xz", [128, 514], BF16, 5, arena=AR)
        ystp = TPool(nc, "yst", [128, 8, 64], BF16, 2, arena=AR)
        for xz_, rxz_ in xzp.t:
            P.op("pool", lambda: nc.gpsimd.memset(xz_[:, 0:1], 0.0), writes=[rxz_])
        psY = PS.t[0:4]
        BT1g, BT2g, CTag, CTbg = g3(BT1[:]), g3(BT2[:]), g3(CTa), g3(CTb)
        for bi in range(16):
            g0 = bi * 4
            pf, rpf = pfp.get()
            qf, rqf = qfp.get()
            pf4 = pf[:].rearrange("p g (b h) -> p g b h", h=16)
            qf4 = qf[:].rearrange("p g (b h) -> p g b h", h=16)
            e_ = lambda tab, nb: tab[:, :, g0:g0 + 4].rearrange("p b g -> p g b").unsqueeze(3).broadcast_to([128, 4, nb, 16])
            b_ = lambda tab, nb: tab[:, g0:g0 + 4, :].unsqueeze(2).broadcast_to([128, 4, nb, 16])
            rwp = dict(reads=[R_sp], writes=[R_sp])
            P.op("dve", lambda: nc.vector.tensor_tensor(out=tP1[:], in0=e_(PRr, 15), in1=b_(BT1g, 15), op=ALU.mult), **rwp)
            P.op("dve", lambda: nc.vector.tensor_tensor(out=tP2[:], in0=e_(PRi, 15), in1=b_(BT2g, 15), op=ALU.mult), **rwp)
            P.op("dve", lambda: nc.vector.tensor_tensor(out=pf4, in0=tP1[:], in1=tP2[:], op=ALU.add), reads=[R_sp], writes=[R_sp, rpf])
            P.op("dve", lambda: nc.vector.tensor_tensor(out=tP1[:, :, 0:9, :], in0=e_(QRr, 9), in1=b_(CTag, 9), op=ALU.mult), **rwp)
            P.op("dve", lambda: nc.vector.tensor_tensor(out=tP2[:, :, 0:9, :], in0=e_(QRi, 9), in1=b_(CTbg, 9), op=ALU.mult), **rwp)
            P.op("dve", lambda: nc.vector.tensor_tensor(out=qf4, in0=tP1[:, :, 0:9, :], in1=tP2[:, :, 0:9, :], op=ALU.add),
                 reads=[R_sp], writes=[R_sp, rqf])
            def grp_gen(gl):
                g = g0 + gl
                U_g = BIGA[:, g // 8, (g % 8) * 512:(g % 8 + 1) * 512]
                R_U = R_A[g // 8]
                rk, rrk = rkp.get()
                for k in range(9):
                    tf, rtf = f32p.get()
                    P.op("act", lambda: nc.scalar.activation(out=tf[:, 0:128], in_=ident_f[:], func=AF.Copy, scale=AKr[:, k, g:g + 1]),
                         reads=[R_sp, R_c], writes=[rtf])
                    P.op("dve", lambda: nc.vector.scalar_tensor_tensor(out=rk[:, k, :], in0=jswap_f[:], scalar=AKs[:, k, g:g + 1],
                                                                       in1=tf[:, 0:128], op0=ALU.mult, op1=ALU.add),
                         reads=[R_sp, R_c, rtf], writes=[rrk])
                ps, rps = PSr.get()
                psb = ps.bitcast(BF16)
                P.op("pe", lambda: nc.tensor.transpose(psb[:, 0:128], pf[:, gl, 0:128], ident_b[:]), reads=[rpf, R_c], writes=[rps])
                mi, rmi = minp.get()
                P.op("act", lambda: nc.scalar.copy(out=mi[:], in_=psb[:, 0:128]), reads=[rps], writes=[rmi])
                ps, rps = PSm.get()
                P.op("pe", lambda: nc.tensor.matmul(ps[:, 0:128], lhsT=pf[:, gl, 112:240], rhs=qf[:, gl, 0:128], start=True, stop=True),
                     reads=[rpf, rqf], writes=[rps])
                tf, rtf = f32p.get()
                P.op("dve", lambda: nc.vector.tensor_tensor(out=tf[:, 0:128], in0=ps[:, 0:128], in1=mask_intra[:], op=ALU.mult),
                     reads=[rps, R_c], writes=[rtf])
                mt, rmt = mintp.get()
                P.op("dve", lambda: nc.vector.scalar_tensor_tensor(out=mt[:], in0=ident_f[:], scalar=drep[:, g:g + 1], in1=tf[:, 0:128],
                                                                   op0=ALU.mult, op1=ALU.add), reads=[R_sp, R_c, rtf], writes=[rmt])
                yield
                xz, rxz = xzp.get()
                ps, rps = PSr.get()
                P.op("pe", lambda: nc.tensor.matmul(ps[:], lhsT=mi[:], rhs=U_g, start=True, stop=True), reads=[rmi, R_U], writes=[rps])
                P.op("act", lambda: nc.scalar.copy(out=xz[:, 1:513], in_=ps[:]), reads=[rps], writes=[rxz])
                for k in range(9):
                    yield
                    sh = 1 << k
                    ps, rps = PSr.get()
                    P.op("pe", lambda: nc.tensor.matmul(ps[:, 0:512 - sh], lhsT=rk[:, k, :], rhs=xz[:, 1:513 - sh], start=True, stop=True),
                         reads=[rrk, rxz], writes=[rps])
                    P.op("dve", lambda: nc.vector.tensor_tensor(out=xz[:, 1 + sh:513], in0=ps[:, 0:512 - sh], in1=xz[:, 1 + sh:513],
                                                                op=ALU.add), reads=[rps], writes=[rxz])
                yield
                for cb in range(4):
                    P.op("pe", lambda: nc.tensor.matmul(psY[cb][0][:, gl * 128:(gl + 1) * 128], lhsT=U_g[:, cb * 128:(cb + 1) * 128], rhs=mt[:],
                                                        start=True, stop=False), reads=[R_U, rmt], writes=[psY[cb][1]])
                    P.op("pe", lambda: nc.tensor.matmul(psY[cb][0][:, gl * 128:(gl + 1) * 128], lhsT=xz[:, cb * 128:(cb + 1) * 128],
                                                        rhs=qf[:, gl, 16:144], start=False, stop=True), reads=[rxz, rqf], writes=[psY[cb][1]])
            run_pipe(grp_gen(gl) for gl in range(4))
            for cb in range(4):
                yst, ryst = ystp.get()
                if "s5raw" in dbg:
                    P.op("act", lambda: nc.scalar.copy(out=yst[:].rearrange("p j (g h) -> p g j h", h=16),
                                                       in_=psY[cb][0][:].rearrange("p (g j h) -> p g j h", g=4, j=8)),
                         reads=[psY[cb][1]], writes=[ryst])
                else:
                    P.op("act", lambda: nc.scalar.activation(out=yst[:].rearrange("p j (g h) -> p g j h", h=16),
                                                             in_=psY[cb][0][:].rearrange("p (g j h) -> p g j h", g=4, j=8),
                                                             func=AF.Gelu_apprx_tanh), reads=[psY[cb][1]], writes=[ryst])
                P.dma(y_s.ap()[1024 * cb:1024 * (cb + 1), 64 * bi:64 * bi + 64].rearrange("(c j) n -> c j n", j=8), yst[:],
                      reads=[ryst], writes=[R_ys])

        P.barrier()
        AR.reset()
        ytp = TPool(nc, "ytk", [128, 1024], BF16, 2, arena=AR)
        for tt_ in range(32):
            yt, ryt = ytp.get()
            P.dma(yt[:], y_s.ap()[tt_ * 128:(tt_ + 1) * 128, :], reads=[R_ys], writes=[ryt])
            ps, rps = PSr.get()
            psb = ps.bitcast(BF16)
            for kc in range(8):
                P.op("pe", lambda: nc.tensor.transpose(psb[:, kc * 128:(kc + 1) * 128], yt[:, kc * 128:(kc + 1) * 128], ident_b[:]),
                     reads=[ryt, R_c], writes=[rps])
            dst = BIGA[:, :, tt_ * 128:(tt_ + 1) * 128]
            src = psb[:, :].rearrange("p (k c) -> p k c", c=128)
            if tt_ % 2 == 0:
                P.op("act", lambda: nc.scalar.copy(out=dst, in_=src), reads=[rps], writes=R_A)
            else:
                P.op("dve", lambda: nc.vector.tensor_copy(out=dst, in_=src), reads=[rps], writes=R_A)
        wp4 = TPool(nc, "wt4", [128, 8, 512], BF16, 4, arena=AR)
        for grp in range(2):
            w1t, rw1 = wp4.get()
            P.dma(w1t[:], W["s5_glu_w1"].ap()[l][:, grp * 512:(grp + 1) * 512].rearrange("(kc p) n -> p kc n", p=128), writes=[rw1], q="pool")
            w2t, rw2 = wp4.get()
            P.dma(w2t[:], W["s5_glu_w2"].ap()[l][:, grp * 512:(grp + 1) * 512].rearrange("(kc p) n -> p kc n", p=128), writes=[rw2], q="pool")
            for ci in range(4):
                for tb in range(NTB):
                    psa, rpsa = mm_fm(w1t, rw1, ci, tb)
                    psb_, rpsb = mm_fm(w2t, rw2, ci, tb)
                    sg_, rsg = f32p.get()
                    P.op("act", lambda: nc.scalar.activation(out=sg_[:], in_=psb_[:], func=AF.Sigmoid), reads=[rpsb], writes=[rsg])
                    o, ro = b16p.get()
                    P.op("dve", lambda: nc.vector.tensor_tensor(out=o[:], in0=psa[:], in1=sg_[:], op=ALU.mult), reads=[rpsa, rsg], writes=[ro])
                    r0_ = D + (grp * 4 + ci) * 128
                    P.dma(br_s.ap()[r0_:r0_ + 128, tb * TB:(tb + 1) * TB], o[:], reads=[ro], writes=[R_br])

    for l in range(nlayers):
        x_src, R_xsrc = (xT_in, Res("xin")) if l == 0 else (xmidT_s, R_xmid)
        x_dst, R_xdst = (yT_out, Res("yout")) if l == nlayers - 1 else (xmidT_s, R_xmid)
        P.barrier()
        P.dma(vec[:], vecs_in.ap()[l], writes=[R_vec])
        P.dma(lam4[:], lamv_in.ap()[l], writes=[R_lam])
        lam_init = 0.8 - 0.6 * math.exp(-0.3 * l)
        lt, rlt = f32p.get()
        P.op("dve", lambda: nc.vector.tensor_tensor(out=lt[:, 0:64], in0=lam4[:, 0:64], in1=lam4[:, 64:128], op=ALU.mult),
             reads=[R_lam], writes=[rlt])
        P.op("dve", lambda: nc.vector.tensor_tensor(out=lt[:, 64:128], in0=lam4[:, 128:192], in1=lam4[:, 192:256], op=ALU.mult),
             reads=[R_lam], writes=[rlt])
        P.op("dve", lambda: nc.vector.reduce_sum(out=lt[:, 128:130], in_=lt[:, 0:128].rearrange("p (a b) -> p a b", a=2),
                                                 axis=AX.X), reads=[rlt], writes=[rlt])
        P.op("act", lambda: nc.scalar.activation(out=lt[:, 130:132], in_=lt[:, 128:130], func=AF.Exp), reads=[rlt], writes=[rlt])
        P.op("dve", lambda: nc.vector.scalar_tensor_tensor(out=neglam[:], in0=lt[:, 131:132], scalar=-lam_init, in1=lt[:, 130:131],
                                                           op0=ALU.add, op1=ALU.subtract), reads=[rlt], writes=[R_lam])
        P.op("dve", lambda: nc.vector.tensor_scalar(out=subsc[:], in0=vec[:, V_SUB:V_SUB + 1], scalar1=(1.0 - lam_init), scalar2=None,
                                                    op0=ALU.mult), reads=[R_vec], writes=[R_lam])

        norm_stage(x_src, R_xsrc, V_GMIX)

        w_in2 = W["w_in"].ap()[l]
        for seg, (off0, gcol, dst, R_dst) in enumerate([(OFF_Q, V_QG, qT_s, R_qT), (OFF_K, V_KG, kT_s, R_kT)]):
            for grp in range(2):
                wt, rwt = load_w(w_in2, [(off0 + grp * 512, 512)])
                run_pipe(qk_tile(wt, rwt, ci, gcol, dst, R_dst, (grp * 4 + ci) * 128, tb) for ci in range(4) for tb in range(NTB))
        for grp in range(2):
            wt, rwt = load_w(w_in2, [(OFF_V + grp * 512, 512)])
            for tt in range(32):
                ps, rps = PS.get()
                for kc in range(8):
                    P.op("pe", lambda: nc.tensor.matmul(ps[:], lhsT=BIGA[:, kc, tt * 128:(tt + 1) * 128], rhs=wt[:, kc, :],
                                                        start=(kc == 0), stop=(kc == 7)), reads=[rwt, R_A[kc]], writes=[rps])
                o, ro = b16p.get()
                if tt % 2 == 0:
                    P.op("act", lambda: nc.scalar.copy(out=o[:], in_=ps[:]), reads=[rps], writes=[ro])
                else:
                    P.op("dve", lambda: nc.vector.tensor_copy(out=o[:], in_=ps[:]), reads=[rps], writes=[ro])
                P.dma(v_s.ap()[tt * 128:(tt + 1) * 128, grp * 512:(grp + 1) * 512], o[:], reads=[ro], writes=[R_v])
        for grp in range(2):
            wt, rwt = load_w(w_in2, [(OFF_U + grp * 512, 512)])
            for cb in range(4):
                us, rus = ustage.get()
                for j in range(8):
                    ps, rps = PS.get()
                    for kc in range(8):
                        P.op("pe", lambda: nc.tensor.matmul(ps[:], lhsT=BIGA[:, kc, 1024 * cb + j:1024 * (cb + 1):8], rhs=wt[:, kc, :],
                                                            start=(kc == 0), stop=(kc == 7)), reads=[rwt, R_A[kc]], writes=[rps])
                    src_v = ps[:].rearrange("p (g h) -> p g h", h=16)
                    if j % 2 == 0:
                        P.op("act", lambda: nc.scalar.copy(out=us[:, :, j, :], in_=src_v), reads=[rps], writes=[rus])
                    else:
                        P.op("dve", lambda: nc.vector.tensor_copy(out=us[:, :, j, :], in_=src_v), reads=[rps], writes=[rus])
                P.dma(u_s.ap()[cb * 128:(cb + 1) * 128, grp * 4096:(grp + 1) * 4096], us[:].rearrange("p g j h -> p (g j h)"),
                      reads=[rus], writes=[R_u])
        for grp in range(4):
            wt, rwt = load_w(w_in2, [(OFF_C + grp * 256, 256), (OFF_C + 1024 + grp * 256, 256)])
            for ci in range(2):
                for tb in range(NTB):
                    psa, rpsa = mm_fm(wt, rwt, ci, tb)
                    psb, rpsb = mm_fm(wt, rwt, ci + 2, tb)
                    sg_, rsg = f32p.get()
                    P.op("act", lambda: nc.scalar.activation(out=sg_[:], in_=psb[:], func=AF.Sigmoid), reads=[rpsb], writes=[rsg])
                    o, ro = b16p.get()
                    P.op("dve", lambda: nc.vector.tensor_tensor(out=o[:], in0=psa[:], in1=sg_[:], op=ALU.mult),
                         reads=[rpsa, rsg], writes=[ro])
                    r0 = (grp * 2 + ci) * 128
                    P.dma(cT_s.ap()[r0:r0 + 128, tb * TB:(tb + 1) * TB], o[:], reads=[ro], writes=[R_cT])
        for grp in range(6):
            wt, rwt = load_w(w_in2, [(OFF_G + grp * 512, 512)])
            for ci in range(4):
                for tb in range(NTB):
                    ps, rps = mm_fm(wt, rwt, ci, tb)
                    o, ro = b16p.get()
                    P.op("act", lambda: nc.scalar.activation(out=o[:], in_=ps[:], func=AF.Sigmoid), reads=[rps], writes=[ro])
                    r0 = (grp * 4 + ci) * 128
                    P.dma(gT_s.ap()[r0:r0 + 128, tb * TB:(tb + 1) * TB], o[:], reads=[ro], writes=[R_gT])
        if "stop_s2" in dbg:
            break

        P.barrier()
        AR.reset()
        kpool = TPool(nc, "kT", [128, S], BF16, 2, arena=AR)
        qpool = TPool(nc, "qT", [128, S], BF16, 2, arena=AR)
        vpool = TPool(nc, "vh", [128, 32, 128], BF16, 2, arena=AR)
        epool = TPool(nc, "eT", [128, 512], BF16, 3, arena=AR)
        epool2 = TPool(nc, "eT2", [128, 2, 512], BF16, 6, arena=AR)
        PSp = SubPool(PSPAIR)
        o32 = TPool(nc, "o32", [128, 512], F32, 9, arena=AR)
        psO = [PS.t[0], PS.t[1]]
        psS = [PS.t[2], PS.t[3]]
        PSs = SubPool(PS.t[4:8])
        hbuf = {}

        def att_load(h):
            kt, rkt = kpool.get()
            P.dma(kt[:], kT_s.ap()[h * 128:(h + 1) * 128, :], reads=[R_kT], writes=[rkt])
            qt, rqt = qpool.get()
            P.dma(qt[:], qT_s.ap()[h * 128:(h + 1) * 128, :], reads=[R_qT], writes=[rqt])
            vt, rvt = vpool.get()
            P.dma(vt[:], v_s.ap()[:, h * 128:(h + 1) * 128].rearrange("(j p) e -> p j e", p=128), reads=[R_v], writes=[rvt])
            hbuf[h] = (kt, rkt, qt, rqt, vt, rvt)
            return
            yield

        def att_unit(h, qb, j):
            kt, rkt, qt, rqt, vt, rvt = hbuf[h]
            q0 = qb * TB
            nj = 4 * qb + 4
            lo = max(0, 128 * j - q0)
            pp, rpp = PSp.get()
            for c in range(2):
                P.op("pe", lambda: nc.tensor.matmul(pp[:, c, lo:512], lhsT=kt[64 * c:64 * c + 64, 128 * j:128 * j + 128],
                                                    rhs=qt[64 * c:64 * c + 64, q0 + lo:q0 + 512], start=True, stop=True),
                     reads=[rkt, rqt], writes=[rpp])
            et, ret = epool2.get()
            P.op("act", lambda: nc.scalar.activation(out=et[:, :, lo:512], in_=pp[:, :, lo:512], func=AF.Exp, scale=0.125),
                 reads=[rpp], writes=[ret])
            if j >= 4 * qb:
                a = 128 * (j - 4 * qb)
                b = min(a + 256, 512)
                ba = 0
            elif j == 4 * qb - 1:
                a, b, ba = 0, 128, 128
            else:
                a = None
            if a is not None:
                P.op("dve", lambda: nc.vector.tensor_tensor(out=et[:, :, a:b], in0=et[:, :, a:b],
                                                            in1=expB[:, h, ba:ba + (b - a)].unsqueeze(1).broadcast_to([128, 2, b - a]),
                                                            op=ALU.mult), reads=[ret, R_c], writes=[ret])
            yield
            yield
            for c in range(2):
                P.op("pe", lambda: nc.tensor.matmul(psO[c][0][:, lo:512], lhsT=vt[:, j, :], rhs=et[:, c, lo:512],
                                                    start=(j == 0), stop=(j == nj - 1)), reads=[rvt, ret], writes=[psO[c][1]])
                P.op("pe", lambda: nc.tensor.matmul(psS[c][0][:, lo:512], lhsT=ones_b[:], rhs=et[:, c, lo:512],
                                                    start=(j == 0), stop=(j == nj - 1)), reads=[ret, R_c], writes=[psS[c][1]])

        def att_final(h, qb):
            q0 = qb * TB
            yield
            r0, rr0 = o32.get()
            t0, rt0 = o32.get()
            t1, rt1 = o32.get()
            P.op("dve", lambda: nc.vector.reciprocal(out=r0[:], in_=psS[0][0][:]), reads=[psS[0][1]], writes=[rr0])
            P.op("dve", lambda: nc.vector.tensor_tensor(out=t0[:], in0=psO[0][0][:], in1=r0[:], op=ALU.mult),
                 reads=[psO[0][1], rr0], writes=[rt0])
            P.op("dve", lambda: nc.vector.reciprocal(out=t1[:], in_=psS[1][0][:]), reads=[psS[1][1]], writes=[rt1])
            P.op("dve", lambda: nc.vector.tensor_tensor(out=t1[:], in0=psO[1][0][:], in1=t1[:], op=ALU.mult),
                 reads=[psO[1][1]], writes=[rt1])
            P.op("dve", lambda: nc.vector.scalar_tensor_tensor(out=t0[:], in0=t1[:], scalar=neglam[:, 0:1], in1=t0[:],
                                                               op0=ALU.mult, op1=ALU.add), reads=[rt1, R_lam], writes=[rt0])
            sq, rsq = epool.get()
            P.op("act", lambda: nc.scalar.activation(out=sq[:], in_=t0[:], func=AF.Square), reads=[rt0], writes=[rsq])
            yield
            yield
            pp_, rpss = PSp.get()
            pss = pp_[:, 0, :]
            P.op("pe", lambda: nc.tensor.matmul(pss, lhsT=ones_b[:], rhs=sq[:], start=True, stop=True),
                 reads=[rsq, R_c], writes=[rpss])
            P.op("act", lambda: nc.scalar.activation(out=r0[:], in_=pss, func=AF.Sqrt, scale=1.0 / 128, bias=epsc[:, 0:1]),
                 reads=[rpss, R_c], writes=[rr0])
            P.op("dve", lambda: nc.vector.reciprocal(out=r0[:], in_=r0[:]), reads=[rr0], writes=[rr0])
            P.op("dve", lambda: nc.vector.scalar_tensor_tensor(out=BIGA[:, h, q0:q0 + TB], in0=t0[:], scalar=subsc[:, 0:1], in1=r0[:],
                                                               op0=ALU.mult, op1=ALU.mult), reads=[rt0, rr0, R_lam], writes=[R_A[h]])

        def att_gens():
            yield att_load(0)
            for h in range(8):
                for qb in range(NTB):
                    for j in range(4 * qb + 4):
                        yield att_unit(h, qb, j)
                    yield att_final(h, qb)
                    if qb == 1 and h + 1 < 8:
                        yield att_load(h + 1)
        run_pipe(att_gens())

        wpool3 = TPool(nc, "wt3", [128, 8, 512], BF16, 2, arena=AR)

        def proj_to_br(wname, row_base, wp):
            for grp in range(2):
                wt, rwt = wp.get()
                P.dma(wt[:], W[wname].ap()[l][:, grp * 512:(grp + 1) * 512].rearrange("(kc p) n -> p kc n", p=128),
                      writes=[rwt], q="pool")
                for ci in range(4):
                    for tb in range(NTB):
                        ps, rps = mm_fm(wt, rwt, ci, tb)
                        o, ro = b16p.get()
                        if tb % 2 == 0:
                            P.op("act", lambda: nc.scalar.copy(out=o[:], in_=ps[:]), reads=[rps], writes=[ro])
                        else:
                            P.op("dve", lambda: nc.vector.tensor_copy(out=o[:], in_=ps[:]), reads=[rps], writes=[ro])
                        r0_ = row_base + (grp * 4 + ci) * 128
                        P.dma(br_s.ap()[r0_:r0_ + 128, tb * TB:(tb + 1) * TB], o[:], reads=[ro], writes=[R_br])
        proj_to_br("w_attn_out", 0, wpool3)

        s5_stage(l)

        P.barrier()
        AR.reset()
        diag = BIGA[:, :, 0:31 * 128].rearrange("p a (k c) -> p a k c", c=128)
        R_diag = Res("diag")
        for kc in range(8):
            for k in range(31):
                P.op("dve", lambda: nc.vector.tensor_scalar(out=diag[:, kc, k, :], in0=ident_f[:],
                                                            scalar1=vec[:, V_CW + kc * 31 + k:V_CW + kc * 31 + k + 1], scalar2=None,
                                                            op0=ALU.mult), reads=[R_vec, R_c], writes=[R_diag])
        wc = AR.alloc([128, 8, 1024], BF16)
        R_wc = Res("wc")
        P.dma(wc[:], W["conv_w_out"].ap()[l].rearrange("(kc p) n -> p kc n", p=128), writes=[R_wc], q="pool")
        cin = TPool(nc, "cin", [128, 544], BF16, 3, arena=AR)
        cvp = TPool(nc, "cv", [128, 8, 512], F32, 2, arena=AR)
        xbp = TPool(nc, "xb", [128, 8, 512], BF16, 2, arena=AR)
        sqp5 = TPool(nc, "sq5", [128, 8, 512], BF16, 2, arena=AR)
        ynp = TPool(nc, "yn", [128, 8, 512], BF16, 2, arena=AR)
        def conv_tb(tb):
            cv, rcv = cvp.get()
            xb, rxb = xbp.get()
            sq, rsq = sqp5.get()
            for kc in range(8):
                ct, rct = cin.get()
                if tb == 0:
                    P.op("pool", lambda: nc.gpsimd.memset(ct[:, 0:30], 0.0), writes=[rct])
                    P.dma(ct[:, 30:542], cT_s.ap()[kc * 128:(kc + 1) * 128, 0:512], reads=[R_cT], writes=[rct])
                else:
                    P.dma(ct[:, 0:542], cT_s.ap()[kc * 128:(kc + 1) * 128, tb * TB - 30:tb * TB + 512], reads=[R_cT], writes=[rct])
                ps, rps = PS.get()
                for k in range(31):
                    P.op("pe", lambda: nc.tensor.matmul(ps[:], lhsT=diag[:, kc, k, :], rhs=ct[:, k:k + 512], start=(k == 0), stop=(k == 30)),
                         reads=[R_diag, rct], writes=[rps])
                P.op("act", lambda: nc.scalar.activation(out=cv[:, kc, :], in_=ps[:], func=AF.Identity, bias=vec[:, V_CB + kc:V_CB + kc + 1]),
                     reads=[rps, R_vec], writes=[rcv])
                P.op("dve", lambda: nc.vector.tensor_copy(out=xb[:, kc, :], in_=cv[:, kc, :]), reads=[rcv], writes=[rxb])
                P.op("act", lambda: nc.scalar.activation(out=sq[:, kc, :], in_=cv[:, kc, :], func=AF.Square), reads=[rcv], writes=[rsq])
            yield
            ps1, rps1 = PS.get()
            ps2, rps2 = PS.get()
            for kc in range(8):
                P.op("pe", lambda: nc.tensor.matmul(ps1[:], lhsT=ones_b[:], rhs=xb[:, kc, :], start=(kc == 0), stop=(kc == 7)),
                     reads=[rxb, R_c], writes=[rps1])
            for kc in range(8):
                P.op("pe", lambda: nc.tensor.matmul(ps2[:], lhsT=ones_b[:], rhs=sq[:, kc, :], start=(kc == 0), stop=(kc == 7)),
                     reads=[rsq, R_c], writes=[rps2])
            mean, rmean = f32p.get()
            msq, rmsq = f32p.get()
            rs, rrs = f32p.get()
            P.op("act", lambda: nc.scalar.mul(out=mean[:], in_=ps1[:], mul=1.0 / D), reads=[rps1], writes=[rmean])
            P.op("dve", lambda: nc.vector.tensor_tensor(out=msq[:], in0=mean[:], in1=mean[:], op=ALU.mult), reads=[rmean], writes=[rmsq])
            P.op("dve", lambda: nc.vector.scalar_tensor_tensor(out=rs[:], in0=ps2[:], scalar=1.0 / D, in1=msq[:], op0=ALU.mult,
                                                               op1=ALU.subtract), reads=[rps2, rmsq], writes=[rrs])
            P.op("act", lambda: nc.scalar.activation(out=rs[:], in_=rs[:], func=AF.Sqrt, bias=epsc[:, 0:1]), reads=[rrs, R_c], writes=[rrs])
            P.op("dve", lambda: nc.vector.reciprocal(out=rs[:], in_=rs[:]), reads=[rrs], writes=[rrs])
            yn, ryn = ynp.get()
            for kc in range(8):
                P.op("dve", lambda: nc.vector.tensor_tensor(out=cv[:, kc, :], in0=cv[:, kc, :], in1=mean[:], op=ALU.subtract),
                     reads=[rmean], writes=[rcv])
                P.op("dve", lambda: nc.vector.tensor_tensor(out=cv[:, kc, :], in0=cv[:, kc, :], in1=rs[:], op=ALU.mult),
                     reads=[rrs], writes=[rcv])
                P.op("act", lambda: nc.scalar.activation(out=yn[:, kc, :], in_=cv[:, kc, :], func=AF.Silu,
                                                         scale=vec[:, V_LNG + kc:V_LNG + kc + 1], bias=vec[:, V_LNB + kc:V_LNB + kc + 1]),
                     reads=[rcv, R_vec], writes=[ryn])
            yield
            for co in range(8):
                ps, rps = PS.get()
                for kc in range(8):
                    P.op("pe", lambda: nc.tensor.matmul(ps[:], lhsT=wc[:, kc, co * 128:(co + 1) * 128], rhs=yn[:, kc, :],
                                                        start=(kc == 0), stop=(kc == 7)), reads=[R_wc, ryn], writes=[rps])
                o, ro = b16p.get()
                if co % 2 == 0:
                    P.op("act", lambda: nc.scalar.copy(out=o[:], in_=ps[:]), reads=[rps], writes=[ro])
                else:
                    P.op("dve", lambda: nc.vector.tensor_copy(out=o[:], in_=ps[:]), reads=[rps], writes=[ro])
                r0_ = 2 * D + co * 128
                P.dma(br_s.ap()[r0_:r0_ + 128, tb * TB:(tb + 1) * TB], o[:], reads=[ro], writes=[R_br])

        run_pipe(conv_tb(tb) for tb in range(NTB))

        P.barrier()
        AR.reset()
        wo = AR.alloc([128, 8, 1024], BF16)
        R_wo = Res("wo")
        P.dma(wo[:], W["w_out"].ap()[l].rearrange("(kc p) n -> p kc n", p=128), writes=[R_wo], q="pool")
        inp6 = TPool(nc, "in6", [128, 512], BF16, 12, arena=AR)
        mixp = TPool(nc, "mix", [128, 8, 512], BF16, 2, arena=AR)
        xin6 = TPool(nc, "xin6", [128, 8, 512], F32, 1, arena=AR)
        x1p = TPool(nc, "x1p", [128, 8, 512], F32, 2, arena=AR)
        sqp6 = TPool(nc, "sq6", [128, 8, 512], BF16, 1, arena=AR)
        for tb in range(NTB):
            mix, rmix = mixp.get()
            for kc in range(8):
                tl = []
                for i in range(3):
                    bt, rbt = inp6.get()
                    P.dma(bt[:], br_s.ap()[i * D + kc * 128:i * D + (kc + 1) * 128, tb * TB:(tb + 1) * TB], reads=[R_br], writes=[rbt])
                    gt, rgt = inp6.get()
                    P.dma(gt[:], gT_s.ap()[i * D + kc * 128:i * D + (kc + 1) * 128, tb * TB:(tb + 1) * TB], reads=[R_gT], writes=[rgt])
                    tl.append((bt, rbt, gt, rgt))
                ta, rta = f32p.get()
                tb2, rtb2 = f32p.get()
                P.op("pool", lambda: nc.gpsimd.tensor_tensor(out=ta[:], in0=tl[0][0][:], in1=tl[0][2][:], op=ALU.mult),
                     reads=[tl[0][1], tl[0][3]], writes=[rta])
                P.op("pool", lambda: nc.gpsimd.tensor_tensor(out=tb2[:], in0=tl[1][0][:], in1=tl[1][2][:], op=ALU.mult),
                     reads=[tl[1][1], tl[1][3]], writes=[rtb2])
                P.op("dve", lambda: nc.vector.tensor_tensor(out=ta[:], in0=ta[:], in1=tb2[:], op=ALU.add), reads=[rtb2], writes=[rta])
                P.op("pool", lambda: nc.gpsimd.tensor_tensor(out=tb2[:], in0=tl[2][0][:], in1=tl[2][2][:], op=ALU.mult),
                     reads=[tl[2][1], tl[2][3]], writes=[rtb2])
                P.op("dve", lambda: nc.vector.tensor_tensor(out=mix[:, kc, :], in0=ta[:], in1=tb2[:], op=ALU.add),
                     reads=[rta, rtb2], writes=[rmix])
            xb, rxb = xin6.get()
            P.dma(xb[:], x_src.ap()[:, tb * TB:(tb + 1) * TB].rearrange("(kc p) t -> p kc t", p=128), reads=[R_xsrc], writes=[rxb])
            x1, rx1 = x1p.get()
            for co in range(8):
                ps, rps = PS.get()
                for kc in range(8):
                    P.op("pe", lambda: nc.tensor.matmul(ps[:], lhsT=wo[:, kc, co * 128:(co + 1) * 128], rhs=mix[:, kc, :],
                                                        start=(kc == 0), stop=(kc == 7)), reads=[R_wo, rmix], writes=[rps])
                P.op("dve", lambda: nc.vector.tensor_tensor(out=x1[:, co, :], in0=ps[:], in1=xb[:, co, :], op=ALU.add),
                     reads=[rps, rxb], writes=[rx1])
            P.dma(x1T_s.ap()[:, tb * TB:(tb + 1) * TB].rearrange("(kc p) t -> p kc t", p=128), x1[:], reads=[rx1], writes=[R_x1])
            norm_block(x1, rx1, V_GFFN, tb, sqp6)

        P.barrier()
        AR.reset()
        wpool8 = TPool(nc, "wt8", [128, 8, 512], BF16, 2, arena=AR)
        ufull = TPool(nc, "uf", [128, 2 + S], BF16, 8, arena=AR)
        dgp = TPool(nc, "dg", [128, 4, 3, 128], BF16, 2, arena=AR)
        w_up2 = W["ffn_w_up"].ap()[l]
        for grp in range(11):
            wt, rwt = wpool8.get()
            P.dma(wt[:, :, 0:256], w_up2[:, grp * 256:(grp + 1) * 256].rearrange("(kc p) n -> p kc n", p=128), writes=[rwt], q="pool")
            P.dma(wt[:, :, 256:512], w_up2[:, FFN_H + grp * 256:FFN_H + (grp + 1) * 256].rearrange("(kc p) n -> p kc n", p=128),
                  writes=[rwt], q="pool")
            dgt, rdg = dgp.get()
            for ci in range(4):
                gch = (2 * grp + ci) if ci < 2 else (22 + 2 * grp + ci - 2)
                for k in range(3):
                    P.op("pool", lambda: nc.gpsimd.tensor_scalar(out=dgt[:, ci, k, :], in0=ident_f[:],
                                                                 scalar1=vec[:, V_FW + gch * 3 + k:V_FW + gch * 3 + k + 1], scalar2=None,
                                                                 op0=ALU.mult), reads=[R_vec, R_c], writes=[rdg])
            ufs = [ufull.get() for _ in range(4)]
            for ci in range(4):
                P.op("pool", lambda: nc.gpsimd.memset(ufs[ci][0][:, 0:2], 0.0), writes=[ufs[ci][1]])
                for tb in range(NTB):
                    ps, rps = mm_fm(wt, rwt, ci, tb)
                    if tb % 2 == 0:
                        P.op("act", lambda: nc.scalar.copy(out=ufs[ci][0][:, 2 + tb * TB:2 + (tb + 1) * TB], in_=ps[:]),
                             reads=[rps], writes=[ufs[ci][1]])
                    else:
                        P.op("dve", lambda: nc.vector.tensor_copy(out=ufs[ci][0][:, 2 + tb * TB:2 + (tb + 1) * TB], in_=ps[:]),
                             reads=[rps], writes=[ufs[ci][1]])
            for pi in range(2):
                for tb in range(NTB):
                    psv, rpsv = PS.get()
                    psg, rpsg = PS.get()
                    for k in range(3):
                        P.op("pe", lambda: nc.tensor.matmul(psv[:], lhsT=dgt[:, pi, k, :], rhs=ufs[pi][0][:, tb * TB + k:tb * TB + k + TB],
                                                            start=(k == 0), stop=(k == 2)), reads=[rdg, ufs[pi][1]], writes=[rpsv])
                    for k in range(3):
                        P.op("pe", lambda: nc.tensor.matmul(psg[:], lhsT=dgt[:, pi + 2, k, :], rhs=ufs[pi + 2][0][:, tb * TB + k:tb * TB + k + TB],
                                                            start=(k == 0), stop=(k == 2)), reads=[rdg, ufs[pi + 2][1]], writes=[rpsg])
                    gl, rgl = f32p.get()
                    P.op("act", lambda: nc.scalar.activation(out=gl[:], in_=psg[:], func=AF.Gelu_apprx_tanh), reads=[rpsg], writes=[rgl])
                    o, ro = b16p.get()
                    P.op("dve", lambda: nc.vector.tensor_tensor(out=o[:], in0=psv[:], in1=gl[:], op=ALU.mult), reads=[rpsv, rgl], writes=[ro])
                    r0_ = (2 * grp + pi) * 128
                    P.dma(act_s.ap()[r0_:r0_ + 128, tb * TB:(tb + 1) * TB], o[:], reads=[ro], writes=[R_act])

        P.barrier()
        AR.reset()
        wd = AR.alloc([128, 22, 1024], BF16)
        R_wd = Res("wd")
        wdd = W["ffn_w_down"].ap()[l].rearrange("(kc p) n -> p kc n", p=128)
        P.dma(wd[:, 0:11, :], wdd[:, 0:11, :], writes=[R_wd], q="pool")
        P.dma(wd[:, 11:22, :], wdd[:, 11:22, :], writes=[R_wd], q="pool")
        actp = TPool(nc, "actp", [128, 22, 512], BF16, 2, arena=AR)
        xp9 = TPool(nc, "xp9", [128, 512], F32, 4, arena=AR)
        for tb in range(NTB):
            at, rat = actp.get()
            P.dma(at[:], act_s.ap()[:, tb * TB:(tb + 1) * TB].rearrange("(kc p) t -> p kc t", p=128), reads=[R_act], writes=[rat])
            for co in range(8):
                xr, rxr = xp9.get()
                P.dma(xr[:], x1T_s.ap()[co * 128:(co + 1) * 128, tb * TB:(tb + 1) * TB], reads=[R_x1], writes=[rxr])
                ps, rps = PS.get()
                for kc in range(22):
                    P.op("pe", lambda: nc.tensor.matmul(ps[:], lhsT=wd[:, kc, co * 128:(co + 1) * 128], rhs=at[:, kc, :],
                                                        start=(kc == 0), stop=(kc == 21)), reads=[R_wd, rat], writes=[rps])
                P.op("dve", lambda: nc.vector.tensor_tensor(out=xr[:], in0=ps[:], in1=xr[:], op=ALU.add), reads=[rps], writes=[rxr])
                P.dma(x_dst.ap()[co * 128:(co + 1) * 128, tb * TB:(tb + 1) * TB], xr[:], reads=[rxr], writes=[R_xdst])

    P.finish()
    return nc


_CACHE = {}


def make_in_maps(inputs, nb=8):
    common = {}
    for nm in ("w_in", "w_attn_out", "s5_glu_w1", "s5_glu_w2", "conv_w_out", "w_out", "ffn_w_up", "ffn_w_down"):
        common[nm] = np.ascontiguousarray(inputs[nm], dtype=np.float32)
    vs, ls, ss = [], [], []
    for l in range(DEPTH):
        v, lamv, s5 = host_layer_params(inputs, l)
        vs.append(v)
        ls.append(lamv)
        ss.append(s5)
    common["vecs"] = np.stack(vs)
    common["lamv"] = np.stack(ls)
    common["s5p"] = np.stack(ss)
    common["rel_bias"] = np.ascontiguousarray(inputs["rel_bias"], dtype=np.float32)
    common.update({"c_" + k: v for k, v in host_consts().items()})
    x = inputs["x"]
    in_maps = []
    for b in range(nb):
        m = dict(common)
        m["xT"] = np.ascontiguousarray(x[b].T)
        in_maps.append(m)
    return in_maps


def kernel(**inputs):
    inputs = {k: np.asarray(v) for k, v in inputs.items()}
    if "nc" not in _CACHE:
        _CACHE["nc"] = build()
    nc = _CACHE["nc"]
    in_maps = make_in_maps(inputs)
    res = run_bass_kernel_spmd(nc, in_maps, core_ids=list(range(8)))
    out = np.stack([np.ascontiguousarray(r["yT"].T) for r in res.results], 0)
    return out.astype(np.float32)
```

```python
import math
import numpy as np
import ml_dtypes
import concourse.bass as bass
import concourse.mybir as mybir
from concourse.bass_utils import run_bass_kernel_spmd

F32 = mybir.dt.float32
BF16 = mybir.dt.bfloat16
I32 = mybir.dt.int32
ALU = mybir.AluOpType
AF = mybir.ActivationFunctionType
AX = mybir.AxisListType

D = 1024
S = 4096
DEPTH = 2
TB = 512
NTB = S // TB
FFN_H = 2816
IN_W = 9216
OFF_Q, OFF_K, OFF_V, OFF_U, OFF_C, OFF_G = 0, 1024, 2048, 3072, 4096, 6144
EPS = 1e-6
GC = 1.5957691216057308


class Res:
    __slots__ = ("name", "lw", "rd")

    def __init__(self, name=""):
        self.name = name
        self.lw = None
        self.rd = []


class Prog:
    ENGS = ("pe", "dve", "act", "pool", "sp")

    def __init__(self, nc, n_dma_sems=32):
        self.nc = nc
        self.eng = {"pe": nc.tensor, "dve": nc.vector, "act": nc.scalar,
                    "pool": nc.gpsimd, "sp": nc.sync}
        self.sems = {}
        self.cnt = {}
        for e in self.ENGS:
            self.sems[e] = nc.alloc_semaphore("c_" + e)
            self.cnt[e] = 0
        self.n_dma = n_dma_sems
        for i in range(n_dma_sems):
            k = "d%d" % i
            self.sems[k] = nc.alloc_semaphore("s_" + k)
            self.cnt[k] = 0
        self.dma_rr = 0
        self.known = {e: {} for e in self.ENGS}
        self.ninstr = 0

    def _deps(self, reads, writes):
        deps = {}

        def add(t):
            if t is None:
                return
            k, v = t
            if deps.get(k, 0) < v:
                deps[k] = v
        for r in reads:
            add(r.lw)
        for w in writes:
            add(w.lw)
            for t in w.rd:
                add(t)
        return deps

    def _wait(self, e, deps):
        kn = self.known[e]
        for k, v in deps.items():
            if k == e and e == "pe":
                continue
            if kn.get(k, 0) >= v:
                continue
            self.eng[e].wait_ge(self.sems[k], v)
            kn[k] = v

    def _commit(self, tok, reads, writes):
        for r in reads:
            r.rd.append(tok)
            if len(r.rd) > 48:
                m = {}
                for k, v in r.rd:
                    if m.get(k, 0) < v:
                        m[k] = v
                r.rd = list(m.items())
        for w in writes:
            w.lw = tok
            w.rd = []

    def op(self, e, fn, reads=(), writes=()):
        deps = self._deps(reads, writes)
        self._wait(e, deps)
        ins = fn()
        self.cnt[e] += 1
        ins.then_inc(self.sems[e], 1)
        self._commit((e, self.cnt[e]), reads, writes)
        self.ninstr += 1
        return ins

    def dma(self, out, in_, reads=(), writes=(), q="sp", **kw):
        deps = self._deps(reads, writes)
        self._wait(q, deps)
        k = "d%d" % self.dma_rr
        self.dma_rr = (self.dma_rr + 1) % self.n_dma
        self._wait(q, {k: self.cnt[k]})
        ins = self.eng[q].dma_start(out=out, in_=in_, **kw)
        self.cnt[k] += 16
        ins.then_inc(self.sems[k], 16)
        self._commit((k, self.cnt[k]), reads, writes)
        self.ninstr += 1
        return ins

    def barrier(self):
        allv = {k: v for k, v in self.cnt.items() if v > 0}
        for e in self.ENGS:
            self._wait(e, allv)

    def finish(self, e="sp"):
        allv = {k: v for k, v in self.cnt.items() if v > 0}
        self._wait(e, allv)


class Arena:
    def __init__(self, t, nelem):
        self.t = t
        self.n = nelem
        self.off = 0

    def reset(self):
        self.off = 0

    def alloc(self, shape, dt):
        nel = 1
        for d in shape[1:]:
            nel *= d
        nb = nel * (2 if dt == BF16 else 4)
        nb = (nb + 31) // 32 * 32
        assert self.off + nb // 2 <= self.n, ("arena overflow", self.off, nb, self.n)
        ap = self.t[:, self.off:self.off + nb // 2]
        self.off += nb // 2
        if dt != BF16:
            ap = ap.bitcast(dt)
        ap = ap[:, 0:nel]
        if len(shape) == 3:
            ap = ap.rearrange("p (a b) -> p a b", a=shape[1])
        elif len(shape) == 4:
            ap = ap.rearrange("p (a b c) -> p a b c", a=shape[1], b=shape[2])
        return ap


class SubPool:
    def __init__(self, items):
        self.t = list(items)
        self.i = 0

    def get(self):
        r = self.t[self.i]
        self.i = (self.i + 1) % len(self.t)
        return r


class TPool:
    def __init__(self, nc, name, shape, dt, n, psum=False, arena=None):
        self.t = []
        for i in range(n):
            if psum:
                h = nc.alloc_psum_tensor("%s%d" % (name, i), shape, dt)
            elif arena is not None:
                h = arena.alloc(shape, dt)
            else:
                h = nc.alloc_sbuf_tensor("%s%d" % (name, i), shape, dt)
            self.t.append((h, Res("%s%d" % (name, i))))
        self.i = 0

    def get(self):
        r = self.t[self.i]
        self.i = (self.i + 1) % len(self.t)
        return r


def run_pipe(gens):
    active = []

    def step():
        nxt = []
        for a in active:
            try:
                next(a)
                nxt.append(a)
            except StopIteration:
                pass
        active[:] = nxt
    for g in gens:
        active.append(g)
        step()
    while active:
        step()


def t5_bucket_np(rel):
    nb = 16
    n = -rel
    ret = np.where(n < 0, nb, 0)
    n = np.abs(n)
    max_exact = nb // 2
    nf = np.maximum(n, 1).astype(np.float32)
    large = max_exact + (np.log(nf / max_exact) / math.log(128 / max_exact) * (nb - max_exact)).astype(np.int32)
    large = np.minimum(large, nb - 1)
    return ret + np.where(n < max_exact, n, large)


def host_consts():
    c = {}
    c["ident_f"] = np.eye(128, dtype=np.float32)
    jsw = np.zeros((128, 128), np.float32)
    for i in range(128):
        jsw[i, (i + 64) % 128] = 1.0
    c["jswap_f"] = jsw
    c["jrev_f"] = np.eye(128, dtype=np.float32)[::-1].copy()
    bd = np.zeros((128, 128), np.float32)
    bd[:64, :64] = 1.0
    bd[64:, 64:] = 1.0
    c["bdones_f"] = bd
    m = np.arange(384)
    rel = 127 - m
    bk = t5_bucket_np(rel)
    oh = np.zeros((32, 384), np.float32)
    oh[bk, m] = 1.0
    oh[15, :] -= 1.0
    oh[:, 383] = 0.0
    c["onehot"] = oh
    assert np.all(t5_bucket_np(-np.arange(129, 4096)) == 15)
    mk = np.ones((128, 256), np.float32)
    mk[64:, :64] = 0.0
    c["bandmask"] = mk
    ii = np.arange(128) // 16
    c["mask_intra"] = (ii[None, :] >= ii[:, None]).astype(np.float32)
    sg = np.zeros((128, 2), np.float32)
    sg[:64, 0] = -1.0
    sg[64:, 0] = 1.0
    sg[:64, 1] = 1.0
    sg[64:, 1] = -1.0
    c["sig"] = sg
    return c


NVEC = 8 + 8 + 1 + 1 + 1 + 8 + 8 + 8 + 8 * 31 + 44 * 3
V_GMIX, V_GFFN, V_QG, V_KG, V_SUB, V_CB, V_LNG, V_LNB, V_CW, V_FW = 0, 8, 16, 17, 18, 19, 27, 35, 43, 43 + 248


def host_layer_params(inp, l):
    v = np.zeros((128, NVEC), np.float32)
    v[:, V_GMIX:V_GMIX + 8] = inp["norm_mix"][l].reshape(8, 128).T
    v[:, V_GFFN:V_GFFN + 8] = inp["norm_ffn"][l].reshape(8, 128).T
    v[:, V_QG] = np.tile(inp["qk_gain_q"][l], 2)
    v[:, V_KG] = np.tile(inp["qk_gain_k"][l], 2)
    v[:, V_SUB] = inp["diff_subln"][l]
    v[:, V_CB:V_CB + 8] = inp["conv_dw_b"][l].reshape(8, 128).T
    v[:, V_LNG:V_LNG + 8] = inp["conv_ln_g"][l].reshape(8, 128).T
    v[:, V_LNB:V_LNB + 8] = inp["conv_ln_b"][l].reshape(8, 128).T
    v[:, V_CW:V_CW + 248] = inp["conv_dw_w"][l].reshape(31, 8, 128).transpose(2, 1, 0).reshape(128, 248)
    v[:, V_FW:V_FW + 132] = inp["ffn_dw_w"][l].reshape(3, 44, 128).transpose(2, 1, 0).reshape(128, 132)
    lamv = np.concatenate([inp["lambda_q1"][l], inp["lambda_k1"][l], inp["lambda_q2"][l], inp["lambda_k2"][l]])
    lamv = np.broadcast_to(lamv[None, :], (128, 256)).copy()
    s5 = np.zeros((128, 64 * 4 + 4 * 64 * 16), np.float32)
    lr = inp["s5_lambda_re"][l].T
    li = inp["s5_lambda_im"][l].T
    s5[:, 0:64] = np.concatenate([lr, lr], 0)
    s5[:, 64:128] = np.concatenate([li, li], 0)
    s5[:, 128:192] = np.broadcast_to(inp["s5_log_step"][l][None, :], (128, 64))
    s5[:, 192:256] = np.tile(inp["s5_d"][l].T, (8, 1))
    br = inp["s5_b_re"][l].transpose(1, 0, 2).reshape(64, 1024)
    bi = inp["s5_b_im"][l].transpose(1, 0, 2).reshape(64, 1024)
    cr = inp["s5_c_re"][l].transpose(2, 0, 1).reshape(64, 1024)
    ci = inp["s5_c_im"][l].transpose(2, 0, 1).reshape(64, 1024)
    s5[:, 256:1280] = np.concatenate([br, bi], 0)
    s5[:, 1280:2304] = np.concatenate([bi, br], 0)
    s5[:, 2304:3328] = np.concatenate([cr, ci], 0)
    s5[:, 3328:4352] = np.concatenate([ci, cr], 0)
    return v, lamv, s5


def build(dbg=None, nlayers=DEPTH):
    nc = bass.Bass("TRN2", target_bir_lowering=False)
    P = Prog(nc)
    dbg = dbg or set()

    def dram_in(name, shape, dt=F32):
        return nc.dram_tensor(name, list(shape), dt, kind="ExternalInput")

    def scratch(name, shape, dt):
        kind = "ExternalOutput" if name in dbg else "Internal"
        return nc.dram_tensor(name, list(shape), dt, kind=kind)

    xT_in = dram_in("xT", [D, S])
    W = {}
    for nm, shp in [("w_in", [DEPTH, D, IN_W]), ("w_attn_out", [DEPTH, D, D]), ("s5_glu_w1", [DEPTH, D, D]),
                    ("s5_glu_w2", [DEPTH, D, D]), ("conv_w_out", [DEPTH, D, D]), ("w_out", [DEPTH, D, D]),
                    ("ffn_w_up", [DEPTH, D, 2 * FFN_H]), ("ffn_w_down", [DEPTH, FFN_H, D])]:
        W[nm] = dram_in(nm, shp)
    vecs_in = dram_in("vecs", [DEPTH, 128, NVEC])
    lamv_in = dram_in("lamv", [DEPTH, 128, 256])
    s5p_in = dram_in("s5p", [DEPTH, 128, 4352])
    relb_in = dram_in("rel_bias", [32, 8])
    W_names = list(W.keys())
    cst = {}
    for nm, shp in [("ident_f", [128, 128]), ("jswap_f", [128, 128]), ("jrev_f", [128, 128]), ("bdones_f", [128, 128]),
                    ("onehot", [32, 384]), ("bandmask", [128, 256]), ("mask_intra", [128, 128]), ("sig", [128, 2])]:
        cst[nm] = dram_in("c_" + nm, shp)
    yT_out = nc.dram_tensor("yT", [D, S], F32, kind="ExternalOutput")

    qT_s = scratch("qT_s", [D, S], BF16)
    kT_s = scratch("kT_s", [D, S], BF16)
    v_s = scratch("v_s", [S, D], BF16)
    u_s = scratch("u_s", [512, 8192], BF16)
    cT_s = scratch("cT_s", [D, S], BF16)
    gT_s = scratch("gT_s", [3 * D, S], BF16)
    br_s = scratch("br_s", [3 * D, S], BF16)
    x1T_s = scratch("x1T_s", [D, S], F32)
    xmidT_s = scratch("xmidT_s", [D, S], F32)
    act_s = scratch("act_s", [FFN_H, S], BF16)
    wr_d = scratch("wr_d", [8, 384], F32)
    R_qT, R_kT, R_v, R_u, R_cT, R_gT, R_br, R_x1, R_xmid, R_act, R_wr = [Res(n) for n in
        ("qT", "kT", "v", "u", "cT", "gT", "br", "x1", "xmid", "act", "wr")]

    sb = nc.alloc_sbuf_tensor
    R_c = Res("consts")
    ident_f = sb("ident_f", [128, 128], F32)
    jswap_f = sb("jswap_f", [128, 128], F32)
    jrev_f = sb("jrev_f", [128, 128], F32)
    mask_intra = sb("mask_intra", [128, 128], F32)
    bandmask = sb("bandmask", [128, 256], F32)
    sig = sb("sig", [128, 2], F32)
    ident_b = sb("ident_b", [128, 128], BF16)
    ones_b = sb("ones_b", [128, 128], BF16)
    bdones_b = sb("bdones_b", [128, 128], BF16)
    epsc = sb("epsc", [128, 1], F32)
    onehot = sb("onehot", [32, 384], F32)
    relb = sb("relb", [32, 8], F32)
    expB = sb("expB", [128, 8, 256], BF16)
    for t_, nm in [(ident_f, "ident_f"), (jswap_f, "jswap_f"), (jrev_f, "jrev_f"), (mask_intra, "mask_intra"),
                   (bandmask, "bandmask"), (sig, "sig"), (onehot, "onehot")]:
        P.dma(t_[:], cst[nm].ap(), writes=[R_c])
    P.dma(relb[:], relb_in.ap(), writes=[R_c])
    P.dma(bdones_b[:], cst["bdones_f"].ap(), writes=[R_c], q="pool")
    P.op("dve", lambda: nc.vector.tensor_copy(out=ident_b[:], in_=ident_f[:]), reads=[R_c], writes=[R_c])
    P.op("dve", lambda: nc.vector.memset(ones_b[:], 1.0), writes=[R_c])
    P.op("dve", lambda: nc.vector.memset(epsc[:], EPS), writes=[R_c])

    BIGA = sb("BIGA", [128, 8, S], BF16)
    R_A = [Res("A%d" % i) for i in range(8)]
    ARENA_N = 55296
    AR = Arena(sb("ARENA", [128, ARENA_N], BF16), ARENA_N)
    vec = sb("vec", [128, NVEC], F32)
    R_vec = Res("vec")
    lam4 = sb("lam4", [128, 256], F32)
    neglam = sb("neglam", [128, 1], F32)
    subsc = sb("subsc", [128, 1], F32)
    R_lam = Res("lam")

    PSALL = nc.alloc_psum_tensor("psall", [128, 4096], F32)
    PS = SubPool([(PSALL[:, i * 512:(i + 1) * 512], Res("ps%d" % i)) for i in range(8)])
    PSPAIR = [(PSALL[:, 2048:3072].rearrange("p (c n) -> p c n", c=2), Res("pp0")),
              (PSALL[:, 3072:4096].rearrange("p (c n) -> p c n", c=2), Res("pp1"))]
    f32p = TPool(nc, "f32t", [128, 512], F32, 6)
    b16p = TPool(nc, "b16t", [128, 512], BF16, 8)
    wpool = TPool(nc, "wt", [128, 8, 512], BF16, 2, arena=AR)
    xin = TPool(nc, "xin", [128, 8, 512], F32, 2, arena=AR)
    sqp = TPool(nc, "sq", [128, 8, 512], BF16, 1, arena=AR)
    ustage = TPool(nc, "ust", [128, 32, 8, 16], BF16, 2, arena=AR)

    ps, rps = PS.get()
    P.op("pe", lambda: nc.tensor.matmul(ps[0:8, 0:384], lhsT=relb[:], rhs=onehot[:], start=True, stop=True),
         reads=[R_c], writes=[rps])
    wr_sb, r_wr_sb = f32p.get()
    P.op("act", lambda: nc.scalar.copy(out=wr_sb[0:8, 0:384], in_=ps[0:8, 0:384]), reads=[rps], writes=[r_wr_sb])
    P.dma(wr_d.ap(), wr_sb[0:8, 0:384], reads=[r_wr_sb], writes=[R_wr])
    for h in range(8):
        xt_, rxt = f32p.get()
        P.dma(xt_[:, 0:256], bass.AP(tensor=wr_d, offset=384 * h, ap=[[1, 128], [1, 256]]), reads=[R_wr], writes=[rxt])
        ps, rps = PS.get()
        P.op("pe", lambda: nc.tensor.matmul(ps[:, 0:256], lhsT=jrev_f[:], rhs=xt_[:, 0:256], start=True, stop=True),
             reads=[R_c, rxt], writes=[rps])
        et_, ret = f32p.get()
        P.op("act", lambda: nc.scalar.activation(out=et_[:, 0:256], in_=ps[:, 0:256], func=AF.Exp), reads=[rps], writes=[ret])
        P.op("dve", lambda: nc.vector.tensor_tensor(out=expB[:, h, :], in0=et_[:, 0:256], in1=bandmask[:], op=ALU.mult),
             reads=[ret, R_c], writes=[R_c])

    def rsqrt_from(ps_ap, scale, n=512, width=None):
        t, rt = f32p.get()
        return t, rt

    def norm_stage(src_dram, R_src, gcol, keep=None):
        for tb in range(NTB):
            xb, rxb = xin.get()
            P.dma(xb[:], src_dram.ap()[:, tb * TB:(tb + 1) * TB].rearrange("(kc p) t -> p kc t", p=128),
                  reads=[R_src], writes=[rxb])
            norm_block(xb, rxb, gcol, tb, sqp)

    def norm_block(xb, rxb, gcol, tb, sqpool):
        sq, rsq = sqpool.get()
        P.op("act", lambda: nc.scalar.activation(out=sq[:], in_=xb[:], func=AF.Square), reads=[rxb], writes=[rsq])
        ps, rps = PS.get()
        for kc in range(8):
            P.op("pe", lambda: nc.tensor.matmul(ps[:], lhsT=ones_b[:], rhs=sq[:, kc, :], start=(kc == 0), stop=(kc == 7)),
                 reads=[rsq, R_c], writes=[rps])
        rs, rrs = f32p.get()
        P.op("act", lambda: nc.scalar.activation(out=rs[:], in_=ps[:], func=AF.Sqrt, scale=1.0 / D, bias=epsc[:, 0:1]),
             reads=[rps, R_c], writes=[rrs])
        P.op("dve", lambda: nc.vector.reciprocal(out=rs[:], in_=rs[:]), reads=[rrs], writes=[rrs])
        for kc in range(8):
            P.op("dve", lambda: nc.vector.scalar_tensor_tensor(
                out=BIGA[:, kc, tb * TB:(tb + 1) * TB], in0=xb[:, kc, :], scalar=vec[:, gcol + kc:gcol + kc + 1],
                in1=rs[:], op0=ALU.mult, op1=ALU.mult), reads=[rxb, rrs, R_vec], writes=[R_A[kc]])

    def load_w(wdram2d, offs, KC=8):
        wt, rwt = wpool.get()
        pos = 0
        for off, n in offs:
            P.dma(wt[:, 0:KC, pos:pos + n], wdram2d[:, off:off + n].rearrange("(kc p) n -> p kc n", p=128),
                  writes=[rwt], q="pool")
            pos += n
        return wt, rwt

    def mm_fm(wt, rwt, ci, tb, KC=8, src=None, rsrc=None):
        ps, rps = PS.get()
        for kc in range(KC):
            P.op("pe", lambda: nc.tensor.matmul(ps[:], lhsT=wt[:, kc, ci * 128:(ci + 1) * 128],
                                                rhs=BIGA[:, kc, tb * TB:(tb + 1) * TB], start=(kc == 0), stop=(kc == KC - 1)),
                 reads=[rwt, R_A[kc]], writes=[rps])
        return ps, rps

    def qk_tile(wt, rwt, ci, gcol, dst, R_dst, row0, tb):
        ps, rps = mm_fm(wt, rwt, ci, tb)
        sq, rsq = b16p.get()
        P.op("act", lambda: nc.scalar.activation(out=sq[:], in_=ps[:], func=AF.Square), reads=[rps], writes=[rsq])
        yield
        ps2, rps2 = PS.get()
        P.op("pe", lambda: nc.tensor.matmul(ps2[:], lhsT=bdones_b[:], rhs=sq[:], start=True, stop=True),
             reads=[rsq, R_c], writes=[rps2])
        rs, rrs = f32p.get()
        P.op("act", lambda: nc.scalar.activation(out=rs[:], in_=ps2[:], func=AF.Sqrt, scale=1.0 / 64, bias=epsc[:, 0:1]),
             reads=[rps2, R_c], writes=[rrs])
        P.op("dve", lambda: nc.vector.reciprocal(out=rs[:], in_=rs[:]), reads=[rrs], writes=[rrs])
        o, ro = b16p.get()
        P.op("dve", lambda: nc.vector.scalar_tensor_tensor(out=o[:], in0=ps[:], scalar=vec[:, gcol:gcol + 1], in1=rs[:],
                                                           op0=ALU.mult, op1=ALU.mult), reads=[rps, rrs, R_vec], writes=[ro])
        P.dma(dst.ap()[row0:row0 + 128, tb * TB:(tb + 1) * TB], o[:], reads=[ro], writes=[R_dst])


    y_s = scratch("y_s", [S, D], BF16)
    R_ys = Res("ys")

    def s5_stage(l):
        P.barrier()
        AR.reset()
        R_sp = Res("sp")
        sp = AR.alloc([128, 4352], F32)
        P.dma(sp[:], s5p_in.ap()[l], writes=[R_sp])
        lr, li, lstep, drep = sp[:, 0:64], sp[:, 64:128], sp[:, 128:192], sp[:, 192:256]
        T1, T2, CTa, CTb = sp[:, 256:1280], sp[:, 1280:2304], sp[:, 2304:3328], sp[:, 3328:4352]
        SL = AR.alloc([128, 20, 64], F32)
        KI = AR.alloc([128, 64], I32)
        PWr = AR.alloc([128, 16, 64], F32)
        PWi = AR.alloc([128, 16, 64], F32)
        AKr = AR.alloc([128, 9, 64], F32)
        AKi = AR.alloc([128, 9, 64], F32)
        AKs = AR.alloc([128, 9, 64], F32)
        PRr = AR.alloc([128, 15, 64], F32)
        PRi = AR.alloc([128, 15, 64], F32)
        QRr = AR.alloc([128, 9, 64], F32)
        QRi = AR.alloc([128, 9, 64], F32)
        BT1 = AR.alloc([128, 1024], F32)
        BT2 = AR.alloc([128, 1024], F32)
        tA = AR.alloc([128, 1024], F32)
        tB = AR.alloc([128, 1024], F32)
        rw = dict(reads=[R_sp, R_c], writes=[R_sp])

        def tt(o, a, b, op):
            P.op("dve", lambda: nc.vector.tensor_tensor(out=o, in0=a, in1=b, op=op), **rw)

        def ts(o, a, s1, op0, s2=None, op1=None):
            if op1 is None:
                P.op("dve", lambda: nc.vector.tensor_scalar(out=o, in0=a, scalar1=s1, scalar2=None, op0=op0), **rw)
            else:
                P.op("dve", lambda: nc.vector.tensor_scalar(out=o, in0=a, scalar1=s1, scalar2=s2, op0=op0, op1=op1), **rw)

        def act(o, a, func, scale=1.0):
            P.op("act", lambda: nc.scalar.activation(out=o, in_=a, func=func, scale=scale), **rw)

        def cp(o, a):
            P.op("dve", lambda: nc.vector.tensor_copy(out=o, in_=a), **rw)

        sl = lambda i: SL[:, i, :]
        dl, mag, ang, tu, kf, r1, rr, mm_, sinv, cosv, ar, ai, den, nr, fr, fi, t1, t2, sfi, nsfi = [sl(i) for i in range(20)]
        act(dl, lstep, AF.Exp)
        tt(t1, lr, dl, ALU.mult)
        act(mag, t1, AF.Exp)
        tt(ang, li, dl, ALU.mult)
        ts(tu, ang, 1.0 / (2 * math.pi), ALU.mult)
        cp(KI[:], tu)
        cp(kf, KI[:])
        tt(r1, tu, kf, ALU.subtract)

        def wrap_sin(dst, src, shift):
            ts(rr, src, shift, ALU.add)
            ts(mm_, rr, 0.5, ALU.is_gt)
            tt(rr, rr, mm_, ALU.subtract)
            ts(mm_, rr, 0.5, ALU.is_gt)
            tt(rr, rr, mm_, ALU.subtract)
            ts(mm_, rr, -0.5, ALU.is_lt)
            tt(rr, rr, mm_, ALU.add)
            ts(mm_, rr, -0.5, ALU.is_lt)
            tt(rr, rr, mm_, ALU.add)
            act(dst, rr, AF.Sin, scale=2 * math.pi)
        wrap_sin(sinv, r1, 0.0)
        wrap_sin(cosv, r1, 0.25)
        tt(ar, mag, cosv, ALU.mult)
        tt(ai, mag, sinv, ALU.mult)
        tt(t1, lr, lr, ALU.mult)
        tt(t2, li, li, ALU.mult)
        tt(den, t1, t2, ALU.add)
        P.op("dve", lambda: nc.vector.reciprocal(out=den, in_=den), **rw)
        ts(nr, ar, -1.0, ALU.add)
        tt(t1, nr, lr, ALU.mult)
        tt(t2, ai, li, ALU.mult)
        tt(t1, t1, t2, ALU.add)
        tt(fr, t1, den, ALU.mult)
        tt(t1, ai, lr, ALU.mult)
        tt(t2, nr, li, ALU.mult)
        tt(t1, t1, t2, ALU.subtract)
        tt(fi, t1, den, ALU.mult)
        ts(sfi, fi, sig[:, 0:1], ALU.mult)
        ts(nsfi, fi, sig[:, 1:2], ALU.mult)
        g3 = lambda a: a.rearrange("p (g h) -> p g h", h=16)
        bc3 = lambda a: a.unsqueeze(2).broadcast_to([128, 64, 16])
        tt(g3(tA[:]), bc3(fr), g3(T1), ALU.mult)
        tt(g3(tB[:]), bc3(sfi), g3(T2), ALU.mult)
        tt(BT1[:], tA[:], tB[:], ALU.add)
        tt(g3(tA[:]), bc3(fr), g3(T2), ALU.mult)
        tt(g3(tB[:]), bc3(nsfi), g3(T1), ALU.mult)
        tt(BT2[:], tA[:], tB[:], ALU.add)

        def cmul(orr, oi, xr, xi, yr, yi):
            tt(t1, xr, yr, ALU.mult)
            tt(t2, xi, yi, ALU.mult)
            tt(orr, t1, t2, ALU.subtract)
            tt(t1, xr, yi, ALU.mult)
            tt(t2, xi, yr, ALU.mult)
            tt(oi, t1, t2, ALU.add)
        P.op("dve", lambda: nc.vector.memset(PWr[:, 7, :], 1.0), **rw)
        P.op("dve", lambda: nc.vector.memset(PWi[:, 7, :], 0.0), **rw)
        cp(PWr[:, 8, :], ar)
        cp(PWi[:, 8, :], ai)
        for n in range(2, 9):
            cmul(PWr[:, 7 + n, :], PWi[:, 7 + n, :], PWr[:, 6 + n, :], PWi[:, 6 + n, :], PWr[:, 8, :], PWi[:, 8, :])
        tt(t1, ar, ar, ALU.mult)
        tt(t2, ai, ai, ALU.mult)
        tt(den, t1, t2, ALU.add)
        P.op("dve", lambda: nc.vector.reciprocal(out=den, in_=den), **rw)
        tt(PWr[:, 6, :], ar, den, ALU.mult)
        tt(t1, ai, den, ALU.mult)
        ts(PWi[:, 6, :], t1, -1.0, ALU.mult)
        for n in range(2, 8):
            cmul(PWr[:, 7 - n, :], PWi[:, 7 - n, :], PWr[:, 8 - n, :], PWi[:, 8 - n, :], PWr[:, 6, :], PWi[:, 6, :])
        cp(AKr[:, 0, :], PWr[:, 15, :])
        cp(AKi[:, 0, :], PWi[:, 15, :])
        for k in range(1, 9):
            cmul(AKr[:, k, :], AKi[:, k, :], AKr[:, k - 1, :], AKi[:, k - 1, :], AKr[:, k - 1, :], AKi[:, k - 1, :])
        ts(AKs[:], AKi[:], sig[:, 1:2], ALU.mult)
        for s_ in range(15):
            cp(PRr[:, s_, :], PWr[:, 14 - s_, :])
            ts(PRi[:, s_, :], PWi[:, 14 - s_, :], sig[:, 0:1], ALU.mult)
        ts(QRr[:], PWr[:, 7:16, :], sig[:, 1:2], ALU.mult)
        ts(QRi[:], PWi[:, 7:16, :], -1.0, ALU.mult)

        off_ut = AR.off
        utok = AR.alloc([128, 8192], BF16)
        R_ut = Res("utok")
        PSr = SubPool(PS.t[4:8])
        for cb in range(4):
            P.dma(utok[:], u_s.ap()[cb * 128:(cb + 1) * 128, :], reads=[R_u], writes=[R_ut])
            for g8 in range(8):
                ps, rps = PSr.get()
                psb = ps.bitcast(BF16)
                for gg in range(8):
                    g = g8 * 8 + gg
                    P.op("pe", lambda: nc.tensor.transpose(psb[:, gg * 128:(gg + 1) * 128], utok[:, g * 128:(g + 1) * 128], ident_b[:]),
                         reads=[R_ut, R_c], writes=[rps])
                dst = BIGA[:, g8, :].rearrange("p (g c) -> p g c", c=512)[:, :, cb * 128:(cb + 1) * 128]
                src = psb[:, :].rearrange("p (g c) -> p g c", c=128)
                if g8 % 2 == 0:
                    P.op("act", lambda: nc.scalar.copy(out=dst, in_=src), reads=[rps], writes=[R_A[g8]])
                else:
                    P.op("dve", lambda: nc.vector.tensor_copy(out=dst, in_=src), reads=[rps], writes=[R_A[g8]])

        P.barrier()
        AR.off = off_ut
        pfp = TPool(nc, "pf", [128, 4, 240], BF16, 2, arena=AR)
        qfp = TPool(nc, "qf", [128, 4, 144], BF16, 2, arena=AR)
        tP1 = AR.alloc([128, 4, 15, 16], F32)
        tP2 = AR.alloc([128, 4, 15, 16], F32)
        rkp = TPool(nc, "rk", [128, 9, 128], BF16, 4, arena=AR)
        minp = TPool(nc, "min", [128, 128], BF16, 4, arena=AR)
        PSm = SubPool(PS.t[4:8])
        mintp = TPool(nc, "mint", [128, 128], BF16, 4, arena=AR)
        xzp = TPool(nc, "xz", [128, 514], BF16, 8, arena=AR)
        ystp = TPool(nc, "yst", [128, 8, 64], BF16, 2, arena=AR)
        for xz_, rxz_ in xzp.t:
            P.op("pool", lambda: nc.gpsimd.memset(xz_[:, 0:1], 0.0), writes=[rxz_])
        psY = PS.t[0:4]
        BT1g, BT2g, CTag, CTbg = g3(BT1[:]), g3(BT2[:]), g3(CTa), g3(CTb)
        for bi in range(16):
            g0 = bi * 4
            pf, rpf = pfp.get()
            qf, rqf = qfp.get()
            pf4 = pf[:].rearrange("p g (b h) -> p g b h", h=16)
            qf4 = qf[:].rearrange("p g (b h) -> p g b h", h=16)
            e_ = lambda tab, nb: tab[:, :, g0:g0 + 4].rearrange("p b g -> p g b").unsqueeze(3).broadcast_to([128, 4, nb, 16])
            b_ = lambda tab, nb: tab[:, g0:g0 + 4, :].unsqueeze(2).broadcast_to([128, 4, nb, 16])
            rwp = dict(reads=[R_sp], writes=[R_sp])
            P.op("dve", lambda: nc.vector.tensor_tensor(out=tP1[:], in0=e_(PRr, 15), in1=b_(BT1g, 15), op=ALU.mult), **rwp)
            P.op("dve", lambda: nc.vector.tensor_tensor(out=tP2[:], in0=e_(PRi, 15), in1=b_(BT2g, 15), op=ALU.mult), **rwp)
            P.op("dve", lambda: nc.vector.tensor_tensor(out=pf4, in0=tP1[:], in1=tP2[:], op=ALU.add), reads=[R_sp], writes=[R_sp, rpf])
            P.op("dve", lambda: nc.vector.tensor_tensor(out=tP1[:, :, 0:9, :], in0=e_(QRr, 9), in1=b_(CTag, 9), op=ALU.mult), **rwp)
            P.op("dve", lambda: nc.vector.tensor_tensor(out=tP2[:, :, 0:9, :], in0=e_(QRi, 9), in1=b_(CTbg, 9), op=ALU.mult), **rwp)
            P.op("dve", lambda: nc.vector.tensor_tensor(out=qf4, in0=tP1[:, :, 0:9, :], in1=tP2[:, :, 0:9, :], op=ALU.add),
                 reads=[R_sp], writes=[R_sp, rqf])
            def grp_gen(gl):
                g = g0 + gl
                U_g = BIGA[:, g // 8, (g % 8) * 512:(g % 8 + 1) * 512]
                R_U = R_A[g // 8]
                rk, rrk = rkp.get()
                for k in range(9):
                    tf, rtf = f32p.get()
                    P.op("act", lambda: nc.scalar.activation(out=tf[:, 0:128], in_=ident_f[:], func=AF.Copy, scale=AKr[:, k, g:g + 1]),
                         reads=[R_sp, R_c], writes=[rtf])
                    P.op("dve", lambda: nc.vector.scalar_tensor_tensor(out=rk[:, k, :], in0=jswap_f[:], scalar=AKs[:, k, g:g + 1],
                                                                       in1=tf[:, 0:128], op0=ALU.mult, op1=ALU.add),
                         reads=[R_sp, R_c, rtf], writes=[rrk])
                ps, rps = PSr.get()
                psb = ps.bitcast(BF16)
                P.op("pe", lambda: nc.tensor.transpose(psb[:, 0:128], pf[:, gl, 0:128], ident_b[:]), reads=[rpf, R_c], writes=[rps])
                mi, rmi = minp.get()
                P.op("act", lambda: nc.scalar.copy(out=mi[:], in_=psb[:, 0:128]), reads=[rps], writes=[rmi])
                ps, rps = PSm.get()
                P.op("pe", lambda: nc.tensor.matmul(ps[:, 0:128], lhsT=pf[:, gl, 112:240], rhs=qf[:, gl, 0:128], start=True, stop=True),
                     reads=[rpf, rqf], writes=[rps])
                tf, rtf = f32p.get()
                P.op("dve", lambda: nc.vector.tensor_tensor(out=tf[:, 0:128], in0=ps[:, 0:128], in1=mask_intra[:], op=ALU.mult),
                     reads=[rps, R_c], writes=[rtf])
                mt, rmt = mintp.get()
                P.op("dve", lambda: nc.vector.scalar_tensor_tensor(out=mt[:], in0=ident_f[:], scalar=drep[:, g:g + 1], in1=tf[:, 0:128],
                                                                   op0=ALU.mult, op1=ALU.add), reads=[R_sp, R_c, rtf], writes=[rmt])
                yield
                xz, rxz = xzp.get()
                ps, rps = PSr.get()
                P.op("pe", lambda: nc.tensor.matmul(ps[:], lhsT=mi[:], rhs=U_g, start=True, stop=True), reads=[rmi, R_U], writes=[rps])
                P.op("act", lambda: nc.scalar.copy(out=xz[:, 1:513], in_=ps[:]), reads=[rps], writes=[rxz])
                xz2, rxz2 = xzp.get()
                cur, rcur, nxt, rnxt = xz, rxz, xz2, rxz2
                for k in range(9):
                    yield
                    sh = 1 << k
                    ps, rps = PSr.get()
                    P.op("pe", lambda: nc.tensor.matmul(ps[:, 0:512], lhsT=ident_b[:], rhs=cur[:, 1:513], start=True, stop=False),
                         reads=[R_c, rcur], writes=[rps])
                    P.op("pe", lambda: nc.tensor.matmul(ps[:, sh:512], lhsT=rk[:, k, :], rhs=cur[:, 1:513 - sh], start=False, stop=True),
                         reads=[rrk, rcur], writes=[rps])
                    if k % 2 == 0:
                        P.op("act", lambda: nc.scalar.copy(out=nxt[:, 1:513], in_=ps[:]), reads=[rps], writes=[rnxt])
                    else:
                        P.op("dve", lambda: nc.vector.tensor_copy(out=nxt[:, 1:513], in_=ps[:]), reads=[rps], writes=[rnxt])
                    cur, rcur, nxt, rnxt = nxt, rnxt, cur, rcur
                xz, rxz = cur, rcur
                yield
                for cb in range(4):
                    P.op("pe", lambda: nc.tensor.matmul(psY[cb][0][:, gl * 128:(gl + 1) * 128], lhsT=U_g[:, cb * 128:(cb + 1) * 128], rhs=mt[:],
                                                        start=True, stop=False), reads=[R_U, rmt], writes=[psY[cb][1]])
                    P.op("pe", lambda: nc.tensor.matmul(psY[cb][0][:, gl * 128:(gl + 1) * 128], lhsT=xz[:, cb * 128:(cb + 1) * 128],
                                                        rhs=qf[:, gl, 16:144], start=False, stop=True), reads=[rxz, rqf], writes=[psY[cb][1]])
            run_pipe(grp_gen(gl) for gl in range(4))
            for cb in range(4):
                yst, ryst = ystp.get()
                if "s5raw" in dbg:
                    P.op("act", lambda: nc.scalar.copy(out=yst[:].rearrange("p j (g h) -> p g j h", h=16),
                                                       in_=psY[cb][0][:].rearrange("p (g j h) -> p g j h", g=4, j=8)),
                         reads=[psY[cb][1]], writes=[ryst])
                else:
                    P.op("act", lambda: nc.scalar.activation(out=yst[:].rearrange("p j (g h) -> p g j h", h=16),
                                                             in_=psY[cb][0][:].rearrange("p (g j h) -> p g j h", g=4, j=8),
                                                             func=AF.Gelu_apprx_tanh), reads=[psY[cb][1]], writes=[ryst])
                P.dma(y_s.ap()[1024 * cb:1024 * (cb + 1), 64 * bi:64 * bi + 64].rearrange("(c j) n -> c j n", j=8), yst[:],
                      reads=[ryst], writes=[R_ys])

        P.barrier()
        AR.reset()
        ytp = TPool(nc, "ytk", [128, 1024], BF16, 2, arena=AR)
        for tt_ in range(32):
            yt, ryt = ytp.get()
            P.dma(yt[:], y_s.ap()[tt_ * 128:(tt_ + 1) * 128, :], reads=[R_ys], writes=[ryt])
            ps, rps = PSr.get()
            psb = ps.bitcast(BF16)
            for kc in range(8):
                P.op("pe", lambda: nc.tensor.transpose(psb[:, kc * 128:(kc + 1) * 128], yt[:, kc * 128:(kc + 1) * 128], ident_b[:]),
                     reads=[ryt, R_c], writes=[rps])
            dst = BIGA[:, :, tt_ * 128:(tt_ + 1) * 128]
            src = psb[:, :].rearrange("p (k c) -> p k c", c=128)
            if tt_ % 2 == 0:
                P.op("act", lambda: nc.scalar.copy(out=dst, in_=src), reads=[rps], writes=R_A)
            else:
                P.op("dve", lambda: nc.vector.tensor_copy(out=dst, in_=src), reads=[rps], writes=R_A)
        wp4 = TPool(nc, "wt4", [128, 8, 512], BF16, 4, arena=AR)
        for grp in range(2):
            w1t, rw1 = wp4.get()
            P.dma(w1t[:], W["s5_glu_w1"].ap()[l][:, grp * 512:(grp + 1) * 512].rearrange("(kc p) n -> p kc n", p=128), writes=[rw1], q="pool")
            w2t, rw2 = wp4.get()
            P.dma(w2t[:], W["s5_glu_w2"].ap()[l][:, grp * 512:(grp + 1) * 512].rearrange("(kc p) n -> p kc n", p=128), writes=[rw2], q="pool")
            for ci in range(4):
                for tb in range(NTB):
                    psa, rpsa = mm_fm(w1t, rw1, ci, tb)
                    psb_, rpsb = mm_fm(w2t, rw2, ci, tb)
                    sg_, rsg = f32p.get()
                    P.op("act", lambda: nc.scalar.activation(out=sg_[:], in_=psb_[:], func=AF.Sigmoid), reads=[rpsb], writes=[rsg])
                    o, ro = b16p.get()
                    P.op("dve", lambda: nc.vector.tensor_tensor(out=o[:], in0=psa[:], in1=sg_[:], op=ALU.mult), reads=[rpsa, rsg], writes=[ro])
                    r0_ = D + (grp * 4 + ci) * 128
                    P.dma(br_s.ap()[r0_:r0_ + 128, tb * TB:(tb + 1) * TB], o[:], reads=[ro], writes=[R_br])

    for l in range(nlayers):
        x_src, R_xsrc = (xT_in, Res("xin")) if l == 0 else (xmidT_s, R_xmid)
        x_dst, R_xdst = (yT_out, Res("yout")) if l == nlayers - 1 else (xmidT_s, R_xmid)
        P.barrier()
        P.dma(vec[:], vecs_in.ap()[l], writes=[R_vec])
        P.dma(lam4[:], lamv_in.ap()[l], writes=[R_lam])
        lam_init = 0.8 - 0.6 * math.exp(-0.3 * l)
        lt, rlt = f32p.get()
        P.op("dve", lambda: nc.vector.tensor_tensor(out=lt[:, 0:64], in0=lam4[:, 0:64], in1=lam4[:, 64:128], op=ALU.mult),
             reads=[R_lam], writes=[rlt])
        P.op("dve", lambda: nc.vector.tensor_tensor(out=lt[:, 64:128], in0=lam4[:, 128:192], in1=lam4[:, 192:256], op=ALU.mult),
             reads=[R_lam], writes=[rlt])
        P.op("dve", lambda: nc.vector.reduce_sum(out=lt[:, 128:130], in_=lt[:, 0:128].rearrange("p (a b) -> p a b", a=2),
                                                 axis=AX.X), reads=[rlt], writes=[rlt])
        P.op("act", lambda: nc.scalar.activation(out=lt[:, 130:132], in_=lt[:, 128:130], func=AF.Exp), reads=[rlt], writes=[rlt])
        P.op("dve", lambda: nc.vector.scalar_tensor_tensor(out=neglam[:], in0=lt[:, 131:132], scalar=-lam_init, in1=lt[:, 130:131],
                                                           op0=ALU.add, op1=ALU.subtract), reads=[rlt], writes=[R_lam])
        P.op("dve", lambda: nc.vector.tensor_scalar(out=subsc[:], in0=vec[:, V_SUB:V_SUB + 1], scalar1=(1.0 - lam_init), scalar2=None,
                                                    op0=ALU.mult), reads=[R_vec], writes=[R_lam])

        norm_stage(x_src, R_xsrc, V_GMIX)

        w_in2 = W["w_in"].ap()[l]
        for seg, (off0, gcol, dst, R_dst) in enumerate([(OFF_Q, V_QG, qT_s, R_qT), (OFF_K, V_KG, kT_s, R_kT)]):
            for grp in range(2):
                wt, rwt = load_w(w_in2, [(off0 + grp * 512, 512)])
                run_pipe(qk_tile(wt, rwt, ci, gcol, dst, R_dst, (grp * 4 + ci) * 128, tb) for ci in range(4) for tb in range(NTB))
        for grp in range(2):
            wt, rwt = load_w(w_in2, [(OFF_V + grp * 512, 512)])
            for tt in range(32):
                ps, rps = PS.get()
                for kc in range(8):
                    P.op("pe", lambda: nc.tensor.matmul(ps[:], lhsT=BIGA[:, kc, tt * 128:(tt + 1) * 128], rhs=wt[:, kc, :],
                                                        start=(kc == 0), stop=(kc == 7)), reads=[rwt, R_A[kc]], writes=[rps])
                o, ro = b16p.get()
                if tt % 2 == 0:
                    P.op("act", lambda: nc.scalar.copy(out=o[:], in_=ps[:]), reads=[rps], writes=[ro])
                else:
                    P.op("dve", lambda: nc.vector.tensor_copy(out=o[:], in_=ps[:]), reads=[rps], writes=[ro])
                P.dma(v_s.ap()[tt * 128:(tt + 1) * 128, grp * 512:(grp + 1) * 512], o[:], reads=[ro], writes=[R_v])
        for grp in range(2):
            wt, rwt = load_w(w_in2, [(OFF_U + grp * 512, 512)])
            for cb in range(4):
                us, rus = ustage.get()
                for j in range(8):
                    ps, rps = PS.get()
                    for kc in range(8):
                        P.op("pe", lambda: nc.tensor.matmul(ps[:], lhsT=BIGA[:, kc, 1024 * cb + j:1024 * (cb + 1):8], rhs=wt[:, kc, :],
                                                            start=(kc == 0), stop=(kc == 7)), reads=[rwt, R_A[kc]], writes=[rps])
                    src_v = ps[:].rearrange("p (g h) -> p g h", h=16)
                    if j % 2 == 0:
                        P.op("act", lambda: nc.scalar.copy(out=us[:, :, j, :], in_=src_v), reads=[rps], writes=[rus])
                    else:
                        P.op("dve", lambda: nc.vector.tensor_copy(out=us[:, :, j, :], in_=src_v), reads=[rps], writes=[rus])
                P.dma(u_s.ap()[cb * 128:(cb + 1) * 128, grp * 4096:(grp + 1) * 4096], us[:].rearrange("p g j h -> p (g j h)"),
                      reads=[rus], writes=[R_u])
        for grp in range(4):
            wt, rwt = load_w(w_in2, [(OFF_C + grp * 256, 256), (OFF_C + 1024 + grp * 256, 256)])
            for ci in range(2):
                for tb in range(NTB):
                    psa, rpsa = mm_fm(wt, rwt, ci, tb)
                    psb, rpsb = mm_fm(wt, rwt, ci + 2, tb)
                    sg_, rsg = f32p.get()
                    P.op("act", lambda: nc.scalar.activation(out=sg_[:], in_=psb[:], func=AF.Sigmoid), reads=[rpsb], writes=[rsg])
                    o, ro = b16p.get()
                    P.op("dve", lambda: nc.vector.tensor_tensor(out=o[:], in0=psa[:], in1=sg_[:], op=ALU.mult),
                         reads=[rpsa, rsg], writes=[ro])
                    r0 = (grp * 2 + ci) * 128
                    P.dma(cT_s.ap()[r0:r0 + 128, tb * TB:(tb + 1) * TB], o[:], reads=[ro], writes=[R_cT])
        for grp in range(6):
            wt, rwt = load_w(w_in2, [(OFF_G + grp * 512, 512)])
            for ci in range(4):
                for tb in range(NTB):
                    ps, rps = mm_fm(wt, rwt, ci, tb)
                    o, ro = b16p.get()
                    P.op("act", lambda: nc.scalar.activation(out=o[:], in_=ps[:], func=AF.Sigmoid), reads=[rps], writes=[ro])
                    r0 = (grp * 4 + ci) * 128
                    P.dma(gT_s.ap()[r0:r0 + 128, tb * TB:(tb + 1) * TB], o[:], reads=[ro], writes=[R_gT])
        if "stop_s2" in dbg:
            break

        P.barrier()
        AR.reset()
        kpool = TPool(nc, "kT", [128, S], BF16, 2, arena=AR)
        qpool = TPool(nc, "qT", [128, S], BF16, 2, arena=AR)
        vpool = TPool(nc, "vh", [128, 32, 128], BF16, 2, arena=AR)
        epool = TPool(nc, "eT", [128, 512], BF16, 3, arena=AR)
        epool2 = TPool(nc, "eT2", [128, 2, 512], BF16, 6, arena=AR)
        PSp = SubPool(PSPAIR)
        o32 = TPool(nc, "o32", [128, 512], F32, 9, arena=AR)
        psO = [PS.t[0], PS.t[1]]
        psS = [PS.t[2], PS.t[3]]
        PSs = SubPool(PS.t[4:8])
        hbuf = {}

        def att_load(h):
            kt, rkt = kpool.get()
            P.dma(kt[:], kT_s.ap()[h * 128:(h + 1) * 128, :], reads=[R_kT], writes=[rkt])
            qt, rqt = qpool.get()
            P.dma(qt[:], qT_s.ap()[h * 128:(h + 1) * 128, :], reads=[R_qT], writes=[rqt])
            vt, rvt = vpool.get()
            P.dma(vt[:], v_s.ap()[:, h * 128:(h + 1) * 128].rearrange("(j p) e -> p j e", p=128), reads=[R_v], writes=[rvt])
            hbuf[h] = (kt, rkt, qt, rqt, vt, rvt)
            return
            yield

        def att_unit(h, qb, j):
            kt, rkt, qt, rqt, vt, rvt = hbuf[h]
            q0 = qb * TB
            nj = 4 * qb + 4
            lo = max(0, 128 * j - q0)
            pp, rpp = PSp.get()
            for c in range(2):
                P.op("pe", lambda: nc.tensor.matmul(pp[:, c, lo:512], lhsT=kt[64 * c:64 * c + 64, 128 * j:128 * j + 128],
                                                    rhs=qt[64 * c:64 * c + 64, q0 + lo:q0 + 512], start=True, stop=True),
                     reads=[rkt, rqt], writes=[rpp])
            et, ret = epool2.get()
            P.op("act", lambda: nc.scalar.activation(out=et[:, :, lo:512], in_=pp[:, :, lo:512], func=AF.Exp, scale=0.125),
                 reads=[rpp], writes=[ret])
            if j >= 4 * qb:
                a = 128 * (j - 4 * qb)
                b = min(a + 256, 512)
                ba = 0
            elif j == 4 * qb - 1:
                a, b, ba = 0, 128, 128
            else:
                a = None
            if a is not None:
                P.op("dve", lambda: nc.vector.tensor_tensor(out=et[:, :, a:b], in0=et[:, :, a:b],
                                                            in1=expB[:, h, ba:ba + (b - a)].unsqueeze(1).broadcast_to([128, 2, b - a]),
                                                            op=ALU.mult), reads=[ret, R_c], writes=[ret])
            yield
            yield
            for c in range(2):
                P.op("pe", lambda: nc.tensor.matmul(psO[c][0][:, lo:512], lhsT=vt[:, j, :], rhs=et[:, c, lo:512],
                                                    start=(j == 0), stop=(j == nj - 1)), reads=[rvt, ret], writes=[psO[c][1]])
                P.op("pe", lambda: nc.tensor.matmul(psS[c][0][:, lo:512], lhsT=ones_b[:], rhs=et[:, c, lo:512],
                                                    start=(j == 0), stop=(j == nj - 1)), reads=[ret, R_c], writes=[psS[c][1]])

        def att_final(h, qb):
            q0 = qb * TB
            yield
            r0, rr0 = o32.get()
            t0, rt0 = o32.get()
            t1, rt1 = o32.get()
            P.op("dve", lambda: nc.vector.reciprocal(out=r0[:], in_=psS[0][0][:]), reads=[psS[0][1]], writes=[rr0])
            P.op("dve", lambda: nc.vector.tensor_tensor(out=t0[:], in0=psO[0][0][:], in1=r0[:], op=ALU.mult),
                 reads=[psO[0][1], rr0], writes=[rt0])
            P.op("dve", lambda: nc.vector.reciprocal(out=t1[:], in_=psS[1][0][:]), reads=[psS[1][1]], writes=[rt1])
            P.op("dve", lambda: nc.vector.tensor_tensor(out=t1[:], in0=psO[1][0][:], in1=t1[:], op=ALU.mult),
                 reads=[psO[1][1]], writes=[rt1])
            P.op("dve", lambda: nc.vector.scalar_tensor_tensor(out=t0[:], in0=t1[:], scalar=neglam[:, 0:1], in1=t0[:],
                                                               op0=ALU.mult, op1=ALU.add), reads=[rt1, R_lam], writes=[rt0])
            sq, rsq = epool.get()
            P.op("act", lambda: nc.scalar.activation(out=sq[:], in_=t0[:], func=AF.Square), reads=[rt0], writes=[rsq])
            yield
            yield
            pp_, rpss = PSp.get()
            pss = pp_[:, 0, :]
            P.op("pe", lambda: nc.tensor.matmul(pss, lhsT=ones_b[:], rhs=sq[:], start=True, stop=True),
                 reads=[rsq, R_c], writes=[rpss])
            P.op("act", lambda: nc.scalar.activation(out=r0[:], in_=pss, func=AF.Sqrt, scale=1.0 / 128, bias=epsc[:, 0:1]),
                 reads=[rpss, R_c], writes=[rr0])
            P.op("dve", lambda: nc.vector.reciprocal(out=r0[:], in_=r0[:]), reads=[rr0], writes=[rr0])
            P.op("dve", lambda: nc.vector.scalar_tensor_tensor(out=BIGA[:, h, q0:q0 + TB], in0=t0[:], scalar=subsc[:, 0:1], in1=r0[:],
                                                               op0=ALU.mult, op1=ALU.mult), reads=[rt0, rr0, R_lam], writes=[R_A[h]])

        def att_gens():
            yield att_load(0)
            for h in range(8):
                for qb in range(NTB):
                    for j in range(4 * qb + 4):
                        yield att_unit(h, qb, j)
                    yield att_final(h, qb)
                    if qb == 1 and h + 1 < 8:
                        yield att_load(h + 1)
        run_pipe(att_gens())

        wpool3 = TPool(nc, "wt3", [128, 8, 512], BF16, 2, arena=AR)

        def proj_to_br(wname, row_base, wp):
            for grp in range(2):
                wt, rwt = wp.get()
                P.dma(wt[:], W[wname].ap()[l][:, grp * 512:(grp + 1) * 512].rearrange("(kc p) n -> p kc n", p=128),
                      writes=[rwt], q="pool")
                for ci in range(4):
                    for tb in range(NTB):
                        ps, rps = mm_fm(wt, rwt, ci, tb)
                        o, ro = b16p.get()
                        if tb % 2 == 0:
                            P.op("act", lambda: nc.scalar.copy(out=o[:], in_=ps[:]), reads=[rps], writes=[ro])
                        else:
                            P.op("dve", lambda: nc.vector.tensor_copy(out=o[:], in_=ps[:]), reads=[rps], writes=[ro])
                        r0_ = row_base + (grp * 4 + ci) * 128
                        P.dma(br_s.ap()[r0_:r0_ + 128, tb * TB:(tb + 1) * TB], o[:], reads=[ro], writes=[R_br])
        proj_to_br("w_attn_out", 0, wpool3)

        s5_stage(l)

        P.barrier()
        AR.reset()
        diag = BIGA[:, :, 0:31 * 128].rearrange("p a (k c) -> p a k c", c=128)
        R_diag = Res("diag")
        for kc in range(8):
            for k in range(31):
                P.op("dve", lambda: nc.vector.tensor_scalar(out=diag[:, kc, k, :], in0=ident_f[:],
                                                            scalar1=vec[:, V_CW + kc * 31 + k:V_CW + kc * 31 + k + 1], scalar2=None,
                                                            op0=ALU.mult), reads=[R_vec, R_c], writes=[R_diag])
        wc = AR.alloc([128, 8, 1024], BF16)
        R_wc = Res("wc")
        P.dma(wc[:], W["conv_w_out"].ap()[l].rearrange("(kc p) n -> p kc n", p=128), writes=[R_wc], q="pool")
        cin = TPool(nc, "cin", [128, 544], BF16, 3, arena=AR)
        cvp = TPool(nc, "cv", [128, 8, 512], F32, 2, arena=AR)
        xbp = TPool(nc, "xb", [128, 8, 512], BF16, 2, arena=AR)
        sqp5 = TPool(nc, "sq5", [128, 8, 512], BF16, 2, arena=AR)
        ynp = TPool(nc, "yn", [128, 8, 512], BF16, 2, arena=AR)
        def conv_tb(tb):
            cv, rcv = cvp.get()
            xb, rxb = xbp.get()
            sq, rsq = sqp5.get()
            for kc in range(8):
                ct, rct = cin.get()
                if tb == 0:
                    P.op("pool", lambda: nc.gpsimd.memset(ct[:, 0:30], 0.0), writes=[rct])
                    P.dma(ct[:, 30:542], cT_s.ap()[kc * 128:(kc + 1) * 128, 0:512], reads=[R_cT], writes=[rct])
                else:
                    P.dma(ct[:, 0:542], cT_s.ap()[kc * 128:(kc + 1) * 128, tb * TB - 30:tb * TB + 512], reads=[R_cT], writes=[rct])
                ps, rps = PS.get()
                for k in range(31):
                    P.op("pe", lambda: nc.tensor.matmul(ps[:], lhsT=diag[:, kc, k, :], rhs=ct[:, k:k + 512], start=(k == 0), stop=(k == 30)),
                         reads=[R_diag, rct], writes=[rps])
                P.op("act", lambda: nc.scalar.activation(out=cv[:, kc, :], in_=ps[:], func=AF.Identity, bias=vec[:, V_CB + kc:V_CB + kc + 1]),
                     reads=[rps, R_vec], writes=[rcv])
                P.op("dve", lambda: nc.vector.tensor_copy(out=xb[:, kc, :], in_=cv[:, kc, :]), reads=[rcv], writes=[rxb])
                P.op("act", lambda: nc.scalar.activation(out=sq[:, kc, :], in_=cv[:, kc, :], func=AF.Square), reads=[rcv], writes=[rsq])
            yield
            ps1, rps1 = PS.get()
            ps2, rps2 = PS.get()
            for kc in range(8):
                P.op("pe", lambda: nc.tensor.matmul(ps1[:], lhsT=ones_b[:], rhs=xb[:, kc, :], start=(kc == 0), stop=(kc == 7)),
                     reads=[rxb, R_c], writes=[rps1])
            for kc in range(8):
                P.op("pe", lambda: nc.tensor.matmul(ps2[:], lhsT=ones_b[:], rhs=sq[:, kc, :], start=(kc == 0), stop=(kc == 7)),
                     reads=[rsq, R_c], writes=[rps2])
            mean, rmean = f32p.get()
            msq, rmsq = f32p.get()
            rs, rrs = f32p.get()
            P.op("act", lambda: nc.scalar.mul(out=mean[:], in_=ps1[:], mul=1.0 / D), reads=[rps1], writes=[rmean])
            P.op("dve", lambda: nc.vector.tensor_tensor(out=msq[:], in0=mean[:], in1=mean[:], op=ALU.mult), reads=[rmean], writes=[rmsq])
            P.op("dve", lambda: nc.vector.scalar_tensor_tensor(out=rs[:], in0=ps2[:], scalar=1.0 / D, in1=msq[:], op0=ALU.mult,
                                                               op1=ALU.subtract), reads=[rps2, rmsq], writes=[rrs])
            P.op("act", lambda: nc.scalar.activation(out=rs[:], in_=rs[:], func=AF.Sqrt, bias=epsc[:, 0:1]), reads=[rrs, R_c], writes=[rrs])
            P.op("dve", lambda: nc.vector.reciprocal(out=rs[:], in_=rs[:]), reads=[rrs], writes=[rrs])
            yn, ryn = ynp.get()
            for kc in range(8):
                P.op("dve", lambda: nc.vector.tensor_tensor(out=cv[:, kc, :], in0=cv[:, kc, :], in1=mean[:], op=ALU.subtract),
                     reads=[rmean], writes=[rcv])
                P.op("dve", lambda: nc.vector.tensor_tensor(out=cv[:, kc, :], in0=cv[:, kc, :], in1=rs[:], op=ALU.mult),
                     reads=[rrs], writes=[rcv])
                P.op("act", lambda: nc.scalar.activation(out=yn[:, kc, :], in_=cv[:, kc, :], func=AF.Silu,
                                                         scale=vec[:, V_LNG + kc:V_LNG + kc + 1], bias=vec[:, V_LNB + kc:V_LNB + kc + 1]),
                     reads=[rcv, R_vec], writes=[ryn])
            yield
            for co in range(8):
                ps, rps = PS.get()
                for kc in range(8):
                    P.op("pe", lambda: nc.tensor.matmul(ps[:], lhsT=wc[:, kc, co * 128:(co + 1) * 128], rhs=yn[:, kc, :],
                                                        start=(kc == 0), stop=(kc == 7)), reads=[R_wc, ryn], writes=[rps])
                o, ro = b16p.get()
                if co % 2 == 0:
                    P.op("act", lambda: nc.scalar.copy(out=o[:], in_=ps[:]), reads=[rps], writes=[ro])
                else:
                    P.op("dve", lambda: nc.vector.tensor_copy(out=o[:], in_=ps[:]), reads=[rps], writes=[ro])
                r0_ = 2 * D + co * 128
                P.dma(br_s.ap()[r0_:r0_ + 128, tb * TB:(tb + 1) * TB], o[:], reads=[ro], writes=[R_br])

        run_pipe(conv_tb(tb) for tb in range(NTB))

        P.barrier()
        AR.reset()
        wo = AR.alloc([128, 8, 1024], BF16)
        R_wo = Res("wo")
        P.dma(wo[:], W["w_out"].ap()[l].rearrange("(kc p) n -> p kc n", p=128), writes=[R_wo], q="pool")
        inp6 = TPool(nc, "in6", [128, 512], BF16, 12, arena=AR)
        mixp = TPool(nc, "mix", [128, 8, 512], BF16, 2, arena=AR)
        xin6 = TPool(nc, "xin6", [128, 8, 512], F32, 1, arena=AR)
        x1p = TPool(nc, "x1p", [128, 8, 512], F32, 2, arena=AR)
        sqp6 = TPool(nc, "sq6", [128, 8, 512], BF16, 1, arena=AR)
        for tb in range(NTB):
            mix, rmix = mixp.get()
            for kc in range(8):
                tl = []
                for i in range(3):
                    bt, rbt = inp6.get()
                    P.dma(bt[:], br_s.ap()[i * D + kc * 128:i * D + (kc + 1) * 128, tb * TB:(tb + 1) * TB], reads=[R_br], writes=[rbt])
                    gt, rgt = inp6.get()
                    P.dma(gt[:], gT_s.ap()[i * D + kc * 128:i * D + (kc + 1) * 128, tb * TB:(tb + 1) * TB], reads=[R_gT], writes=[rgt])
                    tl.append((bt, rbt, gt, rgt))
                ta, rta = f32p.get()
                tb2, rtb2 = f32p.get()
                P.op("pool", lambda: nc.gpsimd.tensor_tensor(out=ta[:], in0=tl[0][0][:], in1=tl[0][2][:], op=ALU.mult),
                     reads=[tl[0][1], tl[0][3]], writes=[rta])
                P.op("pool", lambda: nc.gpsimd.tensor_tensor(out=tb2[:], in0=tl[1][0][:], in1=tl[1][2][:], op=ALU.mult),
                     reads=[tl[1][1], tl[1][3]], writes=[rtb2])
                P.op("dve", lambda: nc.vector.tensor_tensor(out=ta[:], in0=ta[:], in1=tb2[:], op=ALU.add), reads=[rtb2], writes=[rta])
                P.op("pool", lambda: nc.gpsimd.tensor_tensor(out=tb2[:], in0=tl[2][0][:], in1=tl[2][2][:], op=ALU.mult),
                     reads=[tl[2][1], tl[2][3]], writes=[rtb2])
                P.op("dve", lambda: nc.vector.tensor_tensor(out=mix[:, kc, :], in0=ta[:], in1=tb2[:], op=ALU.add),
                     reads=[rta, rtb2], writes=[rmix])
            xb, rxb = xin6.get()
            P.dma(xb[:], x_src.ap()[:, tb * TB:(tb + 1) * TB].rearrange("(kc p) t -> p kc t", p=128), reads=[R_xsrc], writes=[rxb])
            x1, rx1 = x1p.get()
            for co in range(8):
                ps, rps = PS.get()
                for kc in range(8):
                    P.op("pe", lambda: nc.tensor.matmul(ps[:], lhsT=wo[:, kc, co * 128:(co + 1) * 128], rhs=mix[:, kc, :],
                                                        start=(kc == 0), stop=(kc == 7)), reads=[R_wo, rmix], writes=[rps])
                P.op("dve", lambda: nc.vector.tensor_tensor(out=x1[:, co, :], in0=ps[:], in1=xb[:, co, :], op=ALU.add),
                     reads=[rps, rxb], writes=[rx1])
            P.dma(x1T_s.ap()[:, tb * TB:(tb + 1) * TB].rearrange("(kc p) t -> p kc t", p=128), x1[:], reads=[rx1], writes=[R_x1])
            norm_block(x1, rx1, V_GFFN, tb, sqp6)

        P.barrier()
        AR.reset()
        wpool8 = TPool(nc, "wt8", [128, 8, 512], BF16, 2, arena=AR)
        ufull = TPool(nc, "uf", [128, 2 + S], BF16, 8, arena=AR)
        dgp = TPool(nc, "dg", [128, 4, 3, 128], BF16, 2, arena=AR)
        w_up2 = W["ffn_w_up"].ap()[l]
        for grp in range(11):
            wt, rwt = wpool8.get()
            P.dma(wt[:, :, 0:256], w_up2[:, grp * 256:(grp + 1) * 256].rearrange("(kc p) n -> p kc n", p=128), writes=[rwt], q="pool")
            P.dma(wt[:, :, 256:512], w_up2[:, FFN_H + grp * 256:FFN_H + (grp + 1) * 256].rearrange("(kc p) n -> p kc n", p=128),
                  writes=[rwt], q="pool")
            dgt, rdg = dgp.get()
            for ci in range(4):
                gch = (2 * grp + ci) if ci < 2 else (22 + 2 * grp + ci - 2)
                for k in range(3):
                    P.op("pool", lambda: nc.gpsimd.tensor_scalar(out=dgt[:, ci, k, :], in0=ident_f[:],
                                                                 scalar1=vec[:, V_FW + gch * 3 + k:V_FW + gch * 3 + k + 1], scalar2=None,
                                                                 op0=ALU.mult), reads=[R_vec, R_c], writes=[rdg])
            ufs = [ufull.get() for _ in range(4)]
            for ci in range(4):
                P.op("pool", lambda: nc.gpsimd.memset(ufs[ci][0][:, 0:2], 0.0), writes=[ufs[ci][1]])
                for tb in range(NTB):
                    ps, rps = mm_fm(wt, rwt, ci, tb)
                    if tb % 2 == 0:
                        P.op("act", lambda: nc.scalar.copy(out=ufs[ci][0][:, 2 + tb * TB:2 + (tb + 1) * TB], in_=ps[:]),
                             reads=[rps], writes=[ufs[ci][1]])
                    else:
                        P.op("dve", lambda: nc.vector.tensor_copy(out=ufs[ci][0][:, 2 + tb * TB:2 + (tb + 1) * TB], in_=ps[:]),
                             reads=[rps], writes=[ufs[ci][1]])
            for pi in range(2):
                for tb in range(NTB):
                    psv, rpsv = PS.get()
                    psg, rpsg = PS.get()
                    for k in range(3):
                        P.op("pe", lambda: nc.tensor.matmul(psv[:], lhsT=dgt[:, pi, k, :], rhs=ufs[pi][0][:, tb * TB + k:tb * TB + k + TB],
                                                            start=(k == 0), stop=(k == 2)), reads=[rdg, ufs[pi][1]], writes=[rpsv])
                    for k in range(3):
                        P.op("pe", lambda: nc.tensor.matmul(psg[:], lhsT=dgt[:, pi + 2, k, :], rhs=ufs[pi + 2][0][:, tb * TB + k:tb * TB + k + TB],
                                                            start=(k == 0), stop=(k == 2)), reads=[rdg, ufs[pi + 2][1]], writes=[rpsg])
                    gl, rgl = f32p.get()
                    P.op("act", lambda: nc.scalar.activation(out=gl[:], in_=psg[:], func=AF.Gelu_apprx_tanh), reads=[rpsg], writes=[rgl])
                    o, ro = b16p.get()
                    P.op("dve", lambda: nc.vector.tensor_tensor(out=o[:], in0=psv[:], in1=gl[:], op=ALU.mult), reads=[rpsv, rgl], writes=[ro])
                    r0_ = (2 * grp + pi) * 128
                    P.dma(act_s.ap()[r0_:r0_ + 128, tb * TB:(tb + 1) * TB], o[:], reads=[ro], writes=[R_act])

        P.barrier()
        AR.reset()
        wd = AR.alloc([128, 22, 1024], BF16)
        R_wd = Res("wd")
        wdd = W["ffn_w_down"].ap()[l].rearrange("(kc p) n -> p kc n", p=128)
        P.dma(wd[:, 0:11, :], wdd[:, 0:11, :], writes=[R_wd], q="pool")
        P.dma(wd[:, 11:22, :], wdd[:, 11:22, :], writes=[R_wd], q="pool")
        actp = TPool(nc, "actp", [128, 22, 512], BF16, 2, arena=AR)
        xp9 = TPool(nc, "xp9", [128, 512], F32, 4, arena=AR)
        for tb in range(NTB):
            at, rat = actp.get()
            P.dma(at[:], act_s.ap()[:, tb * TB:(tb + 1) * TB].rearrange("(kc p) t -> p kc t", p=128), reads=[R_act], writes=[rat])
            for co in range(8):
                xr, rxr = xp9.get()
                P.dma(xr[:], x1T_s.ap()[co * 128:(co + 1) * 128, tb * TB:(tb + 1) * TB], reads=[R_x1], writes=[rxr])
                ps, rps = PS.get()
                for kc in range(22):
                    P.op("pe", lambda: nc.tensor.matmul(ps[:], lhsT=wd[:, kc, co * 128:(co + 1) * 128], rhs=at[:, kc, :],
                                                        start=(kc == 0), stop=(kc == 21)), reads=[R_wd, rat], writes=[rps])
                P.op("dve", lambda: nc.vector.tensor_tensor(out=xr[:], in0=ps[:], in1=xr[:], op=ALU.add), reads=[rps], writes=[rxr])
                P.dma(x_dst.ap()[co * 128:(co + 1) * 128, tb * TB:(tb + 1) * TB], xr[:], reads=[rxr], writes=[R_xdst])

    P.finish()
    return nc


_CACHE = {}


def make_in_maps(inputs, nb=8):
    common = {}
    for nm in ("w_in", "w_attn_out", "s5_glu_w1", "s5_glu_w2", "conv_w_out", "w_out", "ffn_w_up", "ffn_w_down"):
        common[nm] = np.ascontiguousarray(inputs[nm], dtype=np.float32)
    vs, ls, ss = [], [], []
    for l in range(DEPTH):
        v, lamv, s5 = host_layer_params(inputs, l)
        vs.append(v)
        ls.append(lamv)
        ss.append(s5)
    common["vecs"] = np.stack(vs)
    common["lamv"] = np.stack(ls)
    common["s5p"] = np.stack(ss)
    common["rel_bias"] = np.ascontiguousarray(inputs["rel_bias"], dtype=np.float32)
    common.update({"c_" + k: v for k, v in host_consts().items()})
    x = inputs["x"]
    in_maps = []
    for b in range(nb):
        m = dict(common)
        m["xT"] = np.ascontiguousarray(x[b].T)
        in_maps.append(m)
    return in_maps


def kernel(**inputs):
    inputs = {k: np.asarray(v) for k, v in inputs.items()}
    if "nc" not in _CACHE:
        _CACHE["nc"] = build()
    nc = _CACHE["nc"]
    in_maps = make_in_maps(inputs)
    res = run_bass_kernel_spmd(nc, in_maps, core_ids=list(range(8)))
    out = np.stack([np.ascontiguousarray(r["yT"].T) for r in res.results], 0)
    return out.astype(np.float32)
```
